# Optimizing a Trainium2 kernel written in Bass

```python
import jax, jax.numpy as jnp
from jax import lax
import numpy as np

D_MODEL = 1024
BATCH = 16
SEQ = 2048
DEPTH = 1

N_META = 16
GRID_W = 64
NA_HEADS = 8
NA_HEAD_DIM = 64
NA_WIDTH = NA_HEADS * NA_HEAD_DIM
NA_KH_MAX = 8
NA_KW = 16
GLA_HEADS = 4
GLA_DK = 64
GLA_DV = 128
GLA_KEY_WIDTH = GLA_HEADS * GLA_DK
GLA_VAL_WIDTH = GLA_HEADS * GLA_DV
GLA_GATE_RANK = 16
GLA_GATE_TAU = 16.0
GLA_CHUNK = 64
MIX_WIDTH = NA_WIDTH + GLA_VAL_WIDTH
D_FF = -(-8 * D_MODEL // (3 * 256)) * 256
RMS_EPS = 1e-6
NEG_INF = -1e30
IN_SPLIT_SIZES = (NA_WIDTH, NA_WIDTH, NA_WIDTH,
                  GLA_KEY_WIDTH, GLA_KEY_WIDTH,
                  GLA_VAL_WIDTH, GLA_VAL_WIDTH,
                  GLA_GATE_RANK, GLA_GATE_RANK)
IN_WIDTH = sum(IN_SPLIT_SIZES)

kernel_name = "hybrid_na_gla_bidir_block"


def rms_norm(x, gain, eps=RMS_EPS):
    xf = x.astype(jnp.float32)
    xf = xf * lax.rsqrt(jnp.mean(xf * xf, axis=-1, keepdims=True) + eps)
    return (xf * gain.astype(jnp.float32)).astype(x.dtype)


def neighbourhood_attention(q, k, v, rpb, meta_bias):
    B, L, H, d = q.shape
    N = L - N_META
    rows = N // GRID_W
    kh = min(NA_KH_MAX, rows)
    q = q * (d ** -0.5)
    qm, qg = q[:, :N_META], q[:, N_META:]
    km, kg = k[:, :N_META], k[:, N_META:]
    vm, vg = v[:, :N_META], v[:, N_META:]
    qg = qg.reshape(B, rows, GRID_W, H, d).transpose(1, 0, 3, 2, 4)
    kg = kg.reshape(B, rows, GRID_W, H, d).transpose(0, 3, 1, 2, 4)
    vg = vg.reshape(B, rows, GRID_W, H, d).transpose(0, 3, 1, 2, 4)
    km_h = km.transpose(0, 2, 1, 3)
    vm_h = vm.transpose(0, 2, 1, 3)

    cols = jnp.arange(GRID_W)
    col_start = jnp.clip(cols - NA_KW // 2, 0, GRID_W - NA_KW)
    col_valid = (cols[None, :] >= col_start[:, None]) & (cols[None, :] < col_start[:, None] + NA_KW)
    dc_idx = jnp.clip(cols[None, :] - cols[:, None], -(NA_KW - 1), NA_KW - 1) + (NA_KW - 1)
    row_ids = jnp.arange(rows)
    row_start = jnp.clip(row_ids - kh // 2, 0, rows - kh)

    def row_block(args):
        i, s, q_i = args
        k_win = lax.dynamic_slice_in_dim(kg, s, kh, axis=2)
        v_win = lax.dynamic_slice_in_dim(vg, s, kh, axis=2)
        dr_idx = s + jnp.arange(kh) - i + (NA_KH_MAX - 1)
        bias = rpb[:, dr_idx[:, None, None], dc_idx[None]]
        s_win = jnp.einsum('bhqd,bhrkd->bhqrk', q_i, k_win).astype(jnp.float32)
        s_win = s_win + bias.transpose(0, 2, 1, 3)[None].astype(jnp.float32)
        s_win = jnp.where(col_valid[:, None, :], s_win, NEG_INF)
        s_meta = jnp.einsum('bhqd,bhmd->bhqm', q_i, km_h).astype(jnp.float32)
        s_meta = s_meta + meta_bias[None, :, None, :].astype(jnp.float32)
        scores = jnp.concatenate([s_win.reshape(B, H, GRID_W, kh * GRID_W), s_meta], axis=-1)
        p = jax.nn.softmax(scores, axis=-1).astype(v.dtype)
        p_win = p[..., :kh * GRID_W].reshape(B, H, GRID_W, kh, GRID_W)
        p_meta = p[..., kh * GRID_W:]
        return (jnp.einsum('bhqrk,bhrkd->bhqd', p_win, v_win)
                + jnp.einsum('bhqm,bhmd->bhqd', p_meta, vm_h))

    out_g = lax.map(row_block, (row_ids, row_start, qg))
    out_g = out_g.transpose(1, 0, 3, 2, 4).reshape(B, N, H * d)
    s_mm = jnp.einsum('bmhd,bnhd->bhmn', qm, km).astype(jnp.float32) + meta_bias[None, :, None, :].astype(jnp.float32)
    p_mm = jax.nn.softmax(s_mm, axis=-1).astype(v.dtype)
    out_m = jnp.einsum('bhmn,bnhd->bmhd', p_mm, vm).reshape(B, N_META, H * d)
    return jnp.concatenate([out_m, out_g], axis=1)


def gla_chunked(q, k, v, g):
    B, H, T, dk = q.shape
    dv = v.shape[-1]
    nc = T // GLA_CHUNK
    qf = q.astype(jnp.float32).reshape(B, H, nc, GLA_CHUNK, dk)
    kf = k.astype(jnp.float32).reshape(B, H, nc, GLA_CHUNK, dk)
    vf = v.astype(jnp.float32).reshape(B, H, nc, GLA_CHUNK, dv)
    b = jnp.cumsum(g.astype(jnp.float32).reshape(B, H, nc, GLA_CHUNK, dk), axis=3)
    b_last = b[:, :, :, -1:]
    q_e = qf * jnp.exp(b)
    k_e = kf * jnp.exp(-b)
    k_d = kf * jnp.exp(b_last - b)
    causal = jnp.tril(jnp.ones((GLA_CHUNK, GLA_CHUNK), dtype=bool))
    a = jnp.where(causal, jnp.einsum('bhnik,bhnjk->bhnij', q_e, k_e), 0.0)
    o_intra = jnp.einsum('bhnij,bhnjv->bhniv', a, vf)
    kv = jnp.einsum('bhnjk,bhnjv->bhnkv', k_d, vf)
    decay = jnp.exp(b_last[:, :, :, 0])

    def step(state, inp):
        kv_n, dec_n = inp
        return dec_n[..., None] * state + kv_n, state

    init = jnp.zeros((B, H, dk, dv), jnp.float32)
    _, s_prev = lax.scan(step, init, (kv.transpose(2, 0, 1, 3, 4), decay.transpose(2, 0, 1, 3)))
    o_inter = jnp.einsum('bhnik,nbhkv->bhniv', q_e, s_prev)
    return (o_intra + o_inter).reshape(B, H, T, dv).astype(v.dtype)


def gla_bidirectional(q, k, v, r, gate_f_low, gate_b_low, w_gate_up_f, b_gate_f, w_gate_up_b, b_gate_b, norm_gain):
    B, L, _ = q.shape
    pad = (-N_META) % GLA_CHUNK
    g_f = jax.nn.log_sigmoid((gate_f_low @ w_gate_up_f + b_gate_f).astype(jnp.float32)) / GLA_GATE_TAU
    g_b = jax.nn.log_sigmoid((gate_b_low @ w_gate_up_b + b_gate_b).astype(jnp.float32)) / GLA_GATE_TAU

    def heads(a, dh):
        a = a.reshape(B, L, GLA_HEADS, dh).transpose(0, 2, 1, 3)
        return jnp.pad(a, ((0, 0), (0, 0), (pad, 0), (0, 0)))

    qh = heads(q * (GLA_DK ** -0.5), GLA_DK)
    kh = heads(k, GLA_DK)
    vh = heads(v, GLA_DV)
    gfh = heads(g_f, GLA_DK)
    gbh = heads(g_b, GLA_DK)
    flip = lambda a: jnp.flip(a, axis=2)
    o_fwd = gla_chunked(qh, kh, vh, gfh)
    o_bwd = flip(gla_chunked(flip(qh), flip(kh), flip(vh), flip(gbh)))
    o = (o_fwd + o_bwd)[:, :, pad:].transpose(0, 2, 1, 3)
    o = rms_norm(o, norm_gain).reshape(B, L, GLA_VAL_WIDTH)
    return o * jax.nn.silu(r)


def setup_inputs(seed: int = 0) -> dict:
    key = jax.random.key(seed)
    ks = jax.random.split(key, 20)
    f32 = jnp.float32
    nrm = lambda k, shape, scale: jax.random.normal(k, shape, f32) * scale
    gains = lambda k, shape: 1.0 + 0.05 * jax.random.normal(k, shape, f32)
    return {
        "x": jax.random.normal(ks[0], (BATCH, SEQ, D_MODEL), f32),
        "meta_tokens": nrm(ks[1], (N_META, D_MODEL), 1.0),
        "norm_mix_gain": gains(ks[2], (DEPTH, D_MODEL)),
        "w_in": nrm(ks[3], (DEPTH, D_MODEL, IN_WIDTH), D_MODEL ** -0.5),
        "rpb": nrm(ks[4], (DEPTH, NA_HEADS, 2 * NA_KH_MAX - 1, 2 * NA_KW - 1), 0.1),
        "meta_bias": nrm(ks[5], (DEPTH, NA_HEADS, N_META), 0.1),
        "w_gate_up_fwd": nrm(ks[6], (DEPTH, GLA_GATE_RANK, GLA_KEY_WIDTH), GLA_GATE_RANK ** -0.5),
        "b_gate_fwd": nrm(ks[7], (DEPTH, GLA_KEY_WIDTH), 0.1),
        "w_gate_up_bwd": nrm(ks[8], (DEPTH, GLA_GATE_RANK, GLA_KEY_WIDTH), GLA_GATE_RANK ** -0.5),
        "b_gate_bwd": nrm(ks[9], (DEPTH, GLA_KEY_WIDTH), 0.1),
        "gla_norm_gain": gains(ks[10], (DEPTH, GLA_DV)),
        "w_out": nrm(ks[11], (DEPTH, MIX_WIDTH, D_MODEL), MIX_WIDTH ** -0.5),
        "norm_ffn_gain": gains(ks[12], (DEPTH, D_MODEL)),
        "w_ffn_gate": nrm(ks[13], (DEPTH, D_MODEL, D_FF), D_MODEL ** -0.5),
        "w_ffn_up": nrm(ks[14], (DEPTH, D_MODEL, D_FF), D_MODEL ** -0.5),
        "w_ffn_down": nrm(ks[15], (DEPTH, D_FF, D_MODEL), D_FF ** -0.5),
        "norm_final_gain": gains(ks[16], (D_MODEL,)),
    }


def reference(x, meta_tokens, norm_mix_gain, w_in, rpb, meta_bias, w_gate_up_fwd, b_gate_fwd,
              w_gate_up_bwd, b_gate_bwd, gla_norm_gain, w_out, norm_ffn_gain, w_ffn_gate,
              w_ffn_up, w_ffn_down, norm_final_gain):
    B = x.shape[0]
    meta = jnp.broadcast_to(meta_tokens.astype(x.dtype)[None], (B, N_META, D_MODEL))
    h = jnp.concatenate([meta, x], axis=1)
    L = h.shape[1]
    split_points = [int(s) for s in np.cumsum(IN_SPLIT_SIZES)[:-1]]
    for layer in range(DEPTH):
        xn = rms_norm(h, norm_mix_gain[layer])
        proj = xn @ w_in[layer]
        na_q, na_k, na_v, g_q, g_k, g_v, g_r, g_fl, g_bl = jnp.split(proj, split_points, axis=-1)
        na_heads = lambda a: a.reshape(B, L, NA_HEADS, NA_HEAD_DIM)
        na_out = neighbourhood_attention(na_heads(na_q), na_heads(na_k), na_heads(na_v),
                                         rpb[layer], meta_bias[layer])
        gla_out = gla_bidirectional(g_q, g_k, g_v, g_r, g_fl, g_bl,
                                    w_gate_up_fwd[layer], b_gate_fwd[layer],
                                    w_gate_up_bwd[layer], b_gate_bwd[layer],
                                    gla_norm_gain[layer])
        h = h + jnp.concatenate([na_out, gla_out], axis=-1) @ w_out[layer]
        hn = rms_norm(h, norm_ffn_gain[layer])
        h = h + (jax.nn.silu(hn @ w_ffn_gate[layer]) * (hn @ w_ffn_up[layer])) @ w_ffn_down[layer]
    h = rms_norm(h, norm_final_gain)
    return h[:, N_META:]
```

```python
import numpy as np
from contextlib import ExitStack
import concourse.bass as bass
import concourse.mybir as mybir
from concourse.bass_utils import run_bass_kernel_spmd

F32 = mybir.dt.float32
BF16 = mybir.dt.bfloat16
AF = mybir.ActivationFunctionType
ALU = mybir.AluOpType
AX = mybir.AxisListType

DEBUG = False
LAST_PHASE = 99
P2_CUT = 0

D = 1024
SEQ = 2048
NT = 16
NTILES = 33
META_T = 32
IN_W = 3104
DFF = 2816
NFF = 22
EPS = 1e-6
NEG = -30000.0

C_NAQ, C_NAK, C_NAV = 0, 512, 1024
C_GQ, C_GK, C_GV, C_GR, C_GF, C_GB = 1536, 1792, 2048, 2560, 3072, 3088

K_ID, K_TRIF, K_TRIB, K_SUF, K_SUB, K_MF, K_MB = [128 * i for i in range(7)]
NCONST = 128 * 7

NCFG = 12


class Buf:
    __slots__ = ("name", "last_w", "readers", "sem", "cnt", "kind", "sub")

    def __init__(self, name):
        self.name = name
        self.last_w = None
        self.readers = []
        self.sem = None
        self.cnt = 0
        self.kind = None
        self.sub = None


class Ctx:
    def __init__(self, nc, es):
        self.nc = nc
        self.es = es
        self.eng_sem = {}
        self.eng_cnt = {}
        for e in ("pe", "dve", "act", "pool", "sp"):
            self.eng_sem[e] = es.enter_context(nc.semaphore("es_" + e))
            self.eng_cnt[e] = 0
        self.dma_bufs = []
        self.sem_pool = {"sw": [], "hw": []}
        self.nsem = 0
        self.nbuf = 0

    def buf(self, name=None):
        self.nbuf += 1
        return Buf(name or "b%d" % self.nbuf)

    def bufs(self, n, name=None):
        return [self.buf((name or "b") + str(i)) for i in range(n)]


class Ph:
    def __init__(self, cx, name):
        self.cx = cx
        self.name = name
        self.ops = {e: [] for e in ("pe", "dve", "act", "pool", "sp")}
        self.waited = {e: {} for e in self.ops}
        self.touched = []

    def _add_dep(self, deps, d, eng, raw, waw=False):
        if d is None:
            return
        sem, val, deng, is_dma = d
        if deng == eng and not is_dma:
            if eng == "pe":
                return
            if not raw and eng != "pool" and not waw:
                return
        deps.append((sem, val))

    def op(self, eng, fn, reads=(), writes=(), dma=None):
        cx = self.cx
        deps = []
        for b in reads:
            self._add_dep(deps, b.last_w, eng, True)
        for b in writes:
            self._add_dep(deps, b.last_w, eng, False, True)
            for r in b.readers:
                self._add_dep(deps, r, eng, False)
        if dma is not None:
            kind = "sw" if eng == "pool" else "hw"
            if dma.sub is None:
                dma.sub = {}
            if kind not in dma.sub:
                dma.sub[kind] = Buf(dma.name + "_" + kind)
            dma = dma.sub[kind]
            if dma.sem is None:
                if cx.sem_pool[kind]:
                    dma.sem, dma.cnt = cx.sem_pool[kind].pop()
                else:
                    cx.nsem += 1
                    dma.sem = cx.es.enter_context(cx.nc.semaphore("ds%d" % cx.nsem))
                    dma.cnt = 0
                dma.kind = kind
                cx.dma_bufs.append(dma)
            assert dma.kind == kind, dma.name
            dma.cnt += 16
            sig = (dma.sem, dma.cnt, eng, True)
            inc = (dma.sem, 16)
        else:
            cx.eng_cnt[eng] += 1
            sig = (cx.eng_sem[eng], cx.eng_cnt[eng], eng, False)
            inc = (cx.eng_sem[eng], 1)
        w = self.waited[eng]
        waits = []
        best = {}
        for sem, val in deps:
            k = id(sem)
            if w.get(k, 0) >= val:
                continue
            if k not in best or best[k][1] < val:
                best[k] = (sem, val)
        for k, (sem, val) in best.items():
            w[k] = val
            waits.append((sem, val))
        self.ops[eng].append((waits, fn, inc))
        for b in reads:
            b.readers.append(sig)
            self.touched.append(b)
        for b in writes:
            b.last_w = sig
            b.readers = []
            self.touched.append(b)
        return sig

    def run(self):
        cx = self.cx
        nc = cx.nc
        fin = [(b.sem, b.cnt) for b in cx.dma_bufs]

        def emit(engname):
            lst = self.ops[engname]

            def body(e):
                for waits, fn, inc in lst:
                    for sem, val in waits:
                        e.wait_ge(sem, val)
                    ins = fn(e)
                    ins.then_inc(inc[0], inc[1])
                if engname == "sp":
                    for sem, val in fin:
                        e.wait_ge(sem, val)
            return body

        with nc.Block() as blk:
            blk.sync(emit("sp"))
            blk.tensor(emit("pe"))
            blk.vector(emit("dve"))
            blk.scalar(emit("act"))
            blk.gpsimd(emit("pool"))
        for b in self.touched:
            b.last_w = None
            b.readers = []
        for b in cx.dma_bufs:
            cx.sem_pool[b.kind].append((b.sem, b.cnt))
            b.sem = None
        cx.dma_bufs = []


def _ap(t, off, pat):
    return bass.AP(tensor=t, offset=off, ap=[list(p) for p in pat])


def build_program():
    nc = bass.Bass("TRN2", target_bir_lowering=False)
    es = ExitStack()
    cx = Ctx(nc, es)
    skind = "ExternalOutput" if DEBUG else "Internal"

    def din(name, shape, dt=F32):
        return nc.dram_tensor(name, list(shape), dt, kind="ExternalInput")

    def dscr(name, shape, dt=BF16):
        return nc.dram_tensor(name, list(shape), dt, kind=skind)

    x_d = din("x", [2, SEQ, D])
    meta_d = din("meta_tokens", [16, D])
    gmix_d = din("norm_mix_gain", [1, D])
    win_d = din("w_in", [D, IN_W])
    tbl_d = din("na_tbl", [8, NCFG, 128, 128])
    mb_d = din("meta_bias", [8, 16])
    wgf_d = din("w_gate_up_fwd", [16, 256])
    bgf_d = din("b_gate_fwd", [1, 256])
    wgb_d = din("w_gate_up_bwd", [16, 256])
    bgb_d = din("b_gate_bwd", [1, 256])
    ggla_d = din("gla_norm_gain", [1, 128])
    wout_d = din("w_out", [D, D])
    gffn_d = din("norm_ffn_gain", [1, D])
    wg_d = din("w_ffn_gate", [D, DFF])
    wu_d = din("w_ffn_up", [D, DFF])
    wd_d = din("w_ffn_down", [DFF, D])
    gfin_d = din("norm_final_gain", [1, D])
    consts_d = din("consts", [128, NCONST])
    out_d = nc.dram_tensor("out", [2, SEQ, D], F32, kind="ExternalOutput")

    QT_d = dscr("s_qt", [8, 128, SEQ])
    KT_d = dscr("s_kt", [8, 128, SEQ])
    KTM_d = dscr("s_ktm", [4, 128, 128])
    VA_d = dscr("s_va", [NTILES, 128, 520])
    GQ_d = dscr("s_gq", [8, 128, SEQ])
    GK_d = dscr("s_gk", [8, 128, SEQ])
    GKD_d = dscr("s_gkd", [NTILES, 128, 512])
    GV_d = dscr("s_gv", [NTILES, 128, 512])
    GR_d = dscr("s_gr", [NTILES, 128, 512])
    MIXT_d = dscr("s_mixt", [2, 8, 128, SEQ])
    H1_d = dscr("s_h1", [32, 128, D], F32)
    HNT_d = dscr("s_hnt", [8, 128, 8 * 512])

    def sb(name, shape, dt):
        return es.enter_context(nc.sbuf_tensor("sb_" + name, list(shape), dt))

    consts = sb("consts", [128, NCONST], F32)
    ident_bf = sb("ident_bf", [128, 128], BF16)
    maskF = sb("maskF", [128, 128], BF16)
    maskB = sb("maskB", [128, 128], BF16)
    dec = sb("dec", [128, NTILES, 4], F32)
    b_consts = cx.buf("consts")
    b_cbf = cx.buf("cbf")
    b_dec = cx.buf("dec")

    def phase1():
        st = ExitStack()

        def sbt(name, shape, dt):
            return st.enter_context(nc.sbuf_tensor("p1_" + name, list(shape), dt))

        def pst(name, shape, dt):
            return st.enter_context(nc.psum_tensor("p1_" + name, list(shape), dt))

        ph = Ph(cx, "p1")
        win = sbt("win", [128, 8, IN_W], BF16)
        WBLK = 776
        b_win = cx.bufs(4, "win")

        def wb(c0, n):
            return [b_win[b] for b in range(4) if b * WBLK < c0 + n and (b + 1) * WBLK > c0]
        w2 = sbt("w2", [64, 512], BF16)
        w2f = sbt("w2f", [64, 512], F32)
        b_w2f = cx.buf("w2f")
        b_w2 = cx.buf("w2")
        gmix = sbt("gmix", [128, D], F32)
        b_gmix = cx.buf("gmix")
        xt = [sbt("xt%d" % i, [128, D], F32) for i in range(3)]
        b_xt = cx.bufs(3, "xt")
        junk = sbt("junk", [128, D], BF16)
        b_junk = cx.buf("junk")
        ss = sbt("ss", [128, 8], F32)
        b_ss = cx.bufs(4, "ss")
        xn = [sbt("xn%d" % i, [128, D], BF16) for i in range(4)]
        b_xn = cx.bufs(4, "xn")
        xnT = [sbt("xnT%d" % i, [128, 8, 512], BF16) for i in range(2)]
        b_xnT = [cx.bufs(4, "xnT%d_" % i) for i in range(2)]
        glT = [sbt("glT%d" % i, [64, 512], BF16) for i in range(2)]
        b_glT = cx.bufs(2, "glT")
        stQK = [sbt("stQK%d" % i, [128, 8, 512], BF16) for i in range(2)]
        b_stQK = cx.bufs(2, "stQK")
        gqk = [sbt("gqk%d" % i, [128, 4, 512], F32) for i in range(2)]
        b_gqk = cx.bufs(2, "gqk")
        stG = [sbt("stG%d" % i, [128, 8, 512], BF16) for i in range(2)]
        b_stG = cx.bufs(2, "stG")
        stVA = [sbt("stVA%d" % i, [128, 520], BF16) for i in range(2)]
        b_stVA = cx.bufs(2, "stVA")
        stGV = [sbt("stGV%d" % i, [128, 512], BF16) for i in range(2)]
        b_stGV = cx.bufs(2, "stGV")
        rraw = [sbt("rraw%d" % i, [128, 512], BF16) for i in range(2)]
        b_rraw = cx.bufs(2, "rraw")
        stGR = [sbt("stGR%d" % i, [128, 512], BF16) for i in range(2)]
        b_stGR = cx.bufs(2, "stGR")
        ktok = [sbt("ktok%d" % i, [128, 256], F32) for i in range(2)]
        b_ktok = cx.bufs(2, "ktok")
        stKD = [sbt("stKD%d" % i, [128, 512], BF16) for i in range(2)]
        b_stKD = cx.bufs(2, "stKD")
        ge = [sbt("ge%d" % i, [128, 512], F32) for i in range(2)]
        b_ge = cx.bufs(2, "ge")
        gp = [sbt("gp%d" % i, [128, 512], F32) for i in range(2)]
        b_gp = cx.bufs(2, "gp")
        Eq = [sbt("Eq%d" % i, [128, 4, 128], F32) for i in range(2)]
        b_Eq = cx.bufs(2, "Eq")
        Ek = [sbt("Ek%d" % i, [128, 4, 128], F32) for i in range(2)]
        b_Ek = cx.bufs(2, "Ek")
        Ed = [sbt("Ed%d" % i, [128, 512], F32) for i in range(2)]
        b_Ed = cx.bufs(2, "Ed")

        psT = [pst("psT%d" % i, [128, D], BF16) for i in range(2)]
        b_psT = cx.bufs(2, "psT")
        psF = [pst("psF%d" % i, [128, 512], F32) for i in range(4)]
        b_psF = cx.bufs(4, "psF")
        psK = psF
        b_psK = b_psF
        psB = pst("psB", [128, 4, 128], F32)
        b_psB = cx.buf("psB")
        psS = pst("psS", [128, 512], F32)
        b_psS = cx.buf("psS")

        ph.op("sp", lambda e: e.dma_start(out=consts[:, :], in_=consts_d.ap()),
              writes=[b_consts], dma=b_consts)
        ph.op("sp", lambda e: e.dma_start(out=gmix[:, :], in_=gmix_d.ap().broadcast_to([128, D])),
              writes=[b_gmix], dma=b_gmix)
        ph.op("dve", lambda e: e.tensor_copy(out=ident_bf[:, :], in_=consts[:, K_ID:K_ID + 128]),
              reads=[b_consts], writes=[b_cbf])
        ph.op("dve", lambda e: e.tensor_copy(out=maskF[:, :], in_=consts[:, K_MF:K_MF + 128]),
              reads=[b_consts], writes=[b_cbf])
        ph.op("dve", lambda e: e.tensor_copy(out=maskB[:, :], in_=consts[:, K_MB:K_MB + 128]),
              reads=[b_consts], writes=[b_cbf])
        ph.op("pool", lambda e: e.memset(w2f[:, :], 0.0), writes=[b_w2f])
        ph.op("sp", lambda e: e.dma_start(out=w2f[0:16, 0:256], in_=wgf_d.ap()), writes=[b_w2f], dma=b_w2f)
        ph.op("sp", lambda e: e.dma_start(out=w2f[16:32, 256:512], in_=wgb_d.ap()), writes=[b_w2f], dma=b_w2f)
        ph.op("sp", lambda e: e.dma_start(out=w2f[32:33, 0:256], in_=bgf_d.ap()), writes=[b_w2f], dma=b_w2f)
        ph.op("sp", lambda e: e.dma_start(out=w2f[32:33, 256:512], in_=bgb_d.ap()), writes=[b_w2f], dma=b_w2f)
        ph.op("dve", lambda e: e.tensor_copy(out=w2[:, :], in_=w2f[:, :]), reads=[b_w2f], writes=[b_w2])
        for i in range(2):
            ph.op("pool", (lambda i: lambda e: e.memset(glT[i][32:64, :], 1.0))(i), writes=[b_glT[i]])
            ph.op("pool", (lambda i: lambda e: e.memset(stVA[i][:, :], 1.0))(i), writes=[b_stVA[i]])
        for b in range(4):
            ph.op("pool", (lambda b: lambda e: e.dma_start(
                out=win[:, :, b * WBLK:(b + 1) * WBLK],
                in_=win_d.ap()[:, b * WBLK:(b + 1) * WBLK].rearrange("(k p) c -> p k c", p=128),
                max_dma_last_dim=4096))(b),
                writes=[b_win[b]], dma=b_win[b])

        state = {"xi": 0}

        def stageA1(g):
            tiles = [META_T] if g == 8 else [g * 4 + j for j in range(4)]
            for j, T in enumerate(tiles):
                xi = state["xi"] % 3
                state["xi"] += 1
                ti = j
                if T == META_T:
                    ph.op("pool", (lambda xi: lambda e: e.memset(xt[xi][:, :], 0.0))(xi), writes=[b_xt[xi]])
                    ph.op("sp", (lambda xi: lambda e: e.dma_start(out=xt[xi][112:128, :], in_=meta_d.ap()))(xi),
                          writes=[b_xt[xi]], dma=b_xt[xi])
                else:
                    s_, t_ = T // 16, T % 16
                    ph.op("sp", (lambda xi, s_, t_: lambda e: e.dma_start(
                        out=xt[xi][:, :], in_=x_d.ap()[s_, t_ * 128:(t_ + 1) * 128, :]))(xi, s_, t_),
                        writes=[b_xt[xi]], dma=b_xt[xi])
                ph.op("dve", (lambda xi, ti: lambda e: e.scalar_tensor_tensor(
                    out=junk[:, :], in0=xt[xi][:, :], scalar=1.0, in1=xt[xi][:, :],
                    op0=ALU.mult, op1=ALU.mult, accum_out=ss[:, ti * 2:ti * 2 + 1]))(xi, ti),
                    reads=[b_xt[xi]], writes=[b_junk, b_ss[ti]])
                ph.op("act", (lambda ti: lambda e: e.activation(
                    out=ss[:, ti * 2 + 1:ti * 2 + 2], in_=ss[:, ti * 2:ti * 2 + 1], func=AF.Ln,
                    scale=1.0 / D, bias=EPS))(ti), reads=[b_ss[ti]], writes=[b_ss[ti]])
                ph.op("act", (lambda ti: lambda e: e.activation(
                    out=ss[:, ti * 2 + 1:ti * 2 + 2], in_=ss[:, ti * 2 + 1:ti * 2 + 2], func=AF.Exp,
                    scale=-0.5))(ti), reads=[b_ss[ti]], writes=[b_ss[ti]])
                ph.op("dve", (lambda xi, ti: lambda e: e.scalar_tensor_tensor(
                    out=xn[ti][:, :], in0=xt[xi][:, :], scalar=ss[:, ti * 2 + 1:ti * 2 + 2], in1=gmix[:, :],
                    op0=ALU.mult, op1=ALU.mult))(xi, ti),
                    reads=[b_xt[xi], b_ss[ti], b_gmix], writes=[b_xn[ti]])

        def stageA2(g, gs):
            tiles = [META_T] if g == 8 else [g * 4 + j for j in range(4)]
            for j, T in enumerate(tiles):
                ti = j
                pt = j % 2

                def tr(e, ti=ti, pt=pt):
                    ins = None
                    for kc in range(8):
                        ins = e.transpose(out=psT[pt][:, kc * 128:(kc + 1) * 128],
                                          in_=xn[ti][:, kc * 128:(kc + 1) * 128], identity=ident_bf[:, :])
                    return ins
                ph.op("pe", tr, reads=[b_xn[ti], b_cbf], writes=[b_psT[pt]])
                ph.op("act", (lambda pt, gs, j: lambda e: e.activation(
                    out=xnT[gs][:, :, j * 128:(j + 1) * 128],
                    in_=psT[pt][:, :].rearrange("p (k t) -> p k t", k=8), func=AF.Copy))(pt, gs, j),
                    reads=[b_psT[pt]], writes=[b_xnT[gs][j]])

        fcount = {"f": 0, "k": 0}

        def stageB(g, gs):
            tiles = [META_T] if g == 8 else [g * 4 + j for j in range(4)]
            nt = len(tiles)
            N = nt * 128
            xb = b_xnT[gs][:nt]
            s = g // 4
            tok0 = (g % 4) * 512
            fm = [(C_NAQ + 128 * i, 128, "q", i) for i in range(4)] + \
                 [(C_NAK + 128 * i, 128, "k", i) for i in range(4)] + \
                 [(C_GQ + 128 * i, 128, "gq", i) for i in range(2)] + \
                 [(C_GK + 128 * i, 128, "gk", i) for i in range(2)] + \
                 [(C_GF, 32, "gl", 0)]
            for (c0, M, kind, i) in fm:
                pi = fcount["f"] % 4
                fcount["f"] += 1

                def mm(e, c0=c0, M=M, pi=pi):
                    ins = None
                    for kc in range(8):
                        ins = e.matmul(out=psF[pi][0:M, 0:N], lhsT=win[:, kc, c0:c0 + M],
                                       rhs=xnT[gs][:, kc, 0:N], start=(kc == 0), stop=(kc == 7))
                    return ins
                ph.op("pe", mm, reads=wb(c0, M) + xb, writes=[b_psF[pi]])
                if kind == "q":
                    ph.op("act", (lambda pi, i: lambda e: e.activation(
                        out=stQK[gs][:, i, 0:N], in_=psF[pi][:, 0:N], func=AF.Copy, scale=0.125))(pi, i),
                        reads=[b_psF[pi]], writes=[b_stQK[gs]])
                elif kind == "k":
                    ph.op("dve", (lambda pi, i: lambda e: e.tensor_copy(
                        out=stQK[gs][:, 4 + i, 0:N], in_=psF[pi][:, 0:N]))(pi, i),
                        reads=[b_psF[pi]], writes=[b_stQK[gs]])
                elif kind == "gq":
                    ph.op("act", (lambda pi, i: lambda e: e.activation(
                        out=gqk[gs][:, i, 0:N], in_=psF[pi][:, 0:N], func=AF.Copy, scale=0.125))(pi, i),
                        reads=[b_psF[pi]], writes=[b_gqk[gs]])
                elif kind == "gk":
                    ph.op("dve", (lambda pi, i: lambda e: e.tensor_copy(
                        out=gqk[gs][:, 2 + i, 0:N], in_=psF[pi][:, 0:N]))(pi, i),
                        reads=[b_psF[pi]], writes=[b_gqk[gs]])
                else:
                    ph.op("dve", (lambda pi: lambda e: e.tensor_copy(
                        out=glT[gs][0:32, 0:N], in_=psF[pi][0:32, 0:N]))(pi),
                        reads=[b_psF[pi]], writes=[b_glT[gs]])
            if g == 8:
                ph.op("pool", lambda e: e.dma_start(
                    out=KTM_d.ap().rearrange("c p t -> p c t"), in_=stQK[gs][:, 4:8, 0:128]),
                    reads=[b_stQK[gs]], dma=b_stQK[gs])
            else:
                ph.op("pool", lambda e: e.dma_start(
                    out=QT_d.ap()[s * 4:(s + 1) * 4, :, tok0:tok0 + 512].rearrange("c p t -> p c t"),
                    in_=stQK[gs][:, 0:4, :]), reads=[b_stQK[gs]], dma=b_stQK[gs])
                ph.op("pool", lambda e: e.dma_start(
                    out=KT_d.ap()[s * 4:(s + 1) * 4, :, tok0:tok0 + 512].rearrange("c p t -> p c t"),
                    in_=stQK[gs][:, 4:8, :]), reads=[b_stQK[gs]], dma=b_stQK[gs])

            def tile_ops(j, T):
                k2 = fcount["k"] % 2
                fcount["k"] += 1
                tsl = slice(j * 128, (j + 1) * 128)

                def tokmm(e, c0, n, pi, j=j):
                    ins = None
                    for kc in range(8):
                        ins = e.matmul(out=psK[pi][:, 0:n], lhsT=xnT[gs][:, kc, j * 128:(j + 1) * 128],
                                       rhs=win[:, kc, c0:c0 + n], start=(kc == 0), stop=(kc == 7))
                    return ins
                pi = fcount["f"] % 4; fcount["f"] += 1
                ph.op("pe", (lambda pi, tk: lambda e: tk(e, C_NAV, 512, pi))(pi, tokmm),
                      reads=wb(C_NAV, 512) + [xb[j]], writes=[b_psK[pi]])
                ph.op("act", (lambda pi, k2: lambda e: e.activation(
                    out=stVA[k2][:, :].rearrange("p (h c) -> p h c", h=8)[:, :, 0:64],
                    in_=psK[pi][:, :].rearrange("p (h c) -> p h c", h=8), func=AF.Copy))(pi, k2),
                    reads=[b_psK[pi]], writes=[b_stVA[k2]])
                ph.op("pool", (lambda k2, T: lambda e: e.dma_start(out=VA_d.ap()[T], in_=stVA[k2][:, :]))(k2, T),
                      reads=[b_stVA[k2]], dma=b_stVA[k2])
                pi = fcount["f"] % 4; fcount["f"] += 1
                ph.op("pe", (lambda pi, tk: lambda e: tk(e, C_GV, 512, pi))(pi, tokmm),
                      reads=wb(C_GV, 512) + [xb[j]], writes=[b_psK[pi]])
                ph.op("dve", (lambda pi, k2: lambda e: e.tensor_copy(out=stGV[k2][:, :], in_=psK[pi][:, :]))(pi, k2),
                      reads=[b_psK[pi]], writes=[b_stGV[k2]])
                ph.op("pool", (lambda k2, T: lambda e: e.dma_start(out=GV_d.ap()[T], in_=stGV[k2][:, :]))(k2, T),
                      reads=[b_stGV[k2]], dma=b_stGV[k2])
                pi = fcount["f"] % 4; fcount["f"] += 1
                ph.op("pe", (lambda pi, tk: lambda e: tk(e, C_GR, 512, pi))(pi, tokmm),
                      reads=wb(C_GR, 512) + [xb[j]], writes=[b_psK[pi]])
                ph.op("act", (lambda pi, k2: lambda e: e.activation(
                    out=ge[k2][:, :], in_=psK[pi][:, :], func=AF.Exp, scale=-1.0))(pi, k2),
                    reads=[b_psK[pi]], writes=[b_ge[k2]])
                ph.op("act", (lambda pi, k2: lambda e: e.activation(
                    out=rraw[k2][:, :], in_=psK[pi][:, :], func=AF.Copy))(pi, k2),
                    reads=[b_psK[pi]], writes=[b_rraw[k2]])
                ph.op("act", (lambda k2: lambda e: e.activation(
                    out=ge[k2][:, :], in_=ge[k2][:, :], func=AF.Ln, bias=1.0))(k2),
                    reads=[b_ge[k2]], writes=[b_ge[k2]])
                ph.op("act", (lambda k2: lambda e: e.activation(
                    out=ge[k2][:, :], in_=ge[k2][:, :], func=AF.Exp, scale=-1.0))(k2),
                    reads=[b_ge[k2]], writes=[b_ge[k2]])
                ph.op("dve", (lambda k2: lambda e: e.tensor_tensor(
                    out=stGR[k2][:, :], in0=rraw[k2][:, :], in1=ge[k2][:, :], op=ALU.mult))(k2),
                    reads=[b_rraw[k2], b_ge[k2]], writes=[b_stGR[k2]])
                ph.op("pool", (lambda k2, T: lambda e: e.dma_start(out=GR_d.ap()[T], in_=stGR[k2][:, :]))(k2, T),
                      reads=[b_stGR[k2]], dma=b_stGR[k2])
                pi = fcount["f"] % 4; fcount["f"] += 1
                ph.op("pe", (lambda pi, tk: lambda e: tk(e, C_GK, 256, pi))(pi, tokmm),
                      reads=wb(C_GK, 256) + [xb[j]], writes=[b_psK[pi]])
                ph.op("act", (lambda pi, k2: lambda e: e.activation(
                    out=ktok[k2][:, :], in_=psK[pi][:, 0:256], func=AF.Copy))(pi, k2),
                    reads=[b_psK[pi]], writes=[b_ktok[k2]])
                pi = fcount["f"] % 4; fcount["f"] += 1
                ph.op("pe", (lambda pi, j: lambda e: e.matmul(
                    out=psK[pi][:, :], lhsT=glT[gs][:, j * 128:(j + 1) * 128], rhs=w2[:, :],
                    start=True, stop=True))(pi, j),
                    reads=[b_glT[gs], b_w2], writes=[b_psK[pi]])
                ph.op("act", (lambda pi, k2: lambda e: e.activation(
                    out=gp[k2][:, :], in_=psK[pi][:, :], func=AF.Exp, scale=-1.0))(pi, k2),
                    reads=[b_psK[pi]], writes=[b_gp[k2]])
                ph.op("act", (lambda k2: lambda e: e.activation(
                    out=gp[k2][:, :], in_=gp[k2][:, :], func=AF.Ln, bias=1.0))(k2),
                    reads=[b_gp[k2]], writes=[b_gp[k2]])

                yield
                def cum(e, k2=k2):
                    ins = None
                    for d_ in range(2):
                        tri = consts[:, K_TRIF:K_TRIF + 128] if d_ == 0 else consts[:, K_TRIB:K_TRIB + 128]
                        for hp in range(2):
                            c0 = d_ * 256 + hp * 128
                            ins = e.matmul(out=psB[:, d_ * 2 + hp, :], lhsT=gp[k2][:, c0:c0 + 128], rhs=tri,
                                           start=True, stop=True)
                    for d_ in range(2):
                        su = consts[:, K_SUF:K_SUF + 128] if d_ == 0 else consts[:, K_SUB:K_SUB + 128]
                        ins = e.matmul(out=psS[:, d_ * 256:(d_ + 1) * 256], lhsT=su,
                                       rhs=gp[k2][:, d_ * 256:(d_ + 1) * 256], start=True, stop=True)
                    return ins
                ph.op("pe", cum, reads=[b_gp[k2], b_consts], writes=[b_psB, b_psS])
                ph.op("act", (lambda k2: lambda e: e.activation(
                    out=Eq[k2][:, :, :], in_=psB[:, :, :], func=AF.Exp))(k2),
                    reads=[b_psB], writes=[b_Eq[k2]])
                ph.op("act", (lambda k2: lambda e: e.activation(
                    out=Ek[k2][:, :, :], in_=psB[:, :, :], func=AF.Exp, scale=-1.0))(k2),
                    reads=[b_psB], writes=[b_Ek[k2]])
                ph.op("act", (lambda k2: lambda e: e.activation(
                    out=Ed[k2][:, :], in_=psS[:, :], func=AF.Exp))(k2),
                    reads=[b_psS], writes=[b_Ed[k2]])
                ph.op("pool", (lambda k2, T: lambda e: e.tensor_copy(
                    out=dec[:, T, 0:2], in_=Eq[k2][:, 0:2, 127]))(k2, T),
                    reads=[b_Eq[k2]], writes=[b_dec])
                ph.op("pool", (lambda k2, T: lambda e: e.tensor_copy(
                    out=dec[:, T, 2:4], in_=Eq[k2][:, 2:4, 0]))(k2, T),
                    reads=[b_Eq[k2]], writes=[b_dec])
                for d_ in range(2):
                    ph.op("dve", (lambda k2, d_: lambda e: e.tensor_tensor(
                        out=stKD[k2][:, d_ * 256:(d_ + 1) * 256], in0=ktok[k2][:, :],
                        in1=Ed[k2][:, d_ * 256:(d_ + 1) * 256], op=ALU.mult))(k2, d_),
                        reads=[b_ktok[k2], b_Ed[k2]], writes=[b_stKD[k2]])
                ph.op("pool", (lambda k2, T: lambda e: e.dma_start(out=GKD_d.ap()[T], in_=stKD[k2][:, :]))(k2, T),
                      reads=[b_stKD[k2]], dma=b_stKD[k2])
                if T != META_T:
                    for d_ in range(2):
                        ph.op("dve", (lambda k2, d_, tsl: lambda e: e.tensor_tensor(
                            out=stG[gs][:, d_ * 2:d_ * 2 + 2, tsl], in0=gqk[gs][:, 0:2, tsl],
                            in1=Eq[k2][:, d_ * 2:d_ * 2 + 2, :], op=ALU.mult))(k2, d_, tsl),
                            reads=[b_gqk[gs], b_Eq[k2]], writes=[b_stG[gs]])
                        ph.op("pool", (lambda k2, d_, tsl: lambda e: e.tensor_tensor(
                            out=stG[gs][:, 4 + d_ * 2:4 + d_ * 2 + 2, tsl], in0=gqk[gs][:, 2:4, tsl],
                            in1=Ek[k2][:, d_ * 2:d_ * 2 + 2, :], op=ALU.mult))(k2, d_, tsl),
                            reads=[b_gqk[gs], b_Ek[k2]], writes=[b_stG[gs]])
            gens = [tile_ops(j, T) for j, T in enumerate(tiles)]
            for j in range(len(gens)):
                next(gens[j])
                if j >= 1:
                    for _ in gens[j - 1]:
                        pass
            for _ in gens[-1]:
                pass
            if g != 8:
                ph.op("pool", lambda e: e.dma_start(
                    out=GQ_d.ap()[s * 4:(s + 1) * 4, :, tok0:tok0 + 512].rearrange("c p t -> p c t"),
                    in_=stG[gs][:, 0:4, :]), reads=[b_stG[gs]], dma=b_stG[gs])
                ph.op("pool", lambda e: e.dma_start(
                    out=GK_d.ap()[s * 4:(s + 1) * 4, :, tok0:tok0 + 512].rearrange("c p t -> p c t"),
                    in_=stG[gs][:, 4:8, :]), reads=[b_stG[gs]], dma=b_stG[gs])

        order = [8, 0, 1, 2, 3, 4, 5, 6, 7]
        stageA1(order[0])
        stageA2(order[0], 0)
        for i, g in enumerate(order):
            if i + 1 < len(order):
                stageA1(order[i + 1])
            stageB(g, i % 2)
            if i + 1 < len(order):
                stageA2(order[i + 1], (i + 1) % 2)
        ph.run()
        st.close()


    def rms_rstd(ph, eng_ss, src_ap_fn, b_src, ssbuf, b_ssb, col, junk_ap_fn, b_junkb, n):
        ph.op("dve", lambda e: e.scalar_tensor_tensor(
            out=junk_ap_fn(), in0=src_ap_fn(), scalar=1.0, in1=src_ap_fn(),
            op0=ALU.mult, op1=ALU.mult, accum_out=ssbuf[:, col:col + 1]),
            reads=[b_src], writes=[b_junkb, b_ssb])
        ph.op("act", lambda e: e.activation(
            out=ssbuf[:, col + 1:col + 2], in_=ssbuf[:, col:col + 1], func=AF.Ln, scale=1.0 / n, bias=EPS),
            reads=[b_ssb], writes=[b_ssb])
        ph.op("act", lambda e: e.activation(
            out=ssbuf[:, col + 1:col + 2], in_=ssbuf[:, col + 1:col + 2], func=AF.Exp, scale=-0.5),
            reads=[b_ssb], writes=[b_ssb])

    def phase2():
        st = ExitStack()

        def sbt(name, shape, dt):
            return st.enter_context(nc.sbuf_tensor("p2_" + name, list(shape), dt))

        def pst(name, shape, dt):
            return st.enter_context(nc.psum_tensor("p2_" + name, list(shape), dt))

        ph = Ph(cx, "p2")
        Sf = sbt("Sf", [128, 32, 2, 256], BF16)
        b_Sf = cx.bufs(32, "Sf")
        Sinit = sbt("Sinit", [128, 2, 256], F32)
        b_Sinit = cx.buf("Sinit")
        S32 = sbt("S32", [128, 2, 256], F32)
        b_S32 = cx.buf("S32")
        Sb32 = sbt("Sb32", [128, 2, 256], F32)
        b_Sb32 = cx.buf("Sb32")
        Sbb = [sbt("Sbb%d" % i, [128, 2, 256], BF16) for i in range(2)]
        b_Sbb = cx.bufs(2, "Sbb")
        ggla = sbt("ggla", [128, 128], F32)
        b_ggla = cx.buf("ggla")
        kd = [sbt("kd%d" % i, [128, 512], BF16) for i in range(3)]
        b_kd = cx.bufs(3, "kd")
        gv = [sbt("gv%d" % i, [128, 512], BF16) for i in range(3)]
        b_gv = cx.bufs(3, "gv")
        gr = [sbt("gr%d" % i, [128, 512], BF16) for i in range(3)]
        b_gr = cx.bufs(3, "gr")
        gqs = [sbt("gqs%d" % i, [128, 4, SEQ], BF16) for i in range(2)]
        b_gqs = cx.bufs(2, "gqs")
        gks = [sbt("gks%d" % i, [128, 4, SEQ], BF16) for i in range(2)]
        b_gks = cx.bufs(2, "gks")
        AT = [sbt("AT%d" % i, [128, 128], BF16) for i in range(4)]
        b_AT = cx.bufs(4, "AT")
        osb = [sbt("osb%d" % i, [128, 512], F32) for i in range(2)]
        b_osb = cx.bufs(2, "osb")
        jk = sbt("jk", [128, 512], BF16)
        b_jk = cx.bufs(4, "jk")
        ssg = [sbt("ssg%d" % i, [128, 8], F32) for i in range(2)]
        b_ssg = cx.bufs(2, "ssg")
        t2 = [sbt("t2_%d" % i, [128, 512], F32) for i in range(2)]
        b_t2 = cx.bufs(2, "t2")
        gout = [sbt("gout%d" % i, [128, 512], BF16) for i in range(2)]
        b_gout = cx.bufs(2, "gout")
        mst = [sbt("mst%d" % i, [128, 4, 128], BF16) for i in range(2)]
        b_mst = cx.bufs(2, "mst")

        psA = [pst("psA%d" % i, [128, 512], F32) for i in range(2)]
        b_psA = cx.bufs(2, "psA")
        psO = [pst("psO%d" % i, [128, 512], F32) for i in range(2)]
        b_psO = cx.bufs(2, "psO")
        psKVt = [pst("psKV%d" % i, [128, 512], F32) for i in range(2)]
        b_psKV = cx.bufs(2, "psKV")
        psT = pst("psT", [128, 1024], BF16)
        b_psT = cx.buf("psT")
        psKV1 = pst("psKVs", [128, 512], F32)
        b_psKV1 = cx.buf("psKV1")

        ph.op("sp", lambda e: e.dma_start(out=ggla[:, :], in_=ggla_d.ap().broadcast_to([128, 128])),
              writes=[b_ggla], dma=b_ggla)

        for s_ in range(2):
            ph.op("sp", (lambda s_: lambda e: e.dma_start(
                out=gqs[s_][:, :, :], in_=GQ_d.ap()[s_ * 4:(s_ + 1) * 4].rearrange("c p t -> p c t")))(s_),
                writes=[b_gqs[s_]], dma=b_gqs[s_])
            ph.op("sp", (lambda s_: lambda e: e.dma_start(
                out=gks[s_][:, :, :], in_=GK_d.ap()[s_ * 4:(s_ + 1) * 4].rearrange("c p t -> p c t")))(s_),
                writes=[b_gks[s_]], dma=b_gks[s_])
        cnt = {"ld": 0, "at": 0}
        kd1 = [sbt("kd1_%d" % i, [128, 512], BF16) for i in range(2)]
        b_kd1 = cx.bufs(2, "kd1")
        gv1 = [sbt("gv1_%d" % i, [128, 512], BF16) for i in range(2)]
        b_gv1 = cx.bufs(2, "gv1")

        def load_kv(T):
            i = cnt["ld"] % 2
            cnt["ld"] += 1
            ph.op("sp", lambda e: e.dma_start(out=kd1[i][:, :], in_=GKD_d.ap()[T]), writes=[b_kd1[i]], dma=b_kd1[i])
            ph.op("sp", lambda e: e.dma_start(out=gv1[i][:, :], in_=GV_d.ap()[T]), writes=[b_gv1[i]], dma=b_gv1[i])
            return i

        def kv_mm1(i):
            def f(e):
                ins = None
                for hp in range(2):
                    ins = e.matmul(out=psKV1[:, hp * 256:(hp + 1) * 256], lhsT=kd1[i][:, hp * 128:hp * 128 + 128],
                                   rhs=gv1[i][:, hp * 256:(hp + 1) * 256], start=True, stop=True)
                return ins
            return f

        def kv_mm(i, hp, d_):
            return lambda e: e.matmul(out=psKVt[hp][:, 0:256], lhsT=kd[i][:, d_ * 256 + hp * 128:d_ * 256 + hp * 128 + 128],
                                      rhs=gv[i][:, hp * 256:(hp + 1) * 256], start=True, stop=True)

        i = load_kv(META_T)
        ph.op("pe", kv_mm1(i), reads=[b_kd1[i], b_gv1[i]], writes=[b_psKV1])
        ph.op("act", lambda e: e.activation(
            out=Sinit[:, :, :], in_=psKV1[:, :].rearrange("p (h c) -> p h c", h=2), func=AF.Copy),
            reads=[b_psKV1], writes=[b_Sinit])

        S32pp = [S32, sbt("S32b", [128, 2, 256], F32)]
        b_S32pp = [b_S32, cx.buf("S32b")]

        def sweep1_step(s, t):
            T = s * 16 + t
            src, dst = S32pp[t % 2], S32pp[(t + 1) % 2]
            bsrc, bdst = b_S32pp[t % 2], b_S32pp[(t + 1) % 2]
            if t == 0:
                ph.op("pool", lambda e: e.tensor_copy(out=src[:, :, :], in_=Sinit[:, :, :]),
                      reads=[b_Sinit], writes=[bsrc])
            ph.op("act", lambda e: e.activation(out=Sf[:, T, :, :], in_=src[:, :, :], func=AF.Copy),
                  reads=[bsrc], writes=[b_Sf[T]])
            if t == 15:
                return
            i = load_kv(T)
            ph.op("pe", kv_mm1(i), reads=[b_kd1[i], b_gv1[i]], writes=[b_psKV1])
            for hp in range(2):
                ph.op("dve", (lambda hp: lambda e: e.scalar_tensor_tensor(
                    out=dst[:, hp, :], in0=src[:, hp, :], scalar=dec[:, T, hp:hp + 1],
                    in1=psKV1[:, hp * 256:(hp + 1) * 256], op0=ALU.mult, op1=ALU.add))(hp),
                    reads=[bsrc, b_psKV1, b_dec], writes=[bdst])

        for t in range(16):
            sweep1_step(0, t)
        tiles2 = [(s, t) for s in range(2) for t in range(15, -1, -1)]

        def part1(k):
            s, t = tiles2[k]
            T = s * 16 + t
            cur = k % 2
            nxt = 1 - cur
            if t == 15:
                ph.op("pool", lambda e: e.memset(Sb32[:, :, :], 0.0), writes=[b_Sb32])
                ph.op("pool", lambda e: e.memset(Sbb[cur][:, :, :], 0.0), writes=[b_Sbb[cur]])
            i = k % 3
            ph.op("sp", lambda e: e.dma_start(out=kd[i][:, :], in_=GKD_d.ap()[T]), writes=[b_kd[i]], dma=b_kd[i])
            ph.op("sp", lambda e: e.dma_start(out=gv[i][:, :], in_=GV_d.ap()[T]), writes=[b_gv[i]], dma=b_gv[i])
            ph.op("sp", lambda e: e.dma_start(out=gr[i][:, :], in_=GR_d.ap()[T]), writes=[b_gr[i]], dma=b_gr[i])
            ph.op("pool", lambda e: e.tensor_tensor(
                out=t2[k % 2][:, :].rearrange("p (h c) -> p h c", h=4),
                in0=gr[i][:, :].rearrange("p (h c) -> p h c", h=4),
                in1=_ap(ggla, 0, [[128, 128], [0, 4], [1, 128]]), op=ALU.mult),
                reads=[b_gr[i], b_ggla], writes=[b_t2[k % 2]])
            po = k % 2
            tsl = slice(t * 128, (t + 1) * 128)

            def rec_amm(h):
                hp, base = h // 2, 64 * (h % 2)
                pa = h % 2

                def amm(e):
                    ins = None
                    for d_ in range(2):
                        c = d_ * 2 + hp
                        ins = e.matmul(out=psA[pa][:, d_ * 128:(d_ + 1) * 128], lhsT=gks[s][base:base + 64, c, tsl],
                                       rhs=gqs[s][base:base + 64, c, tsl], start=True, stop=True)
                    return ins
                ph.op("pe", amm, reads=[b_gks[s], b_gqs[s]], writes=[b_psA[pa]])

            def rec_mask_omm(h):
                hp, base = h // 2, 64 * (h % 2)
                pa = h % 2
                for d_ in range(2):
                    ph.op("dve", (lambda d_: lambda e: e.tensor_tensor(
                        out=AT[pa * 2 + d_][:, :], in0=psA[pa][:, d_ * 128:(d_ + 1) * 128],
                        in1=(maskF if d_ == 0 else maskB)[:, :], op=ALU.mult))(d_),
                        reads=[b_psA[pa], b_cbf], writes=[b_AT[pa * 2 + d_]])

                def omm(e):
                    o = psO[po][:, h * 128:(h + 1) * 128]
                    vv = gv[i][:, h * 128:(h + 1) * 128]
                    sc = (h % 2) * 128
                    e.matmul(out=o, lhsT=AT[pa * 2][:, :], rhs=vv, start=True, stop=False)
                    e.matmul(out=o, lhsT=gqs[s][base:base + 64, hp, tsl],
                             rhs=Sf[base:base + 64, T, hp, sc:sc + 128], start=False, stop=False)
                    e.matmul(out=o, lhsT=AT[pa * 2 + 1][:, :], rhs=vv, start=False, stop=False)
                    return e.matmul(out=o, lhsT=gqs[s][base:base + 64, 2 + hp, tsl],
                                    rhs=Sbb[cur][base:base + 64, hp, sc:sc + 128], start=False, stop=True)
                ph.op("pe", omm, reads=[b_AT[pa * 2], b_AT[pa * 2 + 1], b_gv[i], b_gqs[s], b_Sf[T], b_Sbb[cur]],
                      writes=[b_psO[po]])

            if t > 0:
                for hp in range(2):
                    ph.op("pe", kv_mm(i, hp, 1), reads=[b_kd[i], b_gv[i]], writes=[b_psKV[hp]])
                    ph.op("dve", (lambda hp: lambda e: e.scalar_tensor_tensor(
                        out=Sb32[:, hp, :], in0=Sb32[:, hp, :], scalar=dec[:, T, 2 + hp:3 + hp], in1=psKVt[hp][:, 0:256],
                        op0=ALU.mult, op1=ALU.add))(hp),
                        reads=[b_Sb32, b_psKV[hp], b_dec], writes=[b_Sb32])
                ph.op("act", lambda e: e.activation(out=Sbb[nxt][:, :, :], in_=Sb32[:, :, :], func=AF.Copy),
                      reads=[b_Sb32], writes=[b_Sbb[nxt]])
            rec_amm(0)
            for h in range(4):
                if h + 1 < 4:
                    rec_amm(h + 1)
                rec_mask_omm(h)
        def part2(k):
            po = k % 2
            go = k % 2
            ob = k % 2
            ph.op("act", lambda e: e.activation(out=osb[ob][:, :], in_=psO[po][:, :], func=AF.Copy),
                  reads=[b_psO[po]], writes=[b_osb[ob]])
            for h in range(4):
                ph.op("dve", (lambda h: lambda e: e.scalar_tensor_tensor(
                    out=jk[:, h * 128:(h + 1) * 128], in0=osb[ob][:, h * 128:(h + 1) * 128], scalar=1.0,
                    in1=osb[ob][:, h * 128:(h + 1) * 128], op0=ALU.mult, op1=ALU.mult,
                    accum_out=ssg[ob][:, h:h + 1]))(h),
                    reads=[b_osb[ob]], writes=[b_jk[h], b_ssg[ob]])
            ph.op("act", lambda e: e.activation(out=ssg[ob][:, 4:8], in_=ssg[ob][:, 0:4], func=AF.Ln,
                                                scale=1.0 / 128, bias=EPS), reads=[b_ssg[ob]], writes=[b_ssg[ob]])
            ph.op("act", lambda e: e.activation(out=ssg[ob][:, 4:8], in_=ssg[ob][:, 4:8], func=AF.Exp, scale=-0.5),
                  reads=[b_ssg[ob]], writes=[b_ssg[ob]])
            ph.op("pool", lambda e: e.tensor_tensor(
                out=osb[ob][:, :].rearrange("p (h c) -> p h c", h=4),
                in0=osb[ob][:, :].rearrange("p (h c) -> p h c", h=4),
                in1=_ap(ssg[ob], 4, [[8, 128], [1, 4], [0, 128]]), op=ALU.mult),
                reads=[b_osb[ob], b_ssg[ob]], writes=[b_osb[ob]])
            ph.op("pool", lambda e: e.tensor_tensor(
                out=gout[go][:, :], in0=osb[ob][:, :], in1=t2[k % 2][:, :], op=ALU.mult),
                reads=[b_osb[ob], b_t2[k % 2]], writes=[b_gout[go]])

        def part3(k):
            s, t = tiles2[k]
            go = k % 2

            def trg(e):
                ins = None
                for c in range(4):
                    ins = e.transpose(out=psT[:, c * 128:(c + 1) * 128], in_=gout[go][:, c * 128:(c + 1) * 128],
                                      identity=ident_bf[:, :])
                return ins
            ph.op("pe", trg, reads=[b_gout[go], b_cbf], writes=[b_psT])
            ph.op("act", lambda e: e.activation(
                out=mst[go][:, :, :], in_=psT[:, 0:512].rearrange("p (c t) -> p c t", c=4), func=AF.Copy),
                reads=[b_psT], writes=[b_mst[go]])
            ph.op("pool", lambda e: e.dma_start(
                out=MIXT_d.ap()[s, 4:8, :, t * 128:(t + 1) * 128].rearrange("c p t -> p c t"),
                in_=mst[go][:, :, :]), reads=[b_mst[go]], dma=b_mst[go])

        n2 = 0 if P2_CUT in (1, 3, 4) else len(tiles2)
        for k in range(n2 + 2):
            if k < 16:
                sweep1_step(1, k)
            if k < n2:
                part1(k)
            if 0 <= k - 1 < n2:
                part2(k - 1)
            if 0 <= k - 2 < n2:
                part3(k - 2)
        ph.run()
        st.close()

    def phase3(wload):
        st = ExitStack()

        def sbt(name, shape, dt):
            return st.enter_context(nc.sbuf_tensor("p3_" + name, list(shape), dt))

        def pst(name, shape, dt):
            return st.enter_context(nc.psum_tensor("p3_" + name, list(shape), dt))

        ph = Ph(cx, "p3")
        EB = sbt("EB", [128, 8, NCFG, 128], BF16)
        b_EB = cx.bufs(8, "EB")
        tstage = [sbt("tstage%d" % i, [128, NCFG, 128], F32) for i in range(2)]
        b_tstage = cx.bufs(2, "tstage")
        mbias = sbt("mbias", [64, 8], F32)
        b_mbias = cx.buf("mbias")
        qT = sbt("qT", [128, 4, SEQ], BF16)
        b_qT = cx.bufs(4, "qT")
        kT = sbt("kT", [128, 4, SEQ], BF16)
        b_kT = cx.bufs(4, "kT")
        va = sbt("va", [128, 16, 520], BF16)
        b_va = cx.bufs(4, "va")
        ktm = sbt("ktm", [128, 4, 64], BF16)
        b_ktm = cx.buf("ktm")
        vam = sbt("vam", [64, 520], BF16)
        b_vam = cx.buf("vam")
        P = [sbt("P%d" % i, [128, 5, 128], BF16) for i in range(2)]
        b_P = cx.bufs(2, "P")
        Pm = [sbt("Pm%d" % i, [64, 128], BF16) for i in range(2)]
        b_Pm = cx.bufs(2, "Pm")
        rden = sbt("rden", [128, 8], F32)
        b_rden = cx.buf("rden")
        nao = [sbt("nao%d" % i, [128, 512], BF16) for i in range(2)]
        b_nao = cx.bufs(2, "nao")
        mst = [sbt("mst%d" % i, [128, 4, 128], BF16) for i in range(2)]
        b_mst = cx.bufs(2, "mst")

        psS = [pst("psS%d" % i, [128, 8, 128], F32) for i in range(2)]
        b_psS = cx.bufs(2, "psS")
        psO = [pst("psO%d" % i, [128, 4, 128], F32) for i in range(2)]
        b_psO = cx.bufs(2, "psO")
        psT = pst("psT", [128, 1024], BF16)
        b_psT = cx.buf("psT")

        def prep_tables():
          for h in range(8):
            i = h % 2
            ph.op("sp", (lambda i, h: lambda e: e.dma_start(
                out=tstage[i][:, :, :], in_=tbl_d.ap()[h].rearrange("c k q -> k c q")))(i, h),
                writes=[b_tstage[i]], dma=b_tstage[i])
            ph.op("act", (lambda i, h: lambda e: e.activation(
                out=EB[:, h, :, :], in_=tstage[i][:, :, :], func=AF.Exp))(i, h),
                reads=[b_tstage[i]], writes=[b_EB[h]])
        ph.op("pool", lambda e: e.memset(mbias[:, :], NEG), writes=[b_mbias])
        for h in range(8):
            ph.op("sp", (lambda h: lambda e: e.dma_start(
                out=mbias[48:64, h:h + 1], in_=mb_d.ap()[h:h + 1, :].rearrange("o m -> m o")))(h),
                writes=[b_mbias], dma=b_mbias)
        ph.op("sp", lambda e: e.dma_start(out=ktm[:, :, :],
                                          in_=KTM_d.ap()[:, :, 64:128].rearrange("c p t -> p c t")),
              writes=[b_ktm], dma=b_ktm)
        ph.op("sp", lambda e: e.dma_start(out=vam[:, :], in_=VA_d.ap()[META_T, 64:128, :]),
              writes=[b_vam], dma=b_vam)

        units = []
        for s in range(2):
            for t in range(16):
                if t < 2:
                    c0, nch, slot0 = 0, 4, (8 if t == 0 else 7)
                elif t >= 14:
                    c0, nch, slot0 = 12, 4, (6 if t == 14 else 5)
                else:
                    c0, nch, slot0 = t - 2, 5, 0
                for h in range(8):
                    units.append((s, t, h, c0, nch, slot0))

        def load_grp(s, g):
            gs_ = slice(g * 512, (g + 1) * 512)
            ph.op("sp", lambda e: e.dma_start(
                out=qT[:, :, gs_], in_=QT_d.ap()[s * 4:(s + 1) * 4, :, gs_].rearrange("c p t -> p c t")),
                writes=[b_qT[g]], dma=b_qT[g])
            ph.op("sp", lambda e: e.dma_start(
                out=kT[:, :, gs_], in_=KT_d.ap()[s * 4:(s + 1) * 4, :, gs_].rearrange("c p t -> p c t")),
                writes=[b_kT[g]], dma=b_kT[g])
            ph.op("sp", lambda e: e.dma_start(
                out=va[:, g * 4:(g + 1) * 4, :],
                in_=VA_d.ap()[s * 16 + g * 4:s * 16 + (g + 1) * 4].rearrange("c p t -> p c t")),
                writes=[b_va[g]], dma=b_va[g])

        def stage1(ui):
            s, t, h, c0, nch, slot0 = units[ui]
            hp, base = h // 2, 64 * (h % 2)
            pi = ui % 2

            def smm(e):
                q = qT[base:base + 64, hp, t * 128:(t + 1) * 128]
                for ci in range(nch):
                    c = c0 + ci
                    e.matmul(out=psS[pi][:, ci, :], lhsT=kT[base:base + 64, hp, c * 128:(c + 1) * 128],
                             rhs=q, start=True, stop=True)
                return e.matmul(out=psS[pi][0:64, 5, :], lhsT=ktm[base:base + 64, hp, :], rhs=q,
                                start=True, stop=True)
            kgrps = sorted(set((c0 + ci) // 4 for ci in range(nch)))
            ph.op("pe", smm, reads=[b_qT[t // 4], b_ktm] + [b_kT[g_] for g_ in kgrps], writes=[b_psS[pi]])
            ph.op("act", lambda e: e.activation(
                out=P[pi][:, 0:nch, :], in_=psS[pi][:, 0:nch, :], func=AF.Exp),
                reads=[b_psS[pi]], writes=[b_P[pi]])
            ph.op("act", lambda e: e.activation(
                out=Pm[pi][:, :], in_=psS[pi][0:64, 5, :], func=AF.Exp, bias=mbias[:, h:h + 1]),
                reads=[b_psS[pi], b_mbias], writes=[b_Pm[pi]])
            ph.op("dve", lambda e: e.tensor_tensor(
                out=P[pi][:, 0:nch, :], in0=P[pi][:, 0:nch, :], in1=EB[:, h, slot0:slot0 + nch, :],
                op=ALU.mult), reads=[b_P[pi], b_EB[h]], writes=[b_P[pi]])

        def stage2(ui):
            s, t, h, c0, nch, slot0 = units[ui]
            pi = ui % 2

            def pvm(e):
                o = psO[h // 4][:, h % 4, 0:65]
                for ci in range(nch):
                    e.matmul(out=o, lhsT=P[pi][:, ci, :], rhs=va[:, c0 + ci, h * 65:(h + 1) * 65],
                             start=(ci == 0), stop=False)
                return e.matmul(out=o, lhsT=Pm[pi][:, :], rhs=vam[:, h * 65:(h + 1) * 65],
                                start=False, stop=True)
            kgrps = sorted(set((c0 + ci) // 4 for ci in range(nch)))
            ph.op("pe", pvm, reads=[b_P[pi], b_Pm[pi], b_vam] + [b_va[g_] for g_ in kgrps], writes=[b_psO[h // 4]])
            if h == 7:
                k2 = (ui // 8) % 2
                for hb in range(2):
                    ph.op("dve", (lambda hb: lambda e: e.reciprocal(
                        out=rden[:, hb * 4:(hb + 1) * 4], in_=psO[hb][:, :, 64]))(hb),
                        reads=[b_psO[hb]], writes=[b_rden])
                    ph.op("dve", (lambda hb: lambda e: e.tensor_tensor(
                        out=nao[k2][:, hb * 256:(hb + 1) * 256].rearrange("p (h c) -> p h c", h=4),
                        in0=psO[hb][:, :, 0:64],
                        in1=_ap(rden, hb * 4, [[8, 128], [1, 4], [0, 64]]), op=ALU.mult))(hb),
                        reads=[b_psO[hb], b_rden], writes=[b_nao[k2]])

        def stage3(ui):
            s, t, h, c0, nch, slot0 = units[ui]
            if h != 7:
                return
            k2 = (ui // 8) % 2

            def trn(e):
                ins = None
                for c in range(4):
                    ins = e.transpose(out=psT[:, c * 128:(c + 1) * 128], in_=nao[k2][:, c * 128:(c + 1) * 128],
                                      identity=ident_bf[:, :])
                return ins
            ph.op("pe", trn, reads=[b_nao[k2], b_cbf], writes=[b_psT])
            ph.op("act", lambda e: e.activation(
                out=mst[k2][:, :, :], in_=psT[:, 0:512].rearrange("p (c t) -> p c t", c=4), func=AF.Copy),
                reads=[b_psT], writes=[b_mst[k2]])
            ph.op("act", lambda e: e.dma_start(
                out=MIXT_d.ap()[s, 0:4, :, t * 128:(t + 1) * 128].rearrange("c p t -> p c t"),
                in_=mst[k2][:, :, :]), reads=[b_mst[k2]], dma=b_mst[k2])

        nu = len(units)
        load_grp(0, 0)
        prep_tables()
        for g_ in range(1, 4):
            load_grp(0, g_)
        for s in range(2):
            us = [ui for ui in range(nu) if units[ui][0] == s]
            n = len(us)
            for k in range(n + 3):
                if s == 0 and k == 24:
                    wload(ph, b_psT)
                if s == 0:
                    for g_ in range(4):
                        if k == min(n + 2, (4 * g_ + 7) * 8 + 2):
                            load_grp(1, g_)
                if k < n:
                    stage1(us[k])
                if 0 <= k - 1 < n:
                    stage2(us[k - 1])
                if 0 <= k - 3 < n:
                    stage3(us[k - 3])
        ph.run()
        st.close()

    def phase4a(wout, b_wout, wload):
        st = ExitStack()

        def sbt(name, shape, dt):
            return st.enter_context(nc.sbuf_tensor("p4a_" + name, list(shape), dt))

        def pst(name, shape, dt):
            return st.enter_context(nc.psum_tensor("p4a_" + name, list(shape), dt))

        ph = Ph(cx, "p4a")
        gffn = sbt("gffn", [128, D], F32)
        b_gffn = cx.buf("gffn")
        mixt = [sbt("mixt%d" % i, [128, 8, 128], BF16) for i in range(2)]
        b_mixt = cx.bufs(2, "mixt")
        xt = [sbt("xt%d" % i, [128, D], F32) for i in range(2)]
        b_xt = cx.bufs(2, "xt")
        h1 = [sbt("h1%d" % i, [128, D], F32) for i in range(2)]
        b_h1 = cx.bufs(2, "h1")
        junk = sbt("junk", [128, D], BF16)
        b_junk = cx.buf("junk")
        ss = sbt("ss", [128, 4], F32)
        b_ss = cx.bufs(2, "ss")
        hn = [sbt("hn%d" % i, [128, D], BF16) for i in range(2)]
        b_hn = cx.bufs(2, "hn")
        hst = [sbt("hst%d" % i, [128, 8, 128], BF16) for i in range(2)]
        b_hst = cx.bufs(2, "hst")
        psH = [pst("psH%d" % i, [128, D], F32) for i in range(2)]
        b_psH = cx.bufs(2, "psH")
        psT = [pst("psT%d" % i, [128, D], BF16) for i in range(2)]
        b_psT = cx.bufs(2, "psT")

        ph.op("sp", lambda e: e.dma_start(out=gffn[:, :], in_=gffn_d.ap().broadcast_to([128, D])),
              writes=[b_gffn], dma=b_gffn)
        def partA(T):
            s, t = T // 16, T % 16
            i = T % 2
            ph.op("sp", lambda e: e.dma_start(
                out=mixt[i][:, :, :],
                in_=MIXT_d.ap()[s, :, :, t * 128:(t + 1) * 128].rearrange("c p t -> p c t")),
                writes=[b_mixt[i]], dma=b_mixt[i])
            ph.op("sp", lambda e: e.dma_start(
                out=xt[i][:, :], in_=x_d.ap()[s, t * 128:(t + 1) * 128, :]),
                writes=[b_xt[i]], dma=b_xt[i])

            def mm(e):
                ins = None
                for n in range(2):
                    for kc in range(8):
                        ins = e.matmul(out=psH[i][:, n * 512:(n + 1) * 512], lhsT=mixt[i][:, kc, :],
                                       rhs=wout[:, kc, n * 512:(n + 1) * 512], start=(kc == 0), stop=(kc == 7))
                return ins
            ph.op("pe", mm, reads=[b_mixt[i]] + b_wout, writes=[b_psH[i]])
            ph.op("dve", lambda e: e.tensor_tensor(
                out=h1[i][:, :], in0=psH[i][:, :], in1=xt[i][:, :], op=ALU.add),
                reads=[b_psH[i], b_xt[i]], writes=[b_h1[i]])
            ph.op("act", lambda e: e.dma_start(out=H1_d.ap()[T], in_=h1[i][:, :]),
                  reads=[b_h1[i]], dma=b_h1[i])
            rms_rstd(ph, "dve", lambda: h1[i][:, :], b_h1[i], ss, b_ss[i], i * 2,
                     lambda: junk[:, :], b_junk, D)
            ph.op("dve", lambda e: e.scalar_tensor_tensor(
                out=hn[i][:, :], in0=h1[i][:, :], scalar=ss[:, i * 2 + 1:i * 2 + 2], in1=gffn[:, :],
                op0=ALU.mult, op1=ALU.mult),
                reads=[b_h1[i], b_ss[i], b_gffn], writes=[b_hn[i]])

        def partB(T):
            g, j = T // 4, T % 4
            i = T % 2

            def tr(e):
                ins = None
                for kc in range(8):
                    ins = e.transpose(out=psT[i][:, kc * 128:(kc + 1) * 128],
                                      in_=hn[i][:, kc * 128:(kc + 1) * 128], identity=ident_bf[:, :])
                return ins
            ph.op("pe", tr, reads=[b_hn[i], b_cbf], writes=[b_psT[i]])
            ph.op("act", lambda e: e.activation(
                out=hst[i][:, :, :], in_=psT[i][:, :].rearrange("p (k t) -> p k t", k=8), func=AF.Copy),
                reads=[b_psT[i]], writes=[b_hst[i]])
            ph.op("act", lambda e: e.dma_start(
                out=HNT_d.ap()[g].rearrange("p (k t) -> p k t", k=8)[:, :, j * 128:(j + 1) * 128],
                in_=hst[i][:, :, :]), reads=[b_hst[i]], dma=b_hst[i])

        for T in range(33):
            if T == 3:
                wload(ph, b_psH[0])
            if T < 32:
                partA(T)
            if T >= 1:
                partB(T - 1)
        ph.run()
        st.close()

    def phase4b(wg, wu, wd, b_wg, b_wu, b_wd):
        st = ExitStack()

        def sbt(name, shape, dt):
            return st.enter_context(nc.sbuf_tensor("p4b_" + name, list(shape), dt))

        def pst(name, shape, dt):
            return st.enter_context(nc.psum_tensor("p4b_" + name, list(shape), dt))

        ph = Ph(cx, "p4b")
        gfin = sbt("gfin", [128, D], F32)
        b_gfin = cx.buf("gfin")
        hnT = [sbt("hnT%d" % i, [128, 8, 512], BF16) for i in range(2)]
        b_hnT = cx.bufs(2, "hnT")
        actT = sbt("actT", [128, NFF, 512], BF16)
        b_actT = cx.bufs(NFF, "actT")
        sg = [sbt("sg%d" % i, [128, 512], F32) for i in range(2)]
        b_sg = cx.bufs(2, "sg")
        h1 = [sbt("h1%d" % i, [128, D], F32) for i in range(2)]
        b_h1 = cx.bufs(2, "h1")
        junk_ap = sg[0][:, :].bitcast(BF16)
        b_junk = b_sg[0]
        ss = sbt("ss", [128, 4], F32)
        b_ss = cx.bufs(2, "ss")
        psG = [pst("psG%d" % i, [128, 512], F32) for i in range(2)]
        b_psG = cx.bufs(2, "psG")
        psU = [pst("psU%d" % i, [128, 512], F32) for i in range(2)]
        b_psU = cx.bufs(2, "psU")
        psD = [pst("psD%d" % i, [128, D], F32) for i in range(2)]
        b_psD = cx.bufs(2, "psD")

        ph.op("sp", lambda e: e.dma_start(out=gfin[:, :], in_=gfin_d.ap().broadcast_to([128, D])),
              writes=[b_gfin], dma=b_gfin)
        fc = 0
        tcn = 0
        for g in range(8):
            gi = g % 2
            ph.op("sp", (lambda gi, g: lambda e: e.dma_start(
                out=hnT[gi][:, :, :], in_=HNT_d.ap()[g].rearrange("p (k t) -> p k t", k=8)))(gi, g),
                writes=[b_hnT[gi]], dma=b_hnT[gi])
            for f in range(NFF):
                pi = fc % 2
                fc += 1

                def gmm(e, pi=pi, f=f, gi=gi):
                    ins = None
                    for kc in range(8):
                        ins = e.matmul(out=psG[pi][:, :], lhsT=wg[:, kc, f * 128:(f + 1) * 128],
                                       rhs=hnT[gi][:, kc, :], start=(kc == 0), stop=(kc == 7))
                    return ins

                def umm(e, pi=pi, f=f, gi=gi):
                    ins = None
                    for kc in range(8):
                        ins = e.matmul(out=psU[pi][:, :], lhsT=wu[:, kc, f * 128:(f + 1) * 128],
                                       rhs=hnT[gi][:, kc, :], start=(kc == 0), stop=(kc == 7))
                    return ins
                ph.op("pe", gmm, reads=[b_hnT[gi]] + b_wg, writes=[b_psG[pi]])
                ph.op("pe", umm, reads=[b_hnT[gi]] + b_wu, writes=[b_psU[pi]])
                ph.op("act", (lambda pi: lambda e: e.activation(out=sg[pi][:, :], in_=psG[pi][:, :], func=AF.Silu))(pi),
                      reads=[b_psG[pi]], writes=[b_sg[pi]])
                ph.op("dve", (lambda pi, f: lambda e: e.tensor_tensor(
                    out=actT[:, f, :], in0=psU[pi][:, :], in1=sg[pi][:, :], op=ALU.mult))(pi, f),
                    reads=[b_psU[pi], b_sg[pi]], writes=[b_actT[f]])
            for j in range(4):
                T = g * 4 + j
                s, t = T // 16, T % 16
                i = tcn % 2
                tcn += 1
                ph.op("sp", (lambda i, T: lambda e: e.dma_start(out=h1[i][:, :], in_=H1_d.ap()[T]))(i, T),
                      writes=[b_h1[i]], dma=b_h1[i])

                def dmm(e, i=i, j=j):
                    ins = None
                    for n in range(2):
                        for f in range(NFF):
                            ins = e.matmul(out=psD[i][:, n * 512:(n + 1) * 512], lhsT=actT[:, f, j * 128:(j + 1) * 128],
                                           rhs=wd[:, f, n * 512:(n + 1) * 512], start=(f == 0), stop=(f == NFF - 1))
                    return ins
                ph.op("pe", dmm, reads=b_actT + b_wd, writes=[b_psD[i]])
                ph.op("dve", (lambda i: lambda e: e.tensor_tensor(
                    out=h1[i][:, :], in0=psD[i][:, :], in1=h1[i][:, :], op=ALU.add))(i),
                    reads=[b_psD[i], b_h1[i]], writes=[b_h1[i]])
                rms_rstd(ph, "dve", (lambda i: lambda: h1[i][:, :])(i), b_h1[i], ss, b_ss[i], i * 2,
                         lambda: junk_ap, b_junk, D)
                ph.op("dve", (lambda i: lambda e: e.scalar_tensor_tensor(
                    out=h1[i][:, :], in0=h1[i][:, :], scalar=ss[:, i * 2 + 1:i * 2 + 2], in1=gfin[:, :],
                    op0=ALU.mult, op1=ALU.mult))(i),
                    reads=[b_h1[i], b_ss[i], b_gfin], writes=[b_h1[i]])
                ph.op("pool", (lambda i, s, t: lambda e: e.dma_start(
                    out=out_d.ap()[s, t * 128:(t + 1) * 128, :], in_=h1[i][:, :]))(i, s, t),
                    reads=[b_h1[i]], dma=b_h1[i])
        ph.run()
        st.close()

    phase1()
    if LAST_PHASE >= 2:
        phase2()
    if LAST_PHASE >= 3:
        wst = ExitStack()
        wst_o = ExitStack()
        wg = wst.enter_context(nc.sbuf_tensor("w_wg", [128, 8, DFF], BF16))
        wu = wst.enter_context(nc.sbuf_tensor("w_wu", [128, 8, DFF], BF16))
        wout = wst_o.enter_context(nc.sbuf_tensor("w_wout", [128, 8, D], BF16))
        b_wout = cx.bufs(8, "wout")
        b_wg = cx.bufs(8, "wg")
        b_wu = cx.bufs(8, "wu")

        def wload3(ph, after):
            for kc in range(8):
                ph.op("pool", (lambda kc: lambda e: e.dma_start(
                    out=wout[:, kc, :], in_=wout_d.ap()[kc * 128:(kc + 1) * 128, :], max_dma_last_dim=4096))(kc),
                    reads=([after] if kc == 0 else []), writes=[b_wout[kc]], dma=b_wout[kc])
            for kc in range(8):
                ph.op("pool", (lambda kc: lambda e: e.dma_start(
                    out=wg[:, kc, :], in_=wg_d.ap()[kc * 128:(kc + 1) * 128, :], max_dma_last_dim=4096))(kc),
                    writes=[b_wg[kc]], dma=b_wg[kc])
                ph.op("pool", (lambda kc: lambda e: e.dma_start(
                    out=wu[:, kc, :], in_=wu_d.ap()[kc * 128:(kc + 1) * 128, :], max_dma_last_dim=4096))(kc),
                    writes=[b_wu[kc]], dma=b_wu[kc])
        phase3(wload3)
        if LAST_PHASE >= 4:
            wd = wst_o.enter_context(nc.sbuf_tensor("w_wd", [128, NFF, D], BF16))
            b_wd = cx.bufs(NFF, "wd")

            def wload4(ph, after):
                for f in range(NFF):
                    ph.op("pool", (lambda f: lambda e: e.dma_start(
                        out=wd[:, f, :], in_=wd_d.ap()[f * 128:(f + 1) * 128, :], max_dma_last_dim=4096))(f),
                        reads=([after] if f == 0 else []), writes=[b_wd[f]], dma=b_wd[f])
            phase4a(wout, b_wout, wload4)
            if LAST_PHASE >= 5:
                phase4b(wg, wu, wd, b_wg, b_wu, b_wd)
        wst_o.close()
        wst.close()
    return nc, es


def _host_consts():
    c = np.zeros((128, NCONST), np.float32)
    j = np.arange(128)[:, None]
    i = np.arange(128)[None, :]
    c[:, K_ID:K_ID + 128] = (j == i)
    c[:, K_TRIF:K_TRIF + 128] = (j <= i) * (-1.0 / 16.0)
    c[:, K_TRIB:K_TRIB + 128] = (j >= i) * (-1.0 / 16.0)
    c[:, K_SUF:K_SUF + 128] = (j > i) * (-1.0 / 16.0)
    c[:, K_SUB:K_SUB + 128] = (j < i) * (-1.0 / 16.0)
    c[:, K_MF:K_MF + 128] = (j <= i)
    c[:, K_MB:K_MB + 128] = (j >= i)
    return c


def _host_na_table(rpb):
    rpb = np.asarray(rpb, np.float32).reshape(8, 15, 31)
    ext = np.concatenate([rpb.reshape(8, -1), np.full((8, 1), NEG, np.float32)], axis=1)
    SENT = 15 * 31
    cols = np.arange(64)
    col_start = np.clip(cols - 8, 0, 48)
    kc = cols[:, None]
    qc = cols[None, :]
    colvalid = (kc >= col_start[None, :]) & (kc < col_start[None, :] + 16)
    dc = np.clip(kc - qc, -15, 15) + 15
    cfgs = [(-2, True), (-1, False), (0, False), (1, False), (2, True),
            (-3, False), (-2, False), (-1, False), (0, False), (1, False), (2, False), (3, False)]
    idx = np.full((NCFG, 128, 128), SENT, np.int64)
    for ci, (d, interior) in enumerate(cfgs):
        for rl in range(2):
            for il in range(2):
                rel = 2 * d + rl - il
                dr = rel + 7
                if dr < 0 or dr > 14:
                    continue
                if interior and not (-4 <= rel <= 3):
                    continue
                blk = np.where(colvalid, dr * 31 + dc, SENT)
                idx[ci, rl * 64:(rl + 1) * 64, il * 64:(il + 1) * 64] = blk
    return np.ascontiguousarray(ext[:, idx])


def kernel(x, meta_tokens, norm_mix_gain, w_in, rpb, meta_bias, w_gate_up_fwd, b_gate_fwd,
           w_gate_up_bwd, b_gate_bwd, gla_norm_gain, w_out, norm_ffn_gain, w_ffn_gate,
           w_ffn_up, w_ffn_down, norm_final_gain):
    f = lambda a: np.ascontiguousarray(np.asarray(a, np.float32))
    nc, es = build_program()
    shared = {
        "meta_tokens": f(meta_tokens), "norm_mix_gain": f(norm_mix_gain).reshape(1, D),
        "w_in": f(w_in).reshape(D, IN_W), "na_tbl": _host_na_table(rpb),
        "meta_bias": f(meta_bias).reshape(8, 16),
        "w_gate_up_fwd": f(w_gate_up_fwd).reshape(16, 256), "b_gate_fwd": f(b_gate_fwd).reshape(1, 256),
        "w_gate_up_bwd": f(w_gate_up_bwd).reshape(16, 256), "b_gate_bwd": f(b_gate_bwd).reshape(1, 256),
        "gla_norm_gain": f(gla_norm_gain).reshape(1, 128), "w_out": f(w_out).reshape(D, D),
        "norm_ffn_gain": f(norm_ffn_gain).reshape(1, D), "w_ffn_gate": f(w_ffn_gate).reshape(D, DFF),
        "w_ffn_up": f(w_ffn_up).reshape(D, DFF), "w_ffn_down": f(w_ffn_down).reshape(DFF, D),
        "norm_final_gain": f(norm_final_gain).reshape(1, D), "consts": _host_consts(),
    }
    xs = f(x)
    in_maps = []
    for c in range(8):
        m = dict(shared)
        m["x"] = xs[2 * c:2 * c + 2]
        in_maps.append(m)
    res = run_bass_kernel_spmd(nc, in_maps, core_ids=list(range(8)))
    es.close()
    kernel.last_results = res.results
    return np.concatenate([r["out"] for r in res.results], axis=0)
```

```python
import numpy as np
from contextlib import ExitStack
import concourse.bass as bass
import concourse.mybir as mybir
from concourse.bass_utils import run_bass_kernel_spmd

F32 = mybir.dt.float32
BF16 = mybir.dt.bfloat16
AF = mybir.ActivationFunctionType
ALU = mybir.AluOpType
AX = mybir.AxisListType

DEBUG = False
LAST_PHASE = 99
P2_CUT = 0

D = 1024
SEQ = 2048
NT = 16
NTILES = 33
META_T = 32
IN_W = 3104
DFF = 2816
NFF = 22
EPS = 1e-6
NEG = -30000.0

C_NAQ, C_NAK, C_NAV = 0, 512, 1024
C_GQ, C_GK, C_GV, C_GR, C_GF, C_GB = 1536, 1792, 2048, 2560, 3072, 3088

K_ID, K_TRIF, K_TRIB, K_SUF, K_SUB, K_MF, K_MB = [128 * i for i in range(7)]
NCONST = 128 * 7

NCFG = 12


class Buf:
    __slots__ = ("name", "last_w", "readers", "sem", "cnt", "kind", "sub")

    def __init__(self, name):
        self.name = name
        self.last_w = None
        self.readers = []
        self.sem = None
        self.cnt = 0
        self.kind = None
        self.sub = None


class Ctx:
    def __init__(self, nc, es):
        self.nc = nc
        self.es = es
        self.eng_sem = {}
        self.eng_cnt = {}
        for e in ("pe", "dve", "act", "pool", "sp"):
            self.eng_sem[e] = es.enter_context(nc.semaphore("es_" + e))
            self.eng_cnt[e] = 0
        self.dma_bufs = []
        self.sem_pool = {"sw": [], "hw": []}
        self.nsem = 0
        self.nbuf = 0

    def buf(self, name=None):
        self.nbuf += 1
        return Buf(name or "b%d" % self.nbuf)

    def bufs(self, n, name=None):
        return [self.buf((name or "b") + str(i)) for i in range(n)]


class Ph:
    def __init__(self, cx, name):
        self.cx = cx
        self.name = name
        self.ops = {e: [] for e in ("pe", "dve", "act", "pool", "sp")}
        self.waited = {e: {} for e in self.ops}
        self.touched = []

    def _add_dep(self, deps, d, eng, raw, waw=False):
        if d is None:
            return
        sem, val, deng, is_dma = d
        if deng == eng and not is_dma:
            if eng == "pe":
                return
            if not raw and eng != "pool" and not waw:
                return
        deps.append((sem, val))

    def op(self, eng, fn, reads=(), writes=(), dma=None):
        cx = self.cx
        deps = []
        for b in reads:
            self._add_dep(deps, b.last_w, eng, True)
        for b in writes:
            self._add_dep(deps, b.last_w, eng, False, True)
            for r in b.readers:
                self._add_dep(deps, r, eng, False)
        if dma is not None:
            kind = "sw" if eng == "pool" else "hw"
            if dma.sub is None:
                dma.sub = {}
            if kind not in dma.sub:
                dma.sub[kind] = Buf(dma.name + "_" + kind)
            dma = dma.sub[kind]
            if dma.sem is None:
                if cx.sem_pool[kind]:
                    dma.sem, dma.cnt = cx.sem_pool[kind].pop()
                else:
                    cx.nsem += 1
                    dma.sem = cx.es.enter_context(cx.nc.semaphore("ds%d" % cx.nsem))
                    dma.cnt = 0
                dma.kind = kind
                cx.dma_bufs.append(dma)
            assert dma.kind == kind, dma.name
            dma.cnt += 16
            sig = (dma.sem, dma.cnt, eng, True)
            inc = (dma.sem, 16)
        else:
            cx.eng_cnt[eng] += 1
            sig = (cx.eng_sem[eng], cx.eng_cnt[eng], eng, False)
            inc = (cx.eng_sem[eng], 1)
        w = self.waited[eng]
        waits = []
        best = {}
        for sem, val in deps:
            k = id(sem)
            if w.get(k, 0) >= val:
                continue
            if k not in best or best[k][1] < val:
                best[k] = (sem, val)
        for k, (sem, val) in best.items():
            w[k] = val
            waits.append((sem, val))
        self.ops[eng].append((waits, fn, inc))
        for b in reads:
            b.readers.append(sig)
            self.touched.append(b)
        for b in writes:
            b.last_w = sig
            b.readers = []
            self.touched.append(b)
        return sig

    def run(self):
        cx = self.cx
        nc = cx.nc
        fin = [(b.sem, b.cnt) for b in cx.dma_bufs]

        def emit(engname):
            lst = self.ops[engname]

            def body(e):
                for waits, fn, inc in lst:
                    for sem, val in waits:
                        e.wait_ge(sem, val)
                    ins = fn(e)
                    ins.then_inc(inc[0], inc[1])
                if engname == "sp":
                    for sem, val in fin:
                        e.wait_ge(sem, val)
            return body

        with nc.Block() as blk:
            blk.sync(emit("sp"))
            blk.tensor(emit("pe"))
            blk.vector(emit("dve"))
            blk.scalar(emit("act"))
            blk.gpsimd(emit("pool"))
        for b in self.touched:
            b.last_w = None
            b.readers = []
        for b in cx.dma_bufs:
            cx.sem_pool[b.kind].append((b.sem, b.cnt))
            b.sem = None
        cx.dma_bufs = []


def _ap(t, off, pat):
    return bass.AP(tensor=t, offset=off, ap=[list(p) for p in pat])


def build_program():
    nc = bass.Bass("TRN2", target_bir_lowering=False)
    es = ExitStack()
    cx = Ctx(nc, es)
    skind = "ExternalOutput" if DEBUG else "Internal"

    def din(name, shape, dt=F32):
        return nc.dram_tensor(name, list(shape), dt, kind="ExternalInput")

    def dscr(name, shape, dt=BF16):
        return nc.dram_tensor(name, list(shape), dt, kind=skind)

    x_d = din("x", [2, SEQ, D])
    meta_d = din("meta_tokens", [16, D])
    gmix_d = din("norm_mix_gain", [1, D])
    win_d = din("w_in", [D, IN_W])
    tbl_d = din("na_tbl", [8, NCFG, 128, 128])
    mb_d = din("meta_bias", [8, 16])
    wgf_d = din("w_gate_up_fwd", [16, 256])
    bgf_d = din("b_gate_fwd", [1, 256])
    wgb_d = din("w_gate_up_bwd", [16, 256])
    bgb_d = din("b_gate_bwd", [1, 256])
    ggla_d = din("gla_norm_gain", [1, 128])
    wout_d = din("w_out", [D, D])
    gffn_d = din("norm_ffn_gain", [1, D])
    wg_d = din("w_ffn_gate", [D, DFF])
    wu_d = din("w_ffn_up", [D, DFF])
    wd_d = din("w_ffn_down", [DFF, D])
    gfin_d = din("norm_final_gain", [1, D])
    consts_d = din("consts", [128, NCONST])
    out_d = nc.dram_tensor("out", [2, SEQ, D], F32, kind="ExternalOutput")

    QT_d = dscr("s_qt", [8, 128, SEQ])
    KT_d = dscr("s_kt", [8, 128, SEQ])
    KTM_d = dscr("s_ktm", [4, 128, 128])
    VA_d = dscr("s_va", [NTILES, 128, 520])
    GQ_d = dscr("s_gq", [8, 128, SEQ])
    GK_d = dscr("s_gk", [8, 128, SEQ])
    GKD_d = dscr("s_gkd", [NTILES, 128, 512])
    GV_d = dscr("s_gv", [NTILES, 128, 512])
    GR_d = dscr("s_gr", [NTILES, 128, 512])
    MIXT_d = dscr("s_mixt", [2, 8, 128, SEQ])
    H1_d = dscr("s_h1", [32, 128, D], F32)
    HNT_d = dscr("s_hnt", [8, 128, 8 * 512])

    def sb(name, shape, dt):
        return es.enter_context(nc.sbuf_tensor("sb_" + name, list(shape), dt))

    consts = sb("consts", [128, NCONST], F32)
    ident_bf = sb("ident_bf", [128, 128], BF16)
    maskF = sb("maskF", [128, 128], BF16)
    maskB = sb("maskB", [128, 128], BF16)
    dec = sb("dec", [128, NTILES, 4], F32)
    b_consts = cx.buf("consts")
    b_cbf = cx.buf("cbf")
    b_dec = cx.buf("dec")

    def phase1():
        st = ExitStack()

        def sbt(name, shape, dt):
            return st.enter_context(nc.sbuf_tensor("p1_" + name, list(shape), dt))

        def pst(name, shape, dt):
            return st.enter_context(nc.psum_tensor("p1_" + name, list(shape), dt))

        ph = Ph(cx, "p1")
        win = sbt("win", [128, 8, IN_W], BF16)
        WBLK = 776
        b_win = cx.bufs(4, "win")

        def wb(c0, n):
            return [b_win[b] for b in range(4) if b * WBLK < c0 + n and (b + 1) * WBLK > c0]
        w2 = sbt("w2", [64, 512], BF16)
        w2f = sbt("w2f", [64, 512], F32)
        b_w2f = cx.buf("w2f")
        b_w2 = cx.buf("w2")
        gmix = sbt("gmix", [128, D], F32)
        b_gmix = cx.buf("gmix")
        xt = [sbt("xt%d" % i, [128, D], F32) for i in range(3)]
        b_xt = cx.bufs(3, "xt")
        junk = sbt("junk", [128, D], BF16)
        b_junk = cx.buf("junk")
        ss = sbt("ss", [128, 8], F32)
        b_ss = cx.bufs(4, "ss")
        xn = [sbt("xn%d" % i, [128, D], BF16) for i in range(4)]
        b_xn = cx.bufs(4, "xn")
        xnT = [sbt("xnT%d" % i, [128, 8, 512], BF16) for i in range(2)]
        b_xnT = [cx.bufs(4, "xnT%d_" % i) for i in range(2)]
        glT = [sbt("glT%d" % i, [64, 512], BF16) for i in range(2)]
        b_glT = cx.bufs(2, "glT")
        stQK = [sbt("stQK%d" % i, [128, 8, 512], BF16) for i in range(2)]
        b_stQK = cx.bufs(2, "stQK")
        gqk = [sbt("gqk%d" % i, [128, 4, 512], F32) for i in range(2)]
        b_gqk = cx.bufs(2, "gqk")
        stG = [sbt("stG%d" % i, [128, 8, 512], BF16) for i in range(2)]
        b_stG = cx.bufs(2, "stG")
        stVA = [sbt("stVA%d" % i, [128, 520], BF16) for i in range(2)]
        b_stVA = cx.bufs(2, "stVA")
        stGV = [sbt("stGV%d" % i, [128, 512], BF16) for i in range(2)]
        b_stGV = cx.bufs(2, "stGV")
        rraw = [sbt("rraw%d" % i, [128, 512], BF16) for i in range(2)]
        b_rraw = cx.bufs(2, "rraw")
        stGR = [sbt("stGR%d" % i, [128, 512], BF16) for i in range(2)]
        b_stGR = cx.bufs(2, "stGR")
        ktok = [sbt("ktok%d" % i, [128, 256], F32) for i in range(2)]
        b_ktok = cx.bufs(2, "ktok")
        stKD = [sbt("stKD%d" % i, [128, 512], BF16) for i in range(2)]
        b_stKD = cx.bufs(2, "stKD")
        ge = [sbt("ge%d" % i, [128, 512], F32) for i in range(2)]
        b_ge = cx.bufs(2, "ge")
        gp = [sbt("gp%d" % i, [128, 512], F32) for i in range(2)]
        b_gp = cx.bufs(2, "gp")
        Eq = [sbt("Eq%d" % i, [128, 4, 128], F32) for i in range(2)]
        b_Eq = cx.bufs(2, "Eq")
        Ek = [sbt("Ek%d" % i, [128, 4, 128], F32) for i in range(2)]
        b_Ek = cx.bufs(2, "Ek")
        Ed = [sbt("Ed%d" % i, [128, 512], F32) for i in range(2)]
        b_Ed = cx.bufs(2, "Ed")

        psT = [pst("psT%d" % i, [128, D], BF16) for i in range(2)]
        b_psT = cx.bufs(2, "psT")
        psF = [pst("psF%d" % i, [128, 512], F32) for i in range(4)]
        b_psF = cx.bufs(4, "psF")
        psK = psF
        b_psK = b_psF
        psB = pst("psB", [128, 4, 128], F32)
        b_psB = cx.buf("psB")
        psS = pst("psS", [128, 512], F32)
        b_psS = cx.buf("psS")

        ph.op("sp", lambda e: e.dma_start(out=consts[:, :], in_=consts_d.ap()),
              writes=[b_consts], dma=b_consts)
        ph.op("sp", lambda e: e.dma_start(out=gmix[:, :], in_=gmix_d.ap().broadcast_to([128, D])),
              writes=[b_gmix], dma=b_gmix)
        ph.op("dve", lambda e: e.tensor_copy(out=ident_bf[:, :], in_=consts[:, K_ID:K_ID + 128]),
              reads=[b_consts], writes=[b_cbf])
        ph.op("dve", lambda e: e.tensor_copy(out=maskF[:, :], in_=consts[:, K_MF:K_MF + 128]),
              reads=[b_consts], writes=[b_cbf])
        ph.op("dve", lambda e: e.tensor_copy(out=maskB[:, :], in_=consts[:, K_MB:K_MB + 128]),
              reads=[b_consts], writes=[b_cbf])
        ph.op("pool", lambda e: e.memset(w2f[:, :], 0.0), writes=[b_w2f])
        ph.op("sp", lambda e: e.dma_start(out=w2f[0:16, 0:256], in_=wgf_d.ap()), writes=[b_w2f], dma=b_w2f)
        ph.op("sp", lambda e: e.dma_start(out=w2f[16:32, 256:512], in_=wgb_d.ap()), writes=[b_w2f], dma=b_w2f)
        ph.op("sp", lambda e: e.dma_start(out=w2f[32:33, 0:256], in_=bgf_d.ap()), writes=[b_w2f], dma=b_w2f)
        ph.op("sp", lambda e: e.dma_start(out=w2f[32:33, 256:512], in_=bgb_d.ap()), writes=[b_w2f], dma=b_w2f)
        ph.op("dve", lambda e: e.tensor_copy(out=w2[:, :], in_=w2f[:, :]), reads=[b_w2f], writes=[b_w2])
        for i in range(2):
            ph.op("pool", (lambda i: lambda e: e.memset(glT[i][32:64, :], 1.0))(i), writes=[b_glT[i]])
            ph.op("pool", (lambda i: lambda e: e.memset(stVA[i][:, :], 1.0))(i), writes=[b_stVA[i]])
        for b in range(4):
            ph.op("pool", (lambda b: lambda e: e.dma_start(
                out=win[:, :, b * WBLK:(b + 1) * WBLK],
                in_=win_d.ap()[:, b * WBLK:(b + 1) * WBLK].rearrange("(k p) c -> p k c", p=128),
                max_dma_last_dim=4096))(b),
                writes=[b_win[b]], dma=b_win[b])

        state = {"xi": 0}

        def stageA1(g):
            tiles = [META_T] if g == 8 else [g * 4 + j for j in range(4)]
            for j, T in enumerate(tiles):
                xi = state["xi"] % 3
                state["xi"] += 1
                ti = j
                if T == META_T:
                    ph.op("pool", (lambda xi: lambda e: e.memset(xt[xi][:, :], 0.0))(xi), writes=[b_xt[xi]])
                    ph.op("sp", (lambda xi: lambda e: e.dma_start(out=xt[xi][112:128, :], in_=meta_d.ap()))(xi),
                          writes=[b_xt[xi]], dma=b_xt[xi])
                else:
                    s_, t_ = T // 16, T % 16
                    ph.op("sp", (lambda xi, s_, t_: lambda e: e.dma_start(
                        out=xt[xi][:, :], in_=x_d.ap()[s_, t_ * 128:(t_ + 1) * 128, :]))(xi, s_, t_),
                        writes=[b_xt[xi]], dma=b_xt[xi])
                ph.op("dve", (lambda xi, ti: lambda e: e.scalar_tensor_tensor(
                    out=junk[:, :], in0=xt[xi][:, :], scalar=1.0, in1=xt[xi][:, :],
                    op0=ALU.mult, op1=ALU.mult, accum_out=ss[:, ti * 2:ti * 2 + 1]))(xi, ti),
                    reads=[b_xt[xi]], writes=[b_junk, b_ss[ti]])
                ph.op("act", (lambda ti: lambda e: e.activation(
                    out=ss[:, ti * 2 + 1:ti * 2 + 2], in_=ss[:, ti * 2:ti * 2 + 1], func=AF.Ln,
                    scale=1.0 / D, bias=EPS))(ti), reads=[b_ss[ti]], writes=[b_ss[ti]])
                ph.op("act", (lambda ti: lambda e: e.activation(
                    out=ss[:, ti * 2 + 1:ti * 2 + 2], in_=ss[:, ti * 2 + 1:ti * 2 + 2], func=AF.Exp,
                    scale=-0.5))(ti), reads=[b_ss[ti]], writes=[b_ss[ti]])
                ph.op("dve", (lambda xi, ti: lambda e: e.scalar_tensor_tensor(
                    out=xn[ti][:, :], in0=xt[xi][:, :], scalar=ss[:, ti * 2 + 1:ti * 2 + 2], in1=gmix[:, :],
                    op0=ALU.mult, op1=ALU.mult))(xi, ti),
                    reads=[b_xt[xi], b_ss[ti], b_gmix], writes=[b_xn[ti]])

        def stageA2(g, gs):
            tiles = [META_T] if g == 8 else [g * 4 + j for j in range(4)]
            for j, T in enumerate(tiles):
                ti = j
                pt = j % 2

                def tr(e, ti=ti, pt=pt):
                    ins = None
                    for kc in range(8):
                        ins = e.transpose(out=psT[pt][:, kc * 128:(kc + 1) * 128],
                                          in_=xn[ti][:, kc * 128:(kc + 1) * 128], identity=ident_bf[:, :])
                    return ins
                ph.op("pe", tr, reads=[b_xn[ti], b_cbf], writes=[b_psT[pt]])
                ph.op("act", (lambda pt, gs, j: lambda e: e.activation(
                    out=xnT[gs][:, :, j * 128:(j + 1) * 128],
                    in_=psT[pt][:, :].rearrange("p (k t) -> p k t", k=8), func=AF.Copy))(pt, gs, j),
                    reads=[b_psT[pt]], writes=[b_xnT[gs][j]])

        fcount = {"f": 0, "k": 0}

        def stageB(g, gs):
            tiles = [META_T] if g == 8 else [g * 4 + j for j in range(4)]
            nt = len(tiles)
            N = nt * 128
            xb = b_xnT[gs][:nt]
            s = g // 4
            tok0 = (g % 4) * 512
            fm = [(C_NAQ + 128 * i, 128, "q", i) for i in range(4)] + \
                 [(C_NAK + 128 * i, 128, "k", i) for i in range(4)] + \
                 [(C_GQ + 128 * i, 128, "gq", i) for i in range(2)] + \
                 [(C_GK + 128 * i, 128, "gk", i) for i in range(2)] + \
                 [(C_GF, 32, "gl", 0)]
            for (c0, M, kind, i) in fm:
                pi = fcount["f"] % 4
                fcount["f"] += 1

                def mm(e, c0=c0, M=M, pi=pi):
                    ins = None
                    for kc in range(8):
                        ins = e.matmul(out=psF[pi][0:M, 0:N], lhsT=win[:, kc, c0:c0 + M],
                                       rhs=xnT[gs][:, kc, 0:N], start=(kc == 0), stop=(kc == 7))
                    return ins
                ph.op("pe", mm, reads=wb(c0, M) + xb, writes=[b_psF[pi]])
                if kind == "q":
                    ph.op("act", (lambda pi, i: lambda e: e.activation(
                        out=stQK[gs][:, i, 0:N], in_=psF[pi][:, 0:N], func=AF.Copy, scale=0.125))(pi, i),
                        reads=[b_psF[pi]], writes=[b_stQK[gs]])
                elif kind == "k":
                    ph.op("dve", (lambda pi, i: lambda e: e.tensor_copy(
                        out=stQK[gs][:, 4 + i, 0:N], in_=psF[pi][:, 0:N]))(pi, i),
                        reads=[b_psF[pi]], writes=[b_stQK[gs]])
                elif kind == "gq":
                    ph.op("act", (lambda pi, i: lambda e: e.activation(
                        out=gqk[gs][:, i, 0:N], in_=psF[pi][:, 0:N], func=AF.Copy, scale=0.125))(pi, i),
                        reads=[b_psF[pi]], writes=[b_gqk[gs]])
                elif kind == "gk":
                    ph.op("dve", (lambda pi, i: lambda e: e.tensor_copy(
                        out=gqk[gs][:, 2 + i, 0:N], in_=psF[pi][:, 0:N]))(pi, i),
                        reads=[b_psF[pi]], writes=[b_gqk[gs]])
                else:
                    ph.op("dve", (lambda pi: lambda e: e.tensor_copy(
                        out=glT[gs][0:32, 0:N], in_=psF[pi][0:32, 0:N]))(pi),
                        reads=[b_psF[pi]], writes=[b_glT[gs]])
            if g == 8:
                ph.op("pool", lambda e: e.dma_start(
                    out=KTM_d.ap().rearrange("c p t -> p c t"), in_=stQK[gs][:, 4:8, 0:128]),
                    reads=[b_stQK[gs]], dma=b_stQK[gs])
            else:
                ph.op("pool", lambda e: e.dma_start(
                    out=QT_d.ap()[s * 4:(s + 1) * 4, :, tok0:tok0 + 512].rearrange("c p t -> p c t"),
                    in_=stQK[gs][:, 0:4, :]), reads=[b_stQK[gs]], dma=b_stQK[gs])
                ph.op("pool", lambda e: e.dma_start(
                    out=KT_d.ap()[s * 4:(s + 1) * 4, :, tok0:tok0 + 512].rearrange("c p t -> p c t"),
                    in_=stQK[gs][:, 4:8, :]), reads=[b_stQK[gs]], dma=b_stQK[gs])

            def tile_ops(j, T):
                k2 = fcount["k"] % 2
                fcount["k"] += 1
                tsl = slice(j * 128, (j + 1) * 128)

                def tokmm(e, c0, n, pi, j=j):
                    ins = None
                    for kc in range(8):
                        ins = e.matmul(out=psK[pi][:, 0:n], lhsT=xnT[gs][:, kc, j * 128:(j + 1) * 128],
                                       rhs=win[:, kc, c0:c0 + n], start=(kc == 0), stop=(kc == 7))
                    return ins
                pi = fcount["f"] % 4; fcount["f"] += 1
                ph.op("pe", (lambda pi, tk: lambda e: tk(e, C_NAV, 512, pi))(pi, tokmm),
                      reads=wb(C_NAV, 512) + [xb[j]], writes=[b_psK[pi]])
                ph.op("act", (lambda pi, k2: lambda e: e.activation(
                    out=stVA[k2][:, :].rearrange("p (h c) -> p h c", h=8)[:, :, 0:64],
                    in_=psK[pi][:, :].rearrange("p (h c) -> p h c", h=8), func=AF.Copy))(pi, k2),
                    reads=[b_psK[pi]], writes=[b_stVA[k2]])
                ph.op("pool", (lambda k2, T: lambda e: e.dma_start(out=VA_d.ap()[T], in_=stVA[k2][:, :]))(k2, T),
                      reads=[b_stVA[k2]], dma=b_stVA[k2])
                pi = fcount["f"] % 4; fcount["f"] += 1
                ph.op("pe", (lambda pi, tk: lambda e: tk(e, C_GV, 512, pi))(pi, tokmm),
                      reads=wb(C_GV, 512) + [xb[j]], writes=[b_psK[pi]])
                ph.op("dve", (lambda pi, k2: lambda e: e.tensor_copy(out=stGV[k2][:, :], in_=psK[pi][:, :]))(pi, k2),
                      reads=[b_psK[pi]], writes=[b_stGV[k2]])
                ph.op("pool", (lambda k2, T: lambda e: e.dma_start(out=GV_d.ap()[T], in_=stGV[k2][:, :]))(k2, T),
                      reads=[b_stGV[k2]], dma=b_stGV[k2])
                pi = fcount["f"] % 4; fcount["f"] += 1
                ph.op("pe", (lambda pi, tk: lambda e: tk(e, C_GR, 512, pi))(pi, tokmm),
                      reads=wb(C_GR, 512) + [xb[j]], writes=[b_psK[pi]])
                ph.op("act", (lambda pi, k2: lambda e: e.activation(
                    out=ge[k2][:, :], in_=psK[pi][:, :], func=AF.Exp, scale=-1.0))(pi, k2),
                    reads=[b_psK[pi]], writes=[b_ge[k2]])
                ph.op("act", (lambda pi, k2: lambda e: e.activation(
                    out=rraw[k2][:, :], in_=psK[pi][:, :], func=AF.Copy))(pi, k2),
                    reads=[b_psK[pi]], writes=[b_rraw[k2]])
                ph.op("act", (lambda k2: lambda e: e.activation(
                    out=ge[k2][:, :], in_=ge[k2][:, :], func=AF.Ln, bias=1.0))(k2),
                    reads=[b_ge[k2]], writes=[b_ge[k2]])
                ph.op("act", (lambda k2: lambda e: e.activation(
                    out=ge[k2][:, :], in_=ge[k2][:, :], func=AF.Exp, scale=-1.0))(k2),
                    reads=[b_ge[k2]], writes=[b_ge[k2]])
                ph.op("dve", (lambda k2: lambda e: e.tensor_tensor(
                    out=stGR[k2][:, :], in0=rraw[k2][:, :], in1=ge[k2][:, :], op=ALU.mult))(k2),
                    reads=[b_rraw[k2], b_ge[k2]], writes=[b_stGR[k2]])
                ph.op("pool", (lambda k2, T: lambda e: e.dma_start(out=GR_d.ap()[T], in_=stGR[k2][:, :]))(k2, T),
                      reads=[b_stGR[k2]], dma=b_stGR[k2])
                pi = fcount["f"] % 4; fcount["f"] += 1
                ph.op("pe", (lambda pi, tk: lambda e: tk(e, C_GK, 256, pi))(pi, tokmm),
                      reads=wb(C_GK, 256) + [xb[j]], writes=[b_psK[pi]])
                ph.op("act", (lambda pi, k2: lambda e: e.activation(
                    out=ktok[k2][:, :], in_=psK[pi][:, 0:256], func=AF.Copy))(pi, k2),
                    reads=[b_psK[pi]], writes=[b_ktok[k2]])
                pi = fcount["f"] % 4; fcount["f"] += 1
                ph.op("pe", (lambda pi, j: lambda e: e.matmul(
                    out=psK[pi][:, :], lhsT=glT[gs][:, j * 128:(j + 1) * 128], rhs=w2[:, :],
                    start=True, stop=True))(pi, j),
                    reads=[b_glT[gs], b_w2], writes=[b_psK[pi]])
                ph.op("act", (lambda pi, k2: lambda e: e.activation(
                    out=gp[k2][:, :], in_=psK[pi][:, :], func=AF.Exp, scale=-1.0))(pi, k2),
                    reads=[b_psK[pi]], writes=[b_gp[k2]])
                ph.op("act", (lambda k2: lambda e: e.activation(
                    out=gp[k2][:, :], in_=gp[k2][:, :], func=AF.Ln, bias=1.0))(k2),
                    reads=[b_gp[k2]], writes=[b_gp[k2]])

                yield
                def cum(e, k2=k2):
                    ins = None
                    for d_ in range(2):
                        tri = consts[:, K_TRIF:K_TRIF + 128] if d_ == 0 else consts[:, K_TRIB:K_TRIB + 128]
                        for hp in range(2):
                            c0 = d_ * 256 + hp * 128
                            ins = e.matmul(out=psB[:, d_ * 2 + hp, :], lhsT=gp[k2][:, c0:c0 + 128], rhs=tri,
                                           start=True, stop=True)
                    for d_ in range(2):
                        su = consts[:, K_SUF:K_SUF + 128] if d_ == 0 else consts[:, K_SUB:K_SUB + 128]
                        ins = e.matmul(out=psS[:, d_ * 256:(d_ + 1) * 256], lhsT=su,
                                       rhs=gp[k2][:, d_ * 256:(d_ + 1) * 256], start=True, stop=True)
                    return ins
                ph.op("pe", cum, reads=[b_gp[k2], b_consts], writes=[b_psB, b_psS])
                ph.op("act", (lambda k2: lambda e: e.activation(
                    out=Eq[k2][:, :, :], in_=psB[:, :, :], func=AF.Exp))(k2),
                    reads=[b_psB], writes=[b_Eq[k2]])
                ph.op("act", (lambda k2: lambda e: e.activation(
                    out=Ek[k2][:, :, :], in_=psB[:, :, :], func=AF.Exp, scale=-1.0))(k2),
                    reads=[b_psB], writes=[b_Ek[k2]])
                ph.op("act", (lambda k2: lambda e: e.activation(
                    out=Ed[k2][:, :], in_=psS[:, :], func=AF.Exp))(k2),
                    reads=[b_psS], writes=[b_Ed[k2]])
                ph.op("pool", (lambda k2, T: lambda e: e.tensor_copy(
                    out=dec[:, T, 0:2], in_=Eq[k2][:, 0:2, 127]))(k2, T),
                    reads=[b_Eq[k2]], writes=[b_dec])
                ph.op("pool", (lambda k2, T: lambda e: e.tensor_copy(
                    out=dec[:, T, 2:4], in_=Eq[k2][:, 2:4, 0]))(k2, T),
                    reads=[b_Eq[k2]], writes=[b_dec])
                for d_ in range(2):
                    ph.op("dve", (lambda k2, d_: lambda e: e.tensor_tensor(
                        out=stKD[k2][:, d_ * 256:(d_ + 1) * 256], in0=ktok[k2][:, :],
                        in1=Ed[k2][:, d_ * 256:(d_ + 1) * 256], op=ALU.mult))(k2, d_),
                        reads=[b_ktok[k2], b_Ed[k2]], writes=[b_stKD[k2]])
                ph.op("pool", (lambda k2, T: lambda e: e.dma_start(out=GKD_d.ap()[T], in_=stKD[k2][:, :]))(k2, T),
                      reads=[b_stKD[k2]], dma=b_stKD[k2])
                if T != META_T:
                    for d_ in range(2):
                        ph.op("dve", (lambda k2, d_, tsl: lambda e: e.tensor_tensor(
                            out=stG[gs][:, d_ * 2:d_ * 2 + 2, tsl], in0=gqk[gs][:, 0:2, tsl],
                            in1=Eq[k2][:, d_ * 2:d_ * 2 + 2, :], op=ALU.mult))(k2, d_, tsl),
                            reads=[b_gqk[gs], b_Eq[k2]], writes=[b_stG[gs]])
                        ph.op("pool", (lambda k2, d_, tsl: lambda e: e.tensor_tensor(
                            out=stG[gs][:, 4 + d_ * 2:4 + d_ * 2 + 2, tsl], in0=gqk[gs][:, 2:4, tsl],
                            in1=Ek[k2][:, d_ * 2:d_ * 2 + 2, :], op=ALU.mult))(k2, d_, tsl),
                            reads=[b_gqk[gs], b_Ek[k2]], writes=[b_stG[gs]])
            gens = [tile_ops(j, T) for j, T in enumerate(tiles)]
            for j in range(len(gens)):
                next(gens[j])
                if j >= 1:
                    for _ in gens[j - 1]:
                        pass
            for _ in gens[-1]:
                pass
            if g != 8:
                ph.op("pool", lambda e: e.dma_start(
                    out=GQ_d.ap()[s * 4:(s + 1) * 4, :, tok0:tok0 + 512].rearrange("c p t -> p c t"),
                    in_=stG[gs][:, 0:4, :]), reads=[b_stG[gs]], dma=b_stG[gs])
                ph.op("pool", lambda e: e.dma_start(
                    out=GK_d.ap()[s * 4:(s + 1) * 4, :, tok0:tok0 + 512].rearrange("c p t -> p c t"),
                    in_=stG[gs][:, 4:8, :]), reads=[b_stG[gs]], dma=b_stG[gs])

        order = [8, 0, 1, 2, 3, 4, 5, 6, 7]
        stageA1(order[0])
        stageA2(order[0], 0)
        for i, g in enumerate(order):
            if i + 1 < len(order):
                stageA1(order[i + 1])
            stageB(g, i % 2)
            if i + 1 < len(order):
                stageA2(order[i + 1], (i + 1) % 2)
        ph.run()
        st.close()


    def rms_rstd(ph, eng_ss, src_ap_fn, b_src, ssbuf, b_ssb, col, junk_ap_fn, b_junkb, n):
        ph.op("dve", lambda e: e.scalar_tensor_tensor(
            out=junk_ap_fn(), in0=src_ap_fn(), scalar=1.0, in1=src_ap_fn(),
            op0=ALU.mult, op1=ALU.mult, accum_out=ssbuf[:, col:col + 1]),
            reads=[b_src], writes=[b_junkb, b_ssb])
        ph.op("act", lambda e: e.activation(
            out=ssbuf[:, col + 1:col + 2], in_=ssbuf[:, col:col + 1], func=AF.Ln, scale=1.0 / n, bias=EPS),
            reads=[b_ssb], writes=[b_ssb])
        ph.op("act", lambda e: e.activation(
            out=ssbuf[:, col + 1:col + 2], in_=ssbuf[:, col + 1:col + 2], func=AF.Exp, scale=-0.5),
            reads=[b_ssb], writes=[b_ssb])

    def phase2():
        st = ExitStack()

        def sbt(name, shape, dt):
            return st.enter_context(nc.sbuf_tensor("p2_" + name, list(shape), dt))

        def pst(name, shape, dt):
            return st.enter_context(nc.psum_tensor("p2_" + name, list(shape), dt))

        ph = Ph(cx, "p2")
        Sf = sbt("Sf", [128, 32, 2, 256], BF16)
        b_Sf = cx.bufs(32, "Sf")
        Sinit = sbt("Sinit", [128, 2, 256], F32)
        b_Sinit = cx.buf("Sinit")
        S32 = sbt("S32", [128, 2, 256], F32)
        b_S32 = cx.buf("S32")
        Sb32 = sbt("Sb32", [128, 2, 256], F32)
        b_Sb32 = cx.buf("Sb32")
        Sbb = [sbt("Sbb%d" % i, [128, 2, 256], BF16) for i in range(2)]
        b_Sbb = cx.bufs(2, "Sbb")
        ggla = sbt("ggla", [128, 128], F32)
        b_ggla = cx.buf("ggla")
        kd = [sbt("kd%d" % i, [128, 512], BF16) for i in range(3)]
        b_kd = cx.bufs(3, "kd")
        gv = [sbt("gv%d" % i, [128, 512], BF16) for i in range(3)]
        b_gv = cx.bufs(3, "gv")
        gr = [sbt("gr%d" % i, [128, 512], BF16) for i in range(3)]
        b_gr = cx.bufs(3, "gr")
        gqs = [sbt("gqs%d" % i, [128, 4, SEQ], BF16) for i in range(2)]
        b_gqs = cx.bufs(2, "gqs")
        gks = [sbt("gks%d" % i, [128, 4, SEQ], BF16) for i in range(2)]
        b_gks = cx.bufs(2, "gks")
        AT = [sbt("AT%d" % i, [128, 128], BF16) for i in range(4)]
        b_AT = cx.bufs(4, "AT")
        osb = [sbt("osb%d" % i, [128, 512], F32) for i in range(2)]
        b_osb = cx.bufs(2, "osb")
        jk = sbt("jk", [128, 512], BF16)
        b_jk = cx.bufs(4, "jk")
        ssg = [sbt("ssg%d" % i, [128, 8], F32) for i in range(2)]
        b_ssg = cx.bufs(2, "ssg")
        t2 = [sbt("t2_%d" % i, [128, 512], F32) for i in range(2)]
        b_t2 = cx.bufs(2, "t2")
        gout = [sbt("gout%d" % i, [128, 512], BF16) for i in range(2)]
        b_gout = cx.bufs(2, "gout")
        mst = [sbt("mst%d" % i, [128, 4, 128], BF16) for i in range(2)]
        b_mst = cx.bufs(2, "mst")

        psA = [pst("psA%d" % i, [128, 512], F32) for i in range(2)]
        b_psA = cx.bufs(2, "psA")
        psO = [pst("psO%d" % i, [128, 512], F32) for i in range(2)]
        b_psO = cx.bufs(2, "psO")
        psKVt = [pst("psKV%d" % i, [128, 512], F32) for i in range(2)]
        b_psKV = cx.bufs(2, "psKV")
        psT = pst("psT", [128, 1024], BF16)
        b_psT = cx.buf("psT")
        psKV1 = pst("psKVs", [128, 512], F32)
        b_psKV1 = cx.buf("psKV1")

        ph.op("sp", lambda e: e.dma_start(out=ggla[:, :], in_=ggla_d.ap().broadcast_to([128, 128])),
              writes=[b_ggla], dma=b_ggla)

        for s_ in range(2):
            ph.op("sp", (lambda s_: lambda e: e.dma_start(
                out=gqs[s_][:, :, :], in_=GQ_d.ap()[s_ * 4:(s_ + 1) * 4].rearrange("c p t -> p c t")))(s_),
                writes=[b_gqs[s_]], dma=b_gqs[s_])
            ph.op("sp", (lambda s_: lambda e: e.dma_start(
                out=gks[s_][:, :, :], in_=GK_d.ap()[s_ * 4:(s_ + 1) * 4].rearrange("c p t -> p c t")))(s_),
                writes=[b_gks[s_]], dma=b_gks[s_])
        cnt = {"ld": 0, "at": 0}
        kd1 = [sbt("kd1_%d" % i, [128, 512], BF16) for i in range(2)]
        b_kd1 = cx.bufs(2, "kd1")
        gv1 = [sbt("gv1_%d" % i, [128, 512], BF16) for i in range(2)]
        b_gv1 = cx.bufs(2, "gv1")

        def load_kv(T):
            i = cnt["ld"] % 2
            cnt["ld"] += 1
            ph.op("sp", lambda e: e.dma_start(out=kd1[i][:, :], in_=GKD_d.ap()[T]), writes=[b_kd1[i]], dma=b_kd1[i])
            ph.op("sp", lambda e: e.dma_start(out=gv1[i][:, :], in_=GV_d.ap()[T]), writes=[b_gv1[i]], dma=b_gv1[i])
            return i

        def kv_mm1(i):
            def f(e):
                ins = None
                for hp in range(2):
                    ins = e.matmul(out=psKV1[:, hp * 256:(hp + 1) * 256], lhsT=kd1[i][:, hp * 128:hp * 128 + 128],
                                   rhs=gv1[i][:, hp * 256:(hp + 1) * 256], start=True, stop=True)
                return ins
            return f

        def kv_mm(i, hp, d_):
            return lambda e: e.matmul(out=psKVt[hp][:, 0:256], lhsT=kd[i][:, d_ * 256 + hp * 128:d_ * 256 + hp * 128 + 128],
                                      rhs=gv[i][:, hp * 256:(hp + 1) * 256], start=True, stop=True)

        i = load_kv(META_T)
        ph.op("pe", kv_mm1(i), reads=[b_kd1[i], b_gv1[i]], writes=[b_psKV1])
        ph.op("act", lambda e: e.activation(
            out=Sinit[:, :, :], in_=psKV1[:, :].rearrange("p (h c) -> p h c", h=2), func=AF.Copy),
            reads=[b_psKV1], writes=[b_Sinit])

        S32pp = [S32, sbt("S32b", [128, 2, 256], F32)]
        b_S32pp = [b_S32, cx.buf("S32b")]

        def sweep1_step(s, t):
            T = s * 16 + t
            src, dst = S32pp[t % 2], S32pp[(t + 1) % 2]
            bsrc, bdst = b_S32pp[t % 2], b_S32pp[(t + 1) % 2]
            if t == 0:
                ph.op("pool", lambda e: e.tensor_copy(out=src[:, :, :], in_=Sinit[:, :, :]),
                      reads=[b_Sinit], writes=[bsrc])
            ph.op("act", lambda e: e.activation(out=Sf[:, T, :, :], in_=src[:, :, :], func=AF.Copy),
                  reads=[bsrc], writes=[b_Sf[T]])
            if t == 15:
                return
            i = load_kv(T)
            ph.op("pe", kv_mm1(i), reads=[b_kd1[i], b_gv1[i]], writes=[b_psKV1])
            for hp in range(2):
                ph.op("dve", (lambda hp: lambda e: e.scalar_tensor_tensor(
                    out=dst[:, hp, :], in0=src[:, hp, :], scalar=dec[:, T, hp:hp + 1],
                    in1=psKV1[:, hp * 256:(hp + 1) * 256], op0=ALU.mult, op1=ALU.add))(hp),
                    reads=[bsrc, b_psKV1, b_dec], writes=[bdst])

        for t in range(16):
            sweep1_step(0, t)
        tiles2 = [(s, t) for s in range(2) for t in range(15, -1, -1)]

        def part1(k):
            s, t = tiles2[k]
            T = s * 16 + t
            cur = k % 2
            nxt = 1 - cur
            if t == 15:
                ph.op("pool", lambda e: e.memset(Sb32[:, :, :], 0.0), writes=[b_Sb32])
                ph.op("pool", lambda e: e.memset(Sbb[cur][:, :, :], 0.0), writes=[b_Sbb[cur]])
            i = k % 3
            ph.op("sp", lambda e: e.dma_start(out=kd[i][:, :], in_=GKD_d.ap()[T]), writes=[b_kd[i]], dma=b_kd[i])
            ph.op("sp", lambda e: e.dma_start(out=gv[i][:, :], in_=GV_d.ap()[T]), writes=[b_gv[i]], dma=b_gv[i])
            ph.op("sp", lambda e: e.dma_start(out=gr[i][:, :], in_=GR_d.ap()[T]), writes=[b_gr[i]], dma=b_gr[i])
            ph.op("pool", lambda e: e.tensor_tensor(
                out=t2[k % 2][:, :].rearrange("p (h c) -> p h c", h=4),
                in0=gr[i][:, :].rearrange("p (h c) -> p h c", h=4),
                in1=_ap(ggla, 0, [[128, 128], [0, 4], [1, 128]]), op=ALU.mult),
                reads=[b_gr[i], b_ggla], writes=[b_t2[k % 2]])
            po = k % 2
            tsl = slice(t * 128, (t + 1) * 128)

            def rec_amm(h):
                hp, base = h // 2, 64 * (h % 2)
                pa = h % 2

                def amm(e):
                    ins = None
                    for d_ in range(2):
                        c = d_ * 2 + hp
                        ins = e.matmul(out=psA[pa][:, d_ * 128:(d_ + 1) * 128], lhsT=gks[s][base:base + 64, c, tsl],
                                       rhs=gqs[s][base:base + 64, c, tsl], start=True, stop=True)
                    return ins
                ph.op("pe", amm, reads=[b_gks[s], b_gqs[s]], writes=[b_psA[pa]])

            def rec_mask_omm(h):
                hp, base = h // 2, 64 * (h % 2)
                pa = h % 2
                for d_ in range(2):
                    ph.op("dve", (lambda d_: lambda e: e.tensor_tensor(
                        out=AT[pa * 2 + d_][:, :], in0=psA[pa][:, d_ * 128:(d_ + 1) * 128],
                        in1=(maskF if d_ == 0 else maskB)[:, :], op=ALU.mult))(d_),
                        reads=[b_psA[pa], b_cbf], writes=[b_AT[pa * 2 + d_]])

                def omm(e):
                    o = psO[po][:, h * 128:(h + 1) * 128]
                    vv = gv[i][:, h * 128:(h + 1) * 128]
                    sc = (h % 2) * 128
                    e.matmul(out=o, lhsT=AT[pa * 2][:, :], rhs=vv, start=True, stop=False)
                    e.matmul(out=o, lhsT=gqs[s][base:base + 64, hp, tsl],
                             rhs=Sf[base:base + 64, T, hp, sc:sc + 128], start=False, stop=False)
                    e.matmul(out=o, lhsT=AT[pa * 2 + 1][:, :], rhs=vv, start=False, stop=False)
                    return e.matmul(out=o, lhsT=gqs[s][base:base + 64, 2 + hp, tsl],
                                    rhs=Sbb[cur][base:base + 64, hp, sc:sc + 128], start=False, stop=True)
                ph.op("pe", omm, reads=[b_AT[pa * 2], b_AT[pa * 2 + 1], b_gv[i], b_gqs[s], b_Sf[T], b_Sbb[cur]],
                      writes=[b_psO[po]])

            if t > 0:
                for hp in range(2):
                    ph.op("pe", kv_mm(i, hp, 1), reads=[b_kd[i], b_gv[i]], writes=[b_psKV[hp]])
                    ph.op("dve", (lambda hp: lambda e: e.scalar_tensor_tensor(
                        out=Sb32[:, hp, :], in0=Sb32[:, hp, :], scalar=dec[:, T, 2 + hp:3 + hp], in1=psKVt[hp][:, 0:256],
                        op0=ALU.mult, op1=ALU.add))(hp),
                        reads=[b_Sb32, b_psKV[hp], b_dec], writes=[b_Sb32])
                ph.op("act", lambda e: e.activation(out=Sbb[nxt][:, :, :], in_=Sb32[:, :, :], func=AF.Copy),
                      reads=[b_Sb32], writes=[b_Sbb[nxt]])
            rec_amm(0)
            for h in range(4):
                if h + 1 < 4:
                    rec_amm(h + 1)
                rec_mask_omm(h)
        def part2(k):
            po = k % 2
            go = k % 2
            ob = k % 2
            ph.op("act", lambda e: e.activation(out=osb[ob][:, :], in_=psO[po][:, :], func=AF.Copy),
                  reads=[b_psO[po]], writes=[b_osb[ob]])
            for h in range(4):
                ph.op("dve", (lambda h: lambda e: e.scalar_tensor_tensor(
                    out=jk[:, h * 128:(h + 1) * 128], in0=osb[ob][:, h * 128:(h + 1) * 128], scalar=1.0,
                    in1=osb[ob][:, h * 128:(h + 1) * 128], op0=ALU.mult, op1=ALU.mult,
                    accum_out=ssg[ob][:, h:h + 1]))(h),
                    reads=[b_osb[ob]], writes=[b_jk[h], b_ssg[ob]])
            ph.op("act", lambda e: e.activation(out=ssg[ob][:, 4:8], in_=ssg[ob][:, 0:4], func=AF.Ln,
                                                scale=1.0 / 128, bias=EPS), reads=[b_ssg[ob]], writes=[b_ssg[ob]])
            ph.op("act", lambda e: e.activation(out=ssg[ob][:, 4:8], in_=ssg[ob][:, 4:8], func=AF.Exp, scale=-0.5),
                  reads=[b_ssg[ob]], writes=[b_ssg[ob]])
            ph.op("pool", lambda e: e.tensor_tensor(
                out=osb[ob][:, :].rearrange("p (h c) -> p h c", h=4),
                in0=osb[ob][:, :].rearrange("p (h c) -> p h c", h=4),
                in1=_ap(ssg[ob], 4, [[8, 128], [1, 4], [0, 128]]), op=ALU.mult),
                reads=[b_osb[ob], b_ssg[ob]], writes=[b_osb[ob]])
            ph.op("pool", lambda e: e.tensor_tensor(
                out=gout[go][:, :], in0=osb[ob][:, :], in1=t2[k % 2][:, :], op=ALU.mult),
                reads=[b_osb[ob], b_t2[k % 2]], writes=[b_gout[go]])

        def part3(k):
            s, t = tiles2[k]
            go = k % 2

            def trg(e):
                ins = None
                for c in range(4):
                    ins = e.transpose(out=psT[:, c * 128:(c + 1) * 128], in_=gout[go][:, c * 128:(c + 1) * 128],
                                      identity=ident_bf[:, :])
                return ins
            ph.op("pe", trg, reads=[b_gout[go], b_cbf], writes=[b_psT])
            ph.op("act", lambda e: e.activation(
                out=mst[go][:, :, :], in_=psT[:, 0:512].rearrange("p (c t) -> p c t", c=4), func=AF.Copy),
                reads=[b_psT], writes=[b_mst[go]])
            ph.op("pool", lambda e: e.dma_start(
                out=MIXT_d.ap()[s, 4:8, :, t * 128:(t + 1) * 128].rearrange("c p t -> p c t"),
                in_=mst[go][:, :, :]), reads=[b_mst[go]], dma=b_mst[go])

        n2 = 0 if P2_CUT in (1, 3, 4) else len(tiles2)
        for k in range(n2 + 2):
            if k < 16:
                sweep1_step(1, k)
            if k < n2:
                part1(k)
            if 0 <= k - 1 < n2:
                part2(k - 1)
            if 0 <= k - 2 < n2:
                part3(k - 2)
        ph.run()
        st.close()

    def phase3(wload):
        st = ExitStack()

        def sbt(name, shape, dt):
            return st.enter_context(nc.sbuf_tensor("p3_" + name, list(shape), dt))

        def pst(name, shape, dt):
            return st.enter_context(nc.psum_tensor("p3_" + name, list(shape), dt))

        ph = Ph(cx, "p3")
        EB = sbt("EB", [128, 8, NCFG, 128], BF16)
        b_EB = cx.bufs(8, "EB")
        tstage = [sbt("tstage%d" % i, [128, NCFG, 128], F32) for i in range(2)]
        b_tstage = cx.bufs(2, "tstage")
        mbias = sbt("mbias", [64, 8], F32)
        b_mbias = cx.buf("mbias")
        qT = sbt("qT", [128, 4, SEQ], BF16)
        b_qT = cx.bufs(4, "qT")
        kT = sbt("kT", [128, 4, SEQ], BF16)
        b_kT = cx.bufs(4, "kT")
        va = sbt("va", [128, 16, 520], BF16)
        b_va = cx.bufs(4, "va")
        ktm = sbt("ktm", [128, 4, 64], BF16)
        b_ktm = cx.buf("ktm")
        vam = sbt("vam", [64, 520], BF16)
        b_vam = cx.buf("vam")
        P = [sbt("P%d" % i, [128, 5, 128], BF16) for i in range(4)]
        b_P = cx.bufs(4, "P")
        Pm = [sbt("Pm%d" % i, [64, 128], BF16) for i in range(4)]
        b_Pm = cx.bufs(4, "Pm")
        rden = sbt("rden", [128, 8], F32)
        b_rden = cx.buf("rden")
        nao = [sbt("nao%d" % i, [128, 512], BF16) for i in range(2)]
        b_nao = cx.bufs(2, "nao")
        mst = [sbt("mst%d" % i, [128, 4, 128], BF16) for i in range(2)]
        b_mst = cx.bufs(2, "mst")

        psS = [pst("psS%d" % i, [128, 8, 128], F32) for i in range(2)]
        b_psS = cx.bufs(2, "psS")
        psO = [pst("psO%d" % i, [128, 4, 128], F32) for i in range(2)]
        b_psO = cx.bufs(2, "psO")
        psT = pst("psT", [128, 1024], BF16)
        b_psT = cx.buf("psT")

        def prep_tables():
          for h in range(8):
            i = h % 2
            ph.op("sp", (lambda i, h: lambda e: e.dma_start(
                out=tstage[i][:, :, :], in_=tbl_d.ap()[h].rearrange("c k q -> k c q")))(i, h),
                writes=[b_tstage[i]], dma=b_tstage[i])
            ph.op("act", (lambda i, h: lambda e: e.activation(
                out=EB[:, h, :, :], in_=tstage[i][:, :, :], func=AF.Exp))(i, h),
                reads=[b_tstage[i]], writes=[b_EB[h]])
        ph.op("pool", lambda e: e.memset(mbias[:, :], NEG), writes=[b_mbias])
        for h in range(8):
            ph.op("sp", (lambda h: lambda e: e.dma_start(
                out=mbias[48:64, h:h + 1], in_=mb_d.ap()[h:h + 1, :].rearrange("o m -> m o")))(h),
                writes=[b_mbias], dma=b_mbias)
        ph.op("sp", lambda e: e.dma_start(out=ktm[:, :, :],
                                          in_=KTM_d.ap()[:, :, 64:128].rearrange("c p t -> p c t")),
              writes=[b_ktm], dma=b_ktm)
        ph.op("sp", lambda e: e.dma_start(out=vam[:, :], in_=VA_d.ap()[META_T, 64:128, :]),
              writes=[b_vam], dma=b_vam)

        units = []
        for s in range(2):
            for t in range(16):
                if t < 2:
                    c0, nch, slot0 = 0, 4, (8 if t == 0 else 7)
                elif t >= 14:
                    c0, nch, slot0 = 12, 4, (6 if t == 14 else 5)
                else:
                    c0, nch, slot0 = t - 2, 5, 0
                for h in range(8):
                    units.append((s, t, h, c0, nch, slot0))

        def load_grp(s, g):
            gs_ = slice(g * 512, (g + 1) * 512)
            ph.op("sp", lambda e: e.dma_start(
                out=qT[:, :, gs_], in_=QT_d.ap()[s * 4:(s + 1) * 4, :, gs_].rearrange("c p t -> p c t")),
                writes=[b_qT[g]], dma=b_qT[g])
            ph.op("sp", lambda e: e.dma_start(
                out=kT[:, :, gs_], in_=KT_d.ap()[s * 4:(s + 1) * 4, :, gs_].rearrange("c p t -> p c t")),
                writes=[b_kT[g]], dma=b_kT[g])
            ph.op("sp", lambda e: e.dma_start(
                out=va[:, g * 4:(g + 1) * 4, :],
                in_=VA_d.ap()[s * 16 + g * 4:s * 16 + (g + 1) * 4].rearrange("c p t -> p c t")),
                writes=[b_va[g]], dma=b_va[g])

        def stage1_pair(ua, ub):
            s, t, ha, c0, nch, slot0 = units[ua]
            hb_ = units[ub][2]
            assert units[ub][1] == t and ha % 2 == 0 and hb_ == ha + 1 and ua % 2 == 0
            hp = ha // 2

            def smm2(e):
                qa = qT[0:64, hp, t * 128:(t + 1) * 128]
                qb = qT[64:128, hp, t * 128:(t + 1) * 128]
                for ci in range(nch):
                    c = c0 + ci
                    e.matmul(out=psS[0][:, ci, :], lhsT=kT[0:64, hp, c * 128:(c + 1) * 128], rhs=qa,
                             start=True, stop=True)
                    e.matmul(out=psS[1][:, ci, :], lhsT=kT[64:128, hp, c * 128:(c + 1) * 128], rhs=qb,
                             start=True, stop=True)
                e.matmul(out=psS[0][0:64, 5, :], lhsT=ktm[0:64, hp, :], rhs=qa, start=True, stop=True)
                return e.matmul(out=psS[1][0:64, 5, :], lhsT=ktm[64:128, hp, :], rhs=qb, start=True, stop=True)
            kgrps = sorted(set((c0 + ci) // 4 for ci in range(nch)))
            ph.op("pe", smm2, reads=[b_qT[t // 4], b_ktm] + [b_kT[g_] for g_ in kgrps],
                  writes=[b_psS[0], b_psS[1]])
            for ui, pi, h in ((ua, 0, ha), (ub, 1, hb_)):
                p4 = ui % 4
                ph.op("act", (lambda pi, p4: lambda e: e.activation(
                    out=P[p4][:, 0:nch, :], in_=psS[pi][:, 0:nch, :], func=AF.Exp))(pi, p4),
                    reads=[b_psS[pi]], writes=[b_P[p4]])
                ph.op("act", (lambda pi, p4, h: lambda e: e.activation(
                    out=Pm[p4][:, :], in_=psS[pi][0:64, 5, :], func=AF.Exp, bias=mbias[:, h:h + 1]))(pi, p4, h),
                    reads=[b_psS[pi], b_mbias], writes=[b_Pm[p4]])
                ph.op("dve", (lambda p4, h: lambda e: e.tensor_tensor(
                    out=P[p4][:, 0:nch, :], in0=P[p4][:, 0:nch, :], in1=EB[:, h, slot0:slot0 + nch, :],
                    op=ALU.mult))(p4, h), reads=[b_P[p4], b_EB[h]], writes=[b_P[p4]])

        def stage2(ui):
            s, t, h, c0, nch, slot0 = units[ui]
            pi = ui % 4

            def pvm(e):
                o = psO[h // 4][:, h % 4, 0:65]
                for ci in range(nch):
                    e.matmul(out=o, lhsT=P[pi][:, ci, :], rhs=va[:, c0 + ci, h * 65:(h + 1) * 65],
                             start=(ci == 0), stop=False)
                return e.matmul(out=o, lhsT=Pm[pi][:, :], rhs=vam[:, h * 65:(h + 1) * 65],
                                start=False, stop=True)
            kgrps = sorted(set((c0 + ci) // 4 for ci in range(nch)))
            ph.op("pe", pvm, reads=[b_P[pi], b_Pm[pi], b_vam] + [b_va[g_] for g_ in kgrps], writes=[b_psO[h // 4]])
            if h == 7:
                k2 = (ui // 8) % 2
                for hb in range(2):
                    ph.op("dve", (lambda hb: lambda e: e.reciprocal(
                        out=rden[:, hb * 4:(hb + 1) * 4], in_=psO[hb][:, :, 64]))(hb),
                        reads=[b_psO[hb]], writes=[b_rden])
                    ph.op("dve", (lambda hb: lambda e: e.tensor_tensor(
                        out=nao[k2][:, hb * 256:(hb + 1) * 256].rearrange("p (h c) -> p h c", h=4),
                        in0=psO[hb][:, :, 0:64],
                        in1=_ap(rden, hb * 4, [[8, 128], [1, 4], [0, 64]]), op=ALU.mult))(hb),
                        reads=[b_psO[hb], b_rden], writes=[b_nao[k2]])

        def stage3(ui):
            s, t, h, c0, nch, slot0 = units[ui]
            if h != 7:
                return
            k2 = (ui // 8) % 2

            def trn(e):
                ins = None
                for c in range(4):
                    ins = e.transpose(out=psT[:, c * 128:(c + 1) * 128], in_=nao[k2][:, c * 128:(c + 1) * 128],
                                      identity=ident_bf[:, :])
                return ins
            ph.op("pe", trn, reads=[b_nao[k2], b_cbf], writes=[b_psT])
            ph.op("act", lambda e: e.activation(
                out=mst[k2][:, :, :], in_=psT[:, 0:512].rearrange("p (c t) -> p c t", c=4), func=AF.Copy),
                reads=[b_psT], writes=[b_mst[k2]])
            ph.op("act", lambda e: e.dma_start(
                out=MIXT_d.ap()[s, 0:4, :, t * 128:(t + 1) * 128].rearrange("c p t -> p c t"),
                in_=mst[k2][:, :, :]), reads=[b_mst[k2]], dma=b_mst[k2])

        nu = len(units)
        load_grp(0, 0)
        prep_tables()
        for g_ in range(1, 4):
            load_grp(0, g_)
        for s in range(2):
            us = [ui for ui in range(nu) if units[ui][0] == s]
            n = len(us)
            np_ = n // 2
            for p in range(np_ + 3):
                k = 2 * p
                if s == 0 and k == 24:
                    wload(ph, b_psT)
                if s == 0:
                    for g_ in range(4):
                        if k == min(n + 2, (4 * g_ + 7) * 8 + 2):
                            load_grp(1, g_)
                if p < np_:
                    stage1_pair(us[2 * p], us[2 * p + 1])
                if 0 <= p - 1 < np_:
                    stage2(us[2 * (p - 1)])
                    stage2(us[2 * (p - 1) + 1])
                if 0 <= p - 2 < np_:
                    stage3(us[2 * (p - 2)])
                    stage3(us[2 * (p - 2) + 1])
        ph.run()
        st.close()

    def phase4a(wout, b_wout, wload):
        st = ExitStack()

        def sbt(name, shape, dt):
            return st.enter_context(nc.sbuf_tensor("p4a_" + name, list(shape), dt))

        def pst(name, shape, dt):
            return st.enter_context(nc.psum_tensor("p4a_" + name, list(shape), dt))

        ph = Ph(cx, "p4a")
        gffn = sbt("gffn", [128, D], F32)
        b_gffn = cx.buf("gffn")
        mixt = [sbt("mixt%d" % i, [128, 8, 128], BF16) for i in range(2)]
        b_mixt = cx.bufs(2, "mixt")
        xt = [sbt("xt%d" % i, [128, D], F32) for i in range(2)]
        b_xt = cx.bufs(2, "xt")
        h1 = [sbt("h1%d" % i, [128, D], F32) for i in range(2)]
        b_h1 = cx.bufs(2, "h1")
        junk = sbt("junk", [128, D], BF16)
        b_junk = cx.buf("junk")
        ss = sbt("ss", [128, 4], F32)
        b_ss = cx.bufs(2, "ss")
        hn = [sbt("hn%d" % i, [128, D], BF16) for i in range(2)]
        b_hn = cx.bufs(2, "hn")
        hst = [sbt("hst%d" % i, [128, 8, 128], BF16) for i in range(2)]
        b_hst = cx.bufs(2, "hst")
        psH = [pst("psH%d" % i, [128, D], F32) for i in range(2)]
        b_psH = cx.bufs(2, "psH")
        psT = [pst("psT%d" % i, [128, D], BF16) for i in range(2)]
        b_psT = cx.bufs(2, "psT")

        ph.op("sp", lambda e: e.dma_start(out=gffn[:, :], in_=gffn_d.ap().broadcast_to([128, D])),
              writes=[b_gffn], dma=b_gffn)
        def partA(T):
            s, t = T // 16, T % 16
            i = T % 2
            ph.op("sp", lambda e: e.dma_start(
                out=mixt[i][:, :, :],
                in_=MIXT_d.ap()[s, :, :, t * 128:(t + 1) * 128].rearrange("c p t -> p c t")),
                writes=[b_mixt[i]], dma=b_mixt[i])
            ph.op("sp", lambda e: e.dma_start(
                out=xt[i][:, :], in_=x_d.ap()[s, t * 128:(t + 1) * 128, :]),
                writes=[b_xt[i]], dma=b_xt[i])

            def mm(e):
                ins = None
                for n in range(2):
                    for kc in range(8):
                        ins = e.matmul(out=psH[i][:, n * 512:(n + 1) * 512], lhsT=mixt[i][:, kc, :],
                                       rhs=wout[:, kc, n * 512:(n + 1) * 512], start=(kc == 0), stop=(kc == 7))
                return ins
            ph.op("pe", mm, reads=[b_mixt[i]] + b_wout, writes=[b_psH[i]])
            ph.op("dve", lambda e: e.tensor_tensor(
                out=h1[i][:, :], in0=psH[i][:, :], in1=xt[i][:, :], op=ALU.add),
                reads=[b_psH[i], b_xt[i]], writes=[b_h1[i]])
            ph.op("act", lambda e: e.dma_start(out=H1_d.ap()[T], in_=h1[i][:, :]),
                  reads=[b_h1[i]], dma=b_h1[i])
            rms_rstd(ph, "dve", lambda: h1[i][:, :], b_h1[i], ss, b_ss[i], i * 2,
                     lambda: junk[:, :], b_junk, D)
            ph.op("dve", lambda e: e.scalar_tensor_tensor(
                out=hn[i][:, :], in0=h1[i][:, :], scalar=ss[:, i * 2 + 1:i * 2 + 2], in1=gffn[:, :],
                op0=ALU.mult, op1=ALU.mult),
                reads=[b_h1[i], b_ss[i], b_gffn], writes=[b_hn[i]])

        def partB(T):
            g, j = T // 4, T % 4
            i = T % 2

            def tr(e):
                ins = None
                for kc in range(8):
                    ins = e.transpose(out=psT[i][:, kc * 128:(kc + 1) * 128],
                                      in_=hn[i][:, kc * 128:(kc + 1) * 128], identity=ident_bf[:, :])
                return ins
            ph.op("pe", tr, reads=[b_hn[i], b_cbf], writes=[b_psT[i]])
            ph.op("act", lambda e: e.activation(
                out=hst[i][:, :, :], in_=psT[i][:, :].rearrange("p (k t) -> p k t", k=8), func=AF.Copy),
                reads=[b_psT[i]], writes=[b_hst[i]])
            ph.op("act", lambda e: e.dma_start(
                out=HNT_d.ap()[g].rearrange("p (k t) -> p k t", k=8)[:, :, j * 128:(j + 1) * 128],
                in_=hst[i][:, :, :]), reads=[b_hst[i]], dma=b_hst[i])

        for T in range(33):
            if T == 3:
                wload(ph, b_psH[0])
            if T < 32:
                partA(T)
            if T >= 1:
                partB(T - 1)
        ph.run()
        st.close()

    def phase4b(wg, wu, wd, b_wg, b_wu, b_wd):
        st = ExitStack()

        def sbt(name, shape, dt):
            return st.enter_context(nc.sbuf_tensor("p4b_" + name, list(shape), dt))

        def pst(name, shape, dt):
            return st.enter_context(nc.psum_tensor("p4b_" + name, list(shape), dt))

        ph = Ph(cx, "p4b")
        gfin = sbt("gfin", [128, D], F32)
        b_gfin = cx.buf("gfin")
        hnT = [sbt("hnT%d" % i, [128, 8, 512], BF16) for i in range(2)]
        b_hnT = cx.bufs(2, "hnT")
        actT = sbt("actT", [128, NFF, 512], BF16)
        b_actT = cx.bufs(NFF, "actT")
        sg = [sbt("sg%d" % i, [128, 512], F32) for i in range(2)]
        b_sg = cx.bufs(2, "sg")
        h1 = [sbt("h1%d" % i, [128, D], F32) for i in range(2)]
        b_h1 = cx.bufs(2, "h1")
        junk_ap = sg[0][:, :].bitcast(BF16)
        b_junk = b_sg[0]
        ss = sbt("ss", [128, 4], F32)
        b_ss = cx.bufs(2, "ss")
        psG = [pst("psG%d" % i, [128, 512], F32) for i in range(2)]
        b_psG = cx.bufs(2, "psG")
        psU = [pst("psU%d" % i, [128, 512], F32) for i in range(2)]
        b_psU = cx.bufs(2, "psU")
        psD = [pst("psD%d" % i, [128, D], F32) for i in range(2)]
        b_psD = cx.bufs(2, "psD")

        ph.op("sp", lambda e: e.dma_start(out=gfin[:, :], in_=gfin_d.ap().broadcast_to([128, D])),
              writes=[b_gfin], dma=b_gfin)
        fc = 0
        tcn = 0
        for g in range(8):
            gi = g % 2
            ph.op("sp", (lambda gi, g: lambda e: e.dma_start(
                out=hnT[gi][:, :, :], in_=HNT_d.ap()[g].rearrange("p (k t) -> p k t", k=8)))(gi, g),
                writes=[b_hnT[gi]], dma=b_hnT[gi])
            for f in range(NFF):
                pi = fc % 2
                fc += 1

                def gmm(e, pi=pi, f=f, gi=gi):
                    ins = None
                    for kc in range(8):
                        ins = e.matmul(out=psG[pi][:, :], lhsT=wg[:, kc, f * 128:(f + 1) * 128],
                                       rhs=hnT[gi][:, kc, :], start=(kc == 0), stop=(kc == 7))
                    return ins

                def umm(e, pi=pi, f=f, gi=gi):
                    ins = None
                    for kc in range(8):
                        ins = e.matmul(out=psU[pi][:, :], lhsT=wu[:, kc, f * 128:(f + 1) * 128],
                                       rhs=hnT[gi][:, kc, :], start=(kc == 0), stop=(kc == 7))
                    return ins
                ph.op("pe", gmm, reads=[b_hnT[gi]] + b_wg, writes=[b_psG[pi]])
                ph.op("pe", umm, reads=[b_hnT[gi]] + b_wu, writes=[b_psU[pi]])
                ph.op("act", (lambda pi: lambda e: e.activation(out=sg[pi][:, :], in_=psG[pi][:, :], func=AF.Silu))(pi),
                      reads=[b_psG[pi]], writes=[b_sg[pi]])
                ph.op("dve", (lambda pi, f: lambda e: e.tensor_tensor(
                    out=actT[:, f, :], in0=psU[pi][:, :], in1=sg[pi][:, :], op=ALU.mult))(pi, f),
                    reads=[b_psU[pi], b_sg[pi]], writes=[b_actT[f]])
            for j in range(4):
                T = g * 4 + j
                s, t = T // 16, T % 16
                i = tcn % 2
                tcn += 1
                ph.op("sp", (lambda i, T: lambda e: e.dma_start(out=h1[i][:, :], in_=H1_d.ap()[T]))(i, T),
                      writes=[b_h1[i]], dma=b_h1[i])

                def dmm(e, i=i, j=j):
                    ins = None
                    for n in range(2):
                        for f in range(NFF):
                            ins = e.matmul(out=psD[i][:, n * 512:(n + 1) * 512], lhsT=actT[:, f, j * 128:(j + 1) * 128],
                                           rhs=wd[:, f, n * 512:(n + 1) * 512], start=(f == 0), stop=(f == NFF - 1))
                    return ins
                ph.op("pe", dmm, reads=b_actT + b_wd, writes=[b_psD[i]])
                ph.op("dve", (lambda i: lambda e: e.tensor_tensor(
                    out=h1[i][:, :], in0=psD[i][:, :], in1=h1[i][:, :], op=ALU.add))(i),
                    reads=[b_psD[i], b_h1[i]], writes=[b_h1[i]])
                rms_rstd(ph, "dve", (lambda i: lambda: h1[i][:, :])(i), b_h1[i], ss, b_ss[i], i * 2,
                         lambda: junk_ap, b_junk, D)
                ph.op("dve", (lambda i: lambda e: e.scalar_tensor_tensor(
                    out=h1[i][:, :], in0=h1[i][:, :], scalar=ss[:, i * 2 + 1:i * 2 + 2], in1=gfin[:, :],
                    op0=ALU.mult, op1=ALU.mult))(i),
                    reads=[b_h1[i], b_ss[i], b_gfin], writes=[b_h1[i]])
                ph.op("pool", (lambda i, s, t: lambda e: e.dma_start(
                    out=out_d.ap()[s, t * 128:(t + 1) * 128, :], in_=h1[i][:, :]))(i, s, t),
                    reads=[b_h1[i]], dma=b_h1[i])
        ph.run()
        st.close()

    phase1()
    if LAST_PHASE >= 2:
        phase2()
    if LAST_PHASE >= 3:
        wst = ExitStack()
        wst_o = ExitStack()
        wg = wst.enter_context(nc.sbuf_tensor("w_wg", [128, 8, DFF], BF16))
        wu = wst.enter_context(nc.sbuf_tensor("w_wu", [128, 8, DFF], BF16))
        wout = wst_o.enter_context(nc.sbuf_tensor("w_wout", [128, 8, D], BF16))
        b_wout = cx.bufs(8, "wout")
        b_wg = cx.bufs(8, "wg")
        b_wu = cx.bufs(8, "wu")

        def wload3(ph, after):
            for kc in range(8):
                ph.op("pool", (lambda kc: lambda e: e.dma_start(
                    out=wout[:, kc, :], in_=wout_d.ap()[kc * 128:(kc + 1) * 128, :], max_dma_last_dim=4096))(kc),
                    reads=([after] if kc == 0 else []), writes=[b_wout[kc]], dma=b_wout[kc])
            for kc in range(8):
                ph.op("pool", (lambda kc: lambda e: e.dma_start(
                    out=wg[:, kc, :], in_=wg_d.ap()[kc * 128:(kc + 1) * 128, :], max_dma_last_dim=4096))(kc),
                    writes=[b_wg[kc]], dma=b_wg[kc])
                ph.op("pool", (lambda kc: lambda e: e.dma_start(
                    out=wu[:, kc, :], in_=wu_d.ap()[kc * 128:(kc + 1) * 128, :], max_dma_last_dim=4096))(kc),
                    writes=[b_wu[kc]], dma=b_wu[kc])
        phase3(wload3)
        if LAST_PHASE >= 4:
            wd = wst_o.enter_context(nc.sbuf_tensor("w_wd", [128, NFF, D], BF16))
            b_wd = cx.bufs(NFF, "wd")

            def wload4(ph, after):
                for f in range(NFF):
                    ph.op("pool", (lambda f: lambda e: e.dma_start(
                        out=wd[:, f, :], in_=wd_d.ap()[f * 128:(f + 1) * 128, :], max_dma_last_dim=4096))(f),
                        reads=([after] if f == 0 else []), writes=[b_wd[f]], dma=b_wd[f])
            phase4a(wout, b_wout, wload4)
            if LAST_PHASE >= 5:
                phase4b(wg, wu, wd, b_wg, b_wu, b_wd)
        wst_o.close()
        wst.close()
    return nc, es


def _host_consts():
    c = np.zeros((128, NCONST), np.float32)
    j = np.arange(128)[:, None]
    i = np.arange(128)[None, :]
    c[:, K_ID:K_ID + 128] = (j == i)
    c[:, K_TRIF:K_TRIF + 128] = (j <= i) * (-1.0 / 16.0)
    c[:, K_TRIB:K_TRIB + 128] = (j >= i) * (-1.0 / 16.0)
    c[:, K_SUF:K_SUF + 128] = (j > i) * (-1.0 / 16.0)
    c[:, K_SUB:K_SUB + 128] = (j < i) * (-1.0 / 16.0)
    c[:, K_MF:K_MF + 128] = (j <= i)
    c[:, K_MB:K_MB + 128] = (j >= i)
    return c


def _host_na_table(rpb):
    rpb = np.asarray(rpb, np.float32).reshape(8, 15, 31)
    ext = np.concatenate([rpb.reshape(8, -1), np.full((8, 1), NEG, np.float32)], axis=1)
    SENT = 15 * 31
    cols = np.arange(64)
    col_start = np.clip(cols - 8, 0, 48)
    kc = cols[:, None]
    qc = cols[None, :]
    colvalid = (kc >= col_start[None, :]) & (kc < col_start[None, :] + 16)
    dc = np.clip(kc - qc, -15, 15) + 15
    cfgs = [(-2, True), (-1, False), (0, False), (1, False), (2, True),
            (-3, False), (-2, False), (-1, False), (0, False), (1, False), (2, False), (3, False)]
    idx = np.full((NCFG, 128, 128), SENT, np.int64)
    for ci, (d, interior) in enumerate(cfgs):
        for rl in range(2):
            for il in range(2):
                rel = 2 * d + rl - il
                dr = rel + 7
                if dr < 0 or dr > 14:
                    continue
                if interior and not (-4 <= rel <= 3):
                    continue
                blk = np.where(colvalid, dr * 31 + dc, SENT)
                idx[ci, rl * 64:(rl + 1) * 64, il * 64:(il + 1) * 64] = blk
    return np.ascontiguousarray(ext[:, idx])


def kernel(x, meta_tokens, norm_mix_gain, w_in, rpb, meta_bias, w_gate_up_fwd, b_gate_fwd,
           w_gate_up_bwd, b_gate_bwd, gla_norm_gain, w_out, norm_ffn_gain, w_ffn_gate,
           w_ffn_up, w_ffn_down, norm_final_gain):
    f = lambda a: np.ascontiguousarray(np.asarray(a, np.float32))
    nc, es = build_program()
    shared = {
        "meta_tokens": f(meta_tokens), "norm_mix_gain": f(norm_mix_gain).reshape(1, D),
        "w_in": f(w_in).reshape(D, IN_W), "na_tbl": _host_na_table(rpb),
        "meta_bias": f(meta_bias).reshape(8, 16),
        "w_gate_up_fwd": f(w_gate_up_fwd).reshape(16, 256), "b_gate_fwd": f(b_gate_fwd).reshape(1, 256),
        "w_gate_up_bwd": f(w_gate_up_bwd).reshape(16, 256), "b_gate_bwd": f(b_gate_bwd).reshape(1, 256),
        "gla_norm_gain": f(gla_norm_gain).reshape(1, 128), "w_out": f(w_out).reshape(D, D),
        "norm_ffn_gain": f(norm_ffn_gain).reshape(1, D), "w_ffn_gate": f(w_ffn_gate).reshape(D, DFF),
        "w_ffn_up": f(w_ffn_up).reshape(D, DFF), "w_ffn_down": f(w_ffn_down).reshape(DFF, D),
        "norm_final_gain": f(norm_final_gain).reshape(1, D), "consts": _host_consts(),
    }
    xs = f(x)
    in_maps = []
    for c in range(8):
        m = dict(shared)
        m["x"] = xs[2 * c:2 * c + 2]
        in_maps.append(m)
    res = run_bass_kernel_spmd(nc, in_maps, core_ids=list(range(8)))
    es.close()
    kernel.last_results = res.results
    return np.concatenate([r["out"] for r in res.results], axis=0)
```

```python
import numpy as np
from contextlib import ExitStack
import concourse.bass as bass
import concourse.mybir as mybir
from concourse.bass_utils import run_bass_kernel_spmd

F32 = mybir.dt.float32
BF16 = mybir.dt.bfloat16
AF = mybir.ActivationFunctionType
ALU = mybir.AluOpType
AX = mybir.AxisListType

DEBUG = False
LAST_PHASE = 99
P2_CUT = 0

D = 1024
SEQ = 2048
NT = 16
NTILES = 33
META_T = 32
IN_W = 3104
DFF = 2816
NFF = 22
EPS = 1e-6
NEG = -30000.0

C_NAQ, C_NAK, C_NAV = 0, 512, 1024
C_GQ, C_GK, C_GV, C_GR, C_GF, C_GB = 1536, 1792, 2048, 2560, 3072, 3088

K_ID, K_TRIF, K_TRIB, K_SUF, K_SUB, K_MF, K_MB = [128 * i for i in range(7)]
NCONST = 128 * 7

NCFG = 12


class Buf:
    __slots__ = ("name", "last_w", "readers", "sem", "cnt", "kind", "sub")

    def __init__(self, name):
        self.name = name
        self.last_w = None
        self.readers = []
        self.sem = None
        self.cnt = 0
        self.kind = None
        self.sub = None


class Ctx:
    def __init__(self, nc, es):
        self.nc = nc
        self.es = es
        self.eng_sem = {}
        self.eng_cnt = {}
        for e in ("pe", "dve", "act", "pool", "sp"):
            self.eng_sem[e] = es.enter_context(nc.semaphore("es_" + e))
            self.eng_cnt[e] = 0
        self.dma_bufs = []
        self.sem_pool = {"sw": [], "hw": []}
        self.nsem = 0
        self.nbuf = 0

    def buf(self, name=None):
        self.nbuf += 1
        return Buf(name or "b%d" % self.nbuf)

    def bufs(self, n, name=None):
        return [self.buf((name or "b") + str(i)) for i in range(n)]


class Ph:
    def __init__(self, cx, name):
        self.cx = cx
        self.name = name
        self.ops = {e: [] for e in ("pe", "dve", "act", "pool", "sp")}
        self.waited = {e: {} for e in self.ops}
        self.touched = []

    def _add_dep(self, deps, d, eng, raw, waw=False):
        if d is None:
            return
        sem, val, deng, is_dma = d
        if deng == eng and not is_dma:
            if eng == "pe":
                return
            if not raw and eng != "pool" and not waw:
                return
        deps.append((sem, val))

    def op(self, eng, fn, reads=(), writes=(), dma=None):
        cx = self.cx
        deps = []
        for b in reads:
            self._add_dep(deps, b.last_w, eng, True)
        for b in writes:
            self._add_dep(deps, b.last_w, eng, False, True)
            for r in b.readers:
                self._add_dep(deps, r, eng, False)
        if dma is not None:
            kind = "sw" if eng == "pool" else "hw"
            if dma.sub is None:
                dma.sub = {}
            if kind not in dma.sub:
                dma.sub[kind] = Buf(dma.name + "_" + kind)
            dma = dma.sub[kind]
            if dma.sem is None:
                if cx.sem_pool[kind]:
                    dma.sem, dma.cnt = cx.sem_pool[kind].pop()
                else:
                    cx.nsem += 1
                    dma.sem = cx.es.enter_context(cx.nc.semaphore("ds%d" % cx.nsem))
                    dma.cnt = 0
                dma.kind = kind
                cx.dma_bufs.append(dma)
            assert dma.kind == kind, dma.name
            dma.cnt += 16
            sig = (dma.sem, dma.cnt, eng, True)
            inc = (dma.sem, 16)
        else:
            cx.eng_cnt[eng] += 1
            sig = (cx.eng_sem[eng], cx.eng_cnt[eng], eng, False)
            inc = (cx.eng_sem[eng], 1)
        w = self.waited[eng]
        waits = []
        best = {}
        for sem, val in deps:
            k = id(sem)
            if w.get(k, 0) >= val:
                continue
            if k not in best or best[k][1] < val:
                best[k] = (sem, val)
        for k, (sem, val) in best.items():
            w[k] = val
            waits.append((sem, val))
        self.ops[eng].append((waits, fn, inc))
        for b in reads:
            b.readers.append(sig)
            self.touched.append(b)
        for b in writes:
            b.last_w = sig
            b.readers = []
            self.touched.append(b)
        return sig

    def run(self):
        cx = self.cx
        nc = cx.nc
        fin = [(b.sem, b.cnt) for b in cx.dma_bufs]

        def emit(engname):
            lst = self.ops[engname]

            def body(e):
                for waits, fn, inc in lst:
                    for sem, val in waits:
                        e.wait_ge(sem, val)
                    ins = fn(e)
                    ins.then_inc(inc[0], inc[1])
                if engname == "sp":
                    for sem, val in fin:
                        e.wait_ge(sem, val)
            return body

        with nc.Block() as blk:
            blk.sync(emit("sp"))
            blk.tensor(emit("pe"))
            blk.vector(emit("dve"))
            blk.scalar(emit("act"))
            blk.gpsimd(emit("pool"))
        for b in self.touched:
            b.last_w = None
            b.readers = []
        for b in cx.dma_bufs:
            cx.sem_pool[b.kind].append((b.sem, b.cnt))
            b.sem = None
        cx.dma_bufs = []


def _ap(t, off, pat):
    return bass.AP(tensor=t, offset=off, ap=[list(p) for p in pat])


def build_program():
    nc = bass.Bass("TRN2", target_bir_lowering=False)
    es = ExitStack()
    cx = Ctx(nc, es)
    skind = "ExternalOutput" if DEBUG else "Internal"

    def din(name, shape, dt=F32):
        return nc.dram_tensor(name, list(shape), dt, kind="ExternalInput")

    def dscr(name, shape, dt=BF16):
        return nc.dram_tensor(name, list(shape), dt, kind=skind)

    x_d = din("x", [2, SEQ, D])
    meta_d = din("meta_tokens", [16, D])
    gmix_d = din("norm_mix_gain", [1, D])
    win_d = din("w_in", [D, IN_W])
    tbl_d = din("na_tbl", [8, NCFG, 128, 128])
    mb_d = din("meta_bias", [8, 16])
    wgf_d = din("w_gate_up_fwd", [16, 256])
    bgf_d = din("b_gate_fwd", [1, 256])
    wgb_d = din("w_gate_up_bwd", [16, 256])
    bgb_d = din("b_gate_bwd", [1, 256])
    ggla_d = din("gla_norm_gain", [1, 128])
    wout_d = din("w_out", [D, D])
    gffn_d = din("norm_ffn_gain", [1, D])
    wg_d = din("w_ffn_gate", [D, DFF])
    wu_d = din("w_ffn_up", [D, DFF])
    wd_d = din("w_ffn_down", [DFF, D])
    gfin_d = din("norm_final_gain", [1, D])
    consts_d = din("consts", [128, NCONST])
    out_d = nc.dram_tensor("out", [2, SEQ, D], F32, kind="ExternalOutput")

    QT_d = dscr("s_qt", [8, 128, SEQ])
    KT_d = dscr("s_kt", [8, 128, SEQ])
    KTM_d = dscr("s_ktm", [4, 128, 128])
    VA_d = dscr("s_va", [NTILES, 128, 520])
    GQ_d = dscr("s_gq", [8, 128, SEQ])
    GK_d = dscr("s_gk", [8, 128, SEQ])
    GKD_d = dscr("s_gkd", [NTILES, 128, 512])
    GV_d = dscr("s_gv", [NTILES, 128, 512])
    GR_d = dscr("s_gr", [NTILES, 128, 512])
    MIXT_d = dscr("s_mixt", [2, 8, 128, SEQ])
    H1_d = dscr("s_h1", [32, 128, D], F32)
    HNT_d = dscr("s_hnt", [8, 128, 8 * 512])

    def sb(name, shape, dt):
        return es.enter_context(nc.sbuf_tensor("sb_" + name, list(shape), dt))

    consts = sb("consts", [128, NCONST], F32)
    ident_bf = sb("ident_bf", [128, 128], BF16)
    maskF = sb("maskF", [128, 128], BF16)
    maskB = sb("maskB", [128, 128], BF16)
    dec = sb("dec", [128, NTILES, 4], F32)
    b_consts = cx.buf("consts")
    b_cbf = cx.buf("cbf")
    b_dec = cx.buf("dec")

    def phase1():
        st = ExitStack()

        def sbt(name, shape, dt):
            return st.enter_context(nc.sbuf_tensor("p1_" + name, list(shape), dt))

        def pst(name, shape, dt):
            return st.enter_context(nc.psum_tensor("p1_" + name, list(shape), dt))

        ph = Ph(cx, "p1")
        win = sbt("win", [128, 8, IN_W], BF16)
        WBLK = 776
        b_win = cx.bufs(4, "win")

        def wb(c0, n):
            return [b_win[b] for b in range(4) if b * WBLK < c0 + n and (b + 1) * WBLK > c0]
        w2 = sbt("w2", [64, 512], BF16)
        w2f = sbt("w2f", [64, 512], F32)
        b_w2f = cx.buf("w2f")
        b_w2 = cx.buf("w2")
        gmix = sbt("gmix", [128, D], F32)
        b_gmix = cx.buf("gmix")
        xt = [sbt("xt%d" % i, [128, D], F32) for i in range(3)]
        b_xt = cx.bufs(3, "xt")
        junk = sbt("junk", [128, D], BF16)
        b_junk = cx.buf("junk")
        ss = sbt("ss", [128, 8], F32)
        b_ss = cx.bufs(4, "ss")
        xn = [sbt("xn%d" % i, [128, D], BF16) for i in range(4)]
        b_xn = cx.bufs(4, "xn")
        xnT = [sbt("xnT%d" % i, [128, 8, 512], BF16) for i in range(2)]
        b_xnT = [cx.bufs(4, "xnT%d_" % i) for i in range(2)]
        glT = [sbt("glT%d" % i, [64, 512], BF16) for i in range(2)]
        b_glT = cx.bufs(2, "glT")
        stQK = [sbt("stQK%d" % i, [128, 8, 512], BF16) for i in range(2)]
        b_stQK = cx.bufs(2, "stQK")
        gqk = [sbt("gqk%d" % i, [128, 4, 512], F32) for i in range(2)]
        b_gqk = cx.bufs(2, "gqk")
        stG = [sbt("stG%d" % i, [128, 8, 512], BF16) for i in range(2)]
        b_stG = cx.bufs(2, "stG")
        stVA = [sbt("stVA%d" % i, [128, 520], BF16) for i in range(2)]
        b_stVA = cx.bufs(2, "stVA")
        stGV = [sbt("stGV%d" % i, [128, 512], BF16) for i in range(2)]
        b_stGV = cx.bufs(2, "stGV")
        rraw = [sbt("rraw%d" % i, [128, 512], BF16) for i in range(2)]
        b_rraw = cx.bufs(2, "rraw")
        stGR = [sbt("stGR%d" % i, [128, 512], BF16) for i in range(2)]
        b_stGR = cx.bufs(2, "stGR")
        ktok = [sbt("ktok%d" % i, [128, 256], F32) for i in range(2)]
        b_ktok = cx.bufs(2, "ktok")
        stKD = [sbt("stKD%d" % i, [128, 512], BF16) for i in range(2)]
        b_stKD = cx.bufs(2, "stKD")
        ge = [sbt("ge%d" % i, [128, 512], F32) for i in range(2)]
        b_ge = cx.bufs(2, "ge")
        gp = [sbt("gp%d" % i, [128, 512], F32) for i in range(2)]
        b_gp = cx.bufs(2, "gp")
        Eq = [sbt("Eq%d" % i, [128, 4, 128], F32) for i in range(2)]
        b_Eq = cx.bufs(2, "Eq")
        Ek = [sbt("Ek%d" % i, [128, 4, 128], F32) for i in range(2)]
        b_Ek = cx.bufs(2, "Ek")
        Ed = [sbt("Ed%d" % i, [128, 512], F32) for i in range(2)]
        b_Ed = cx.bufs(2, "Ed")

        psT = [pst("psT%d" % i, [128, D], BF16) for i in range(2)]
        b_psT = cx.bufs(2, "psT")
        psF = [pst("psF%d" % i, [128, 512], F32) for i in range(4)]
        b_psF = cx.bufs(4, "psF")
        psK = psF
        b_psK = b_psF
        psB = pst("psB", [128, 4, 128], F32)
        b_psB = cx.buf("psB")
        psS = pst("psS", [128, 512], F32)
        b_psS = cx.buf("psS")

        ph.op("sp", lambda e: e.dma_start(out=consts[:, :], in_=consts_d.ap()),
              writes=[b_consts], dma=b_consts)
        ph.op("sp", lambda e: e.dma_start(out=gmix[:, :], in_=gmix_d.ap().broadcast_to([128, D])),
              writes=[b_gmix], dma=b_gmix)
        ph.op("dve", lambda e: e.tensor_copy(out=ident_bf[:, :], in_=consts[:, K_ID:K_ID + 128]),
              reads=[b_consts], writes=[b_cbf])
        ph.op("dve", lambda e: e.tensor_copy(out=maskF[:, :], in_=consts[:, K_MF:K_MF + 128]),
              reads=[b_consts], writes=[b_cbf])
        ph.op("dve", lambda e: e.tensor_copy(out=maskB[:, :], in_=consts[:, K_MB:K_MB + 128]),
              reads=[b_consts], writes=[b_cbf])
        ph.op("pool", lambda e: e.memset(w2f[:, :], 0.0), writes=[b_w2f])
        ph.op("sp", lambda e: e.dma_start(out=w2f[0:16, 0:256], in_=wgf_d.ap()), writes=[b_w2f], dma=b_w2f)
        ph.op("sp", lambda e: e.dma_start(out=w2f[16:32, 256:512], in_=wgb_d.ap()), writes=[b_w2f], dma=b_w2f)
        ph.op("sp", lambda e: e.dma_start(out=w2f[32:33, 0:256], in_=bgf_d.ap()), writes=[b_w2f], dma=b_w2f)
        ph.op("sp", lambda e: e.dma_start(out=w2f[32:33, 256:512], in_=bgb_d.ap()), writes=[b_w2f], dma=b_w2f)
        ph.op("dve", lambda e: e.tensor_copy(out=w2[:, :], in_=w2f[:, :]), reads=[b_w2f], writes=[b_w2])
        for i in range(2):
            ph.op("pool", (lambda i: lambda e: e.memset(glT[i][32:64, :], 1.0))(i), writes=[b_glT[i]])
            ph.op("pool", (lambda i: lambda e: e.memset(stVA[i][:, :], 1.0))(i), writes=[b_stVA[i]])
        for b in range(4):
            ph.op("pool", (lambda b: lambda e: e.dma_start(
                out=win[:, :, b * WBLK:(b + 1) * WBLK],
                in_=win_d.ap()[:, b * WBLK:(b + 1) * WBLK].rearrange("(k p) c -> p k c", p=128),
                max_dma_last_dim=4096))(b),
                writes=[b_win[b]], dma=b_win[b])

        state = {"xi": 0}

        def stageA1(g):
            tiles = [META_T] if g == 8 else [g * 4 + j for j in range(4)]
            for j, T in enumerate(tiles):
                xi = state["xi"] % 3
                state["xi"] += 1
                ti = j
                if T == META_T:
                    ph.op("pool", (lambda xi: lambda e: e.memset(xt[xi][:, :], 0.0))(xi), writes=[b_xt[xi]])
                    ph.op("sp", (lambda xi: lambda e: e.dma_start(out=xt[xi][112:128, :], in_=meta_d.ap()))(xi),
                          writes=[b_xt[xi]], dma=b_xt[xi])
                else:
                    s_, t_ = T // 16, T % 16
                    ph.op("sp", (lambda xi, s_, t_: lambda e: e.dma_start(
                        out=xt[xi][:, :], in_=x_d.ap()[s_, t_ * 128:(t_ + 1) * 128, :]))(xi, s_, t_),
                        writes=[b_xt[xi]], dma=b_xt[xi])
                ph.op("dve", (lambda xi, ti: lambda e: e.scalar_tensor_tensor(
                    out=junk[:, :], in0=xt[xi][:, :], scalar=1.0, in1=xt[xi][:, :],
                    op0=ALU.mult, op1=ALU.mult, accum_out=ss[:, ti * 2:ti * 2 + 1]))(xi, ti),
                    reads=[b_xt[xi]], writes=[b_junk, b_ss[ti]])
                ph.op("act", (lambda ti: lambda e: e.activation(
                    out=ss[:, ti * 2 + 1:ti * 2 + 2], in_=ss[:, ti * 2:ti * 2 + 1], func=AF.Ln,
                    scale=1.0 / D, bias=EPS))(ti), reads=[b_ss[ti]], writes=[b_ss[ti]])
                ph.op("act", (lambda ti: lambda e: e.activation(
                    out=ss[:, ti * 2 + 1:ti * 2 + 2], in_=ss[:, ti * 2 + 1:ti * 2 + 2], func=AF.Exp,
                    scale=-0.5))(ti), reads=[b_ss[ti]], writes=[b_ss[ti]])
                ph.op("dve", (lambda xi, ti: lambda e: e.scalar_tensor_tensor(
                    out=xn[ti][:, :], in0=xt[xi][:, :], scalar=ss[:, ti * 2 + 1:ti * 2 + 2], in1=gmix[:, :],
                    op0=ALU.mult, op1=ALU.mult))(xi, ti),
                    reads=[b_xt[xi], b_ss[ti], b_gmix], writes=[b_xn[ti]])

        def stageA2(g, gs):
            tiles = [META_T] if g == 8 else [g * 4 + j for j in range(4)]
            for j, T in enumerate(tiles):
                ti = j
                pt = j % 2

                def tr(e, ti=ti, pt=pt):
                    ins = None
                    for kc in range(8):
                        ins = e.transpose(out=psT[pt][:, kc * 128:(kc + 1) * 128],
                                          in_=xn[ti][:, kc * 128:(kc + 1) * 128], identity=ident_bf[:, :])
                    return ins
                ph.op("pe", tr, reads=[b_xn[ti], b_cbf], writes=[b_psT[pt]])
                ph.op("act", (lambda pt, gs, j: lambda e: e.activation(
                    out=xnT[gs][:, :, j * 128:(j + 1) * 128],
                    in_=psT[pt][:, :].rearrange("p (k t) -> p k t", k=8), func=AF.Copy))(pt, gs, j),
                    reads=[b_psT[pt]], writes=[b_xnT[gs][j]])

        fcount = {"f": 0, "k": 0}

        def stageB(g, gs):
            tiles = [META_T] if g == 8 else [g * 4 + j for j in range(4)]
            nt = len(tiles)
            N = nt * 128
            xb = b_xnT[gs][:nt]
            s = g // 4
            tok0 = (g % 4) * 512
            fm = [(C_NAQ + 128 * i, 128, "q", i) for i in range(4)] + \
                 [(C_NAK + 128 * i, 128, "k", i) for i in range(4)] + \
                 [(C_GQ + 128 * i, 128, "gq", i) for i in range(2)] + \
                 [(C_GK + 128 * i, 128, "gk", i) for i in range(2)] + \
                 [(C_GF, 32, "gl", 0)]
            for (c0, M, kind, i) in fm:
                pi = fcount["f"] % 4
                fcount["f"] += 1

                def mm(e, c0=c0, M=M, pi=pi):
                    ins = None
                    for kc in range(8):
                        ins = e.matmul(out=psF[pi][0:M, 0:N], lhsT=win[:, kc, c0:c0 + M],
                                       rhs=xnT[gs][:, kc, 0:N], start=(kc == 0), stop=(kc == 7))
                    return ins
                ph.op("pe", mm, reads=wb(c0, M) + xb, writes=[b_psF[pi]])
                if kind == "q":
                    ph.op("act", (lambda pi, i: lambda e: e.activation(
                        out=stQK[gs][:, i, 0:N], in_=psF[pi][:, 0:N], func=AF.Copy, scale=0.125))(pi, i),
                        reads=[b_psF[pi]], writes=[b_stQK[gs]])
                elif kind == "k":
                    ph.op("dve", (lambda pi, i: lambda e: e.tensor_copy(
                        out=stQK[gs][:, 4 + i, 0:N], in_=psF[pi][:, 0:N]))(pi, i),
                        reads=[b_psF[pi]], writes=[b_stQK[gs]])
                elif kind == "gq":
                    ph.op("act", (lambda pi, i: lambda e: e.activation(
                        out=gqk[gs][:, i, 0:N], in_=psF[pi][:, 0:N], func=AF.Copy, scale=0.125))(pi, i),
                        reads=[b_psF[pi]], writes=[b_gqk[gs]])
                elif kind == "gk":
                    ph.op("dve", (lambda pi, i: lambda e: e.tensor_copy(
                        out=gqk[gs][:, 2 + i, 0:N], in_=psF[pi][:, 0:N]))(pi, i),
                        reads=[b_psF[pi]], writes=[b_gqk[gs]])
                else:
                    ph.op("dve", (lambda pi: lambda e: e.tensor_copy(
                        out=glT[gs][0:32, 0:N], in_=psF[pi][0:32, 0:N]))(pi),
                        reads=[b_psF[pi]], writes=[b_glT[gs]])
            if g == 8:
                ph.op("pool", lambda e: e.dma_start(
                    out=KTM_d.ap().rearrange("c p t -> p c t"), in_=stQK[gs][:, 4:8, 0:128]),
                    reads=[b_stQK[gs]], dma=b_stQK[gs])
            else:
                ph.op("pool", lambda e: e.dma_start(
                    out=QT_d.ap()[s * 4:(s + 1) * 4, :, tok0:tok0 + 512].rearrange("c p t -> p c t"),
                    in_=stQK[gs][:, 0:4, :]), reads=[b_stQK[gs]], dma=b_stQK[gs])
                ph.op("pool", lambda e: e.dma_start(
                    out=KT_d.ap()[s * 4:(s + 1) * 4, :, tok0:tok0 + 512].rearrange("c p t -> p c t"),
                    in_=stQK[gs][:, 4:8, :]), reads=[b_stQK[gs]], dma=b_stQK[gs])

            def tile_ops(j, T):
                k2 = fcount["k"] % 2
                fcount["k"] += 1
                tsl = slice(j * 128, (j + 1) * 128)

                def tokmm(e, c0, n, pi, j=j):
                    ins = None
                    for kc in range(8):
                        ins = e.matmul(out=psK[pi][:, 0:n], lhsT=xnT[gs][:, kc, j * 128:(j + 1) * 128],
                                       rhs=win[:, kc, c0:c0 + n], start=(kc == 0), stop=(kc == 7))
                    return ins
                pi = fcount["f"] % 4; fcount["f"] += 1
                ph.op("pe", (lambda pi, tk: lambda e: tk(e, C_NAV, 512, pi))(pi, tokmm),
                      reads=wb(C_NAV, 512) + [xb[j]], writes=[b_psK[pi]])
                ph.op("act", (lambda pi, k2: lambda e: e.activation(
                    out=stVA[k2][:, :].rearrange("p (h c) -> p h c", h=8)[:, :, 0:64],
                    in_=psK[pi][:, :].rearrange("p (h c) -> p h c", h=8), func=AF.Copy))(pi, k2),
                    reads=[b_psK[pi]], writes=[b_stVA[k2]])
                ph.op("pool", (lambda k2, T: lambda e: e.dma_start(out=VA_d.ap()[T], in_=stVA[k2][:, :]))(k2, T),
                      reads=[b_stVA[k2]], dma=b_stVA[k2])
                pi = fcount["f"] % 4; fcount["f"] += 1
                ph.op("pe", (lambda pi, tk: lambda e: tk(e, C_GV, 512, pi))(pi, tokmm),
                      reads=wb(C_GV, 512) + [xb[j]], writes=[b_psK[pi]])
                ph.op("dve", (lambda pi, k2: lambda e: e.tensor_copy(out=stGV[k2][:, :], in_=psK[pi][:, :]))(pi, k2),
                      reads=[b_psK[pi]], writes=[b_stGV[k2]])
                ph.op("pool", (lambda k2, T: lambda e: e.dma_start(out=GV_d.ap()[T], in_=stGV[k2][:, :]))(k2, T),
                      reads=[b_stGV[k2]], dma=b_stGV[k2])
                pi = fcount["f"] % 4; fcount["f"] += 1
                ph.op("pe", (lambda pi, tk: lambda e: tk(e, C_GR, 512, pi))(pi, tokmm),
                      reads=wb(C_GR, 512) + [xb[j]], writes=[b_psK[pi]])
                ph.op("act", (lambda pi, k2: lambda e: e.activation(
                    out=ge[k2][:, :], in_=psK[pi][:, :], func=AF.Exp, scale=-1.0))(pi, k2),
                    reads=[b_psK[pi]], writes=[b_ge[k2]])
                ph.op("act", (lambda pi, k2: lambda e: e.activation(
                    out=rraw[k2][:, :], in_=psK[pi][:, :], func=AF.Copy))(pi, k2),
                    reads=[b_psK[pi]], writes=[b_rraw[k2]])
                ph.op("act", (lambda k2: lambda e: e.activation(
                    out=ge[k2][:, :], in_=ge[k2][:, :], func=AF.Ln, bias=1.0))(k2),
                    reads=[b_ge[k2]], writes=[b_ge[k2]])
                ph.op("act", (lambda k2: lambda e: e.activation(
                    out=ge[k2][:, :], in_=ge[k2][:, :], func=AF.Exp, scale=-1.0))(k2),
                    reads=[b_ge[k2]], writes=[b_ge[k2]])
                ph.op("dve", (lambda k2: lambda e: e.tensor_tensor(
                    out=stGR[k2][:, :], in0=rraw[k2][:, :], in1=ge[k2][:, :], op=ALU.mult))(k2),
                    reads=[b_rraw[k2], b_ge[k2]], writes=[b_stGR[k2]])
                ph.op("pool", (lambda k2, T: lambda e: e.dma_start(out=GR_d.ap()[T], in_=stGR[k2][:, :]))(k2, T),
                      reads=[b_stGR[k2]], dma=b_stGR[k2])
                pi = fcount["f"] % 4; fcount["f"] += 1
                ph.op("pe", (lambda pi, tk: lambda e: tk(e, C_GK, 256, pi))(pi, tokmm),
                      reads=wb(C_GK, 256) + [xb[j]], writes=[b_psK[pi]])
                ph.op("act", (lambda pi, k2: lambda e: e.activation(
                    out=ktok[k2][:, :], in_=psK[pi][:, 0:256], func=AF.Copy))(pi, k2),
                    reads=[b_psK[pi]], writes=[b_ktok[k2]])
                pi = fcount["f"] % 4; fcount["f"] += 1
                ph.op("pe", (lambda pi, j: lambda e: e.matmul(
                    out=psK[pi][:, :], lhsT=glT[gs][:, j * 128:(j + 1) * 128], rhs=w2[:, :],
                    start=True, stop=True))(pi, j),
                    reads=[b_glT[gs], b_w2], writes=[b_psK[pi]])
                ph.op("act", (lambda pi, k2: lambda e: e.activation(
                    out=gp[k2][:, :], in_=psK[pi][:, :], func=AF.Exp, scale=-1.0))(pi, k2),
                    reads=[b_psK[pi]], writes=[b_gp[k2]])
                ph.op("act", (lambda k2: lambda e: e.activation(
                    out=gp[k2][:, :], in_=gp[k2][:, :], func=AF.Ln, bias=1.0))(k2),
                    reads=[b_gp[k2]], writes=[b_gp[k2]])

                yield
                def cum(e, k2=k2):
                    ins = None
                    for d_ in range(2):
                        tri = consts[:, K_TRIF:K_TRIF + 128] if d_ == 0 else consts[:, K_TRIB:K_TRIB + 128]
                        for hp in range(2):
                            c0 = d_ * 256 + hp * 128
                            ins = e.matmul(out=psB[:, d_ * 2 + hp, :], lhsT=gp[k2][:, c0:c0 + 128], rhs=tri,
                                           start=True, stop=True)
                    for d_ in range(2):
                        su = consts[:, K_SUF:K_SUF + 128] if d_ == 0 else consts[:, K_SUB:K_SUB + 128]
                        ins = e.matmul(out=psS[:, d_ * 256:(d_ + 1) * 256], lhsT=su,
                                       rhs=gp[k2][:, d_ * 256:(d_ + 1) * 256], start=True, stop=True)
                    return ins
                ph.op("pe", cum, reads=[b_gp[k2], b_consts], writes=[b_psB, b_psS])
                ph.op("act", (lambda k2: lambda e: e.activation(
                    out=Eq[k2][:, :, :], in_=psB[:, :, :], func=AF.Exp))(k2),
                    reads=[b_psB], writes=[b_Eq[k2]])
                ph.op("act", (lambda k2: lambda e: e.activation(
                    out=Ek[k2][:, :, :], in_=psB[:, :, :], func=AF.Exp, scale=-1.0))(k2),
                    reads=[b_psB], writes=[b_Ek[k2]])
                ph.op("act", (lambda k2: lambda e: e.activation(
                    out=Ed[k2][:, :], in_=psS[:, :], func=AF.Exp))(k2),
                    reads=[b_psS], writes=[b_Ed[k2]])
                ph.op("pool", (lambda k2, T: lambda e: e.tensor_copy(
                    out=dec[:, T, 0:2], in_=Eq[k2][:, 0:2, 127]))(k2, T),
                    reads=[b_Eq[k2]], writes=[b_dec])
                ph.op("pool", (lambda k2, T: lambda e: e.tensor_copy(
                    out=dec[:, T, 2:4], in_=Eq[k2][:, 2:4, 0]))(k2, T),
                    reads=[b_Eq[k2]], writes=[b_dec])
                for d_ in range(2):
                    ph.op("dve", (lambda k2, d_: lambda e: e.tensor_tensor(
                        out=stKD[k2][:, d_ * 256:(d_ + 1) * 256], in0=ktok[k2][:, :],
                        in1=Ed[k2][:, d_ * 256:(d_ + 1) * 256], op=ALU.mult))(k2, d_),
                        reads=[b_ktok[k2], b_Ed[k2]], writes=[b_stKD[k2]])
                ph.op("pool", (lambda k2, T: lambda e: e.dma_start(out=GKD_d.ap()[T], in_=stKD[k2][:, :]))(k2, T),
                      reads=[b_stKD[k2]], dma=b_stKD[k2])
                if T != META_T:
                    for d_ in range(2):
                        ph.op("dve", (lambda k2, d_, tsl: lambda e: e.tensor_tensor(
                            out=stG[gs][:, d_ * 2:d_ * 2 + 2, tsl], in0=gqk[gs][:, 0:2, tsl],
                            in1=Eq[k2][:, d_ * 2:d_ * 2 + 2, :], op=ALU.mult))(k2, d_, tsl),
                            reads=[b_gqk[gs], b_Eq[k2]], writes=[b_stG[gs]])
                        ph.op("pool", (lambda k2, d_, tsl: lambda e: e.tensor_tensor(
                            out=stG[gs][:, 4 + d_ * 2:4 + d_ * 2 + 2, tsl], in0=gqk[gs][:, 2:4, tsl],
                            in1=Ek[k2][:, d_ * 2:d_ * 2 + 2, :], op=ALU.mult))(k2, d_, tsl),
                            reads=[b_gqk[gs], b_Ek[k2]], writes=[b_stG[gs]])
            gens = [tile_ops(j, T) for j, T in enumerate(tiles)]
            for j in range(len(gens)):
                next(gens[j])
                if j >= 1:
                    for _ in gens[j - 1]:
                        pass
            for _ in gens[-1]:
                pass
            if g != 8:
                ph.op("pool", lambda e: e.dma_start(
                    out=GQ_d.ap()[s * 4:(s + 1) * 4, :, tok0:tok0 + 512].rearrange("c p t -> p c t"),
                    in_=stG[gs][:, 0:4, :]), reads=[b_stG[gs]], dma=b_stG[gs])
                ph.op("pool", lambda e: e.dma_start(
                    out=GK_d.ap()[s * 4:(s + 1) * 4, :, tok0:tok0 + 512].rearrange("c p t -> p c t"),
                    in_=stG[gs][:, 4:8, :]), reads=[b_stG[gs]], dma=b_stG[gs])

        order = [8, 0, 1, 2, 3, 4, 5, 6, 7]
        stageA1(order[0])
        stageA2(order[0], 0)
        for i, g in enumerate(order):
            if i + 1 < len(order):
                stageA1(order[i + 1])
            stageB(g, i % 2)
            if i + 1 < len(order):
                stageA2(order[i + 1], (i + 1) % 2)
        ph.run()
        st.close()


    def rms_rstd(ph, eng_ss, src_ap_fn, b_src, ssbuf, b_ssb, col, junk_ap_fn, b_junkb, n):
        ph.op("dve", lambda e: e.scalar_tensor_tensor(
            out=junk_ap_fn(), in0=src_ap_fn(), scalar=1.0, in1=src_ap_fn(),
            op0=ALU.mult, op1=ALU.mult, accum_out=ssbuf[:, col:col + 1]),
            reads=[b_src], writes=[b_junkb, b_ssb])
        ph.op("act", lambda e: e.activation(
            out=ssbuf[:, col + 1:col + 2], in_=ssbuf[:, col:col + 1], func=AF.Ln, scale=1.0 / n, bias=EPS),
            reads=[b_ssb], writes=[b_ssb])
        ph.op("act", lambda e: e.activation(
            out=ssbuf[:, col + 1:col + 2], in_=ssbuf[:, col + 1:col + 2], func=AF.Exp, scale=-0.5),
            reads=[b_ssb], writes=[b_ssb])

    def phase2():
        st = ExitStack()

        def sbt(name, shape, dt):
            return st.enter_context(nc.sbuf_tensor("p2_" + name, list(shape), dt))

        def pst(name, shape, dt):
            return st.enter_context(nc.psum_tensor("p2_" + name, list(shape), dt))

        ph = Ph(cx, "p2")
        Sf = sbt("Sf", [128, 32, 2, 256], BF16)
        b_Sf = cx.bufs(32, "Sf")
        Sinit = sbt("Sinit", [128, 2, 256], F32)
        b_Sinit = cx.buf("Sinit")
        S32 = sbt("S32", [128, 2, 256], F32)
        b_S32 = cx.buf("S32")
        Sb32 = sbt("Sb32", [128, 2, 256], F32)
        b_Sb32 = cx.buf("Sb32")
        Sbb = [sbt("Sbb%d" % i, [128, 2, 256], BF16) for i in range(2)]
        b_Sbb = cx.bufs(2, "Sbb")
        ggla = sbt("ggla", [128, 128], F32)
        b_ggla = cx.buf("ggla")
        kd = [sbt("kd%d" % i, [128, 512], BF16) for i in range(3)]
        b_kd = cx.bufs(3, "kd")
        gv = [sbt("gv%d" % i, [128, 512], BF16) for i in range(3)]
        b_gv = cx.bufs(3, "gv")
        gr = [sbt("gr%d" % i, [128, 512], BF16) for i in range(3)]
        b_gr = cx.bufs(3, "gr")
        gqs = [sbt("gqs%d" % i, [128, 4, SEQ], BF16) for i in range(2)]
        b_gqs = cx.bufs(2, "gqs")
        gks = [sbt("gks%d" % i, [128, 4, SEQ], BF16) for i in range(2)]
        b_gks = cx.bufs(2, "gks")
        AT = [sbt("AT%d" % i, [128, 128], BF16) for i in range(4)]
        b_AT = cx.bufs(4, "AT")
        osb = [sbt("osb%d" % i, [128, 512], F32) for i in range(2)]
        b_osb = cx.bufs(2, "osb")
        jk = sbt("jk", [128, 512], BF16)
        b_jk = cx.bufs(4, "jk")
        ssg = [sbt("ssg%d" % i, [128, 8], F32) for i in range(2)]
        b_ssg = cx.bufs(2, "ssg")
        t2 = [sbt("t2_%d" % i, [128, 512], F32) for i in range(2)]
        b_t2 = cx.bufs(2, "t2")
        gout = [sbt("gout%d" % i, [128, 512], BF16) for i in range(2)]
        b_gout = cx.bufs(2, "gout")
        mst = [sbt("mst%d" % i, [128, 4, 128], BF16) for i in range(2)]
        b_mst = cx.bufs(2, "mst")

        psA = [pst("psA%d" % i, [128, 512], F32) for i in range(2)]
        b_psA = cx.bufs(2, "psA")
        psO = [pst("psO%d" % i, [128, 512], F32) for i in range(2)]
        b_psO = cx.bufs(2, "psO")
        psKVt = [pst("psKV%d" % i, [128, 512], F32) for i in range(2)]
        b_psKV = cx.bufs(2, "psKV")
        psT = pst("psT", [128, 1024], BF16)
        b_psT = cx.buf("psT")
        psKV1 = pst("psKVs", [128, 512], F32)
        b_psKV1 = cx.buf("psKV1")

        ph.op("sp", lambda e: e.dma_start(out=ggla[:, :], in_=ggla_d.ap().broadcast_to([128, 128])),
              writes=[b_ggla], dma=b_ggla)

        for s_ in range(2):
            ph.op("sp", (lambda s_: lambda e: e.dma_start(
                out=gqs[s_][:, :, :], in_=GQ_d.ap()[s_ * 4:(s_ + 1) * 4].rearrange("c p t -> p c t")))(s_),
                writes=[b_gqs[s_]], dma=b_gqs[s_])
            ph.op("sp", (lambda s_: lambda e: e.dma_start(
                out=gks[s_][:, :, :], in_=GK_d.ap()[s_ * 4:(s_ + 1) * 4].rearrange("c p t -> p c t")))(s_),
                writes=[b_gks[s_]], dma=b_gks[s_])
        cnt = {"ld": 0, "at": 0}
        kd1 = [sbt("kd1_%d" % i, [128, 512], BF16) for i in range(2)]
        b_kd1 = cx.bufs(2, "kd1")
        gv1 = [sbt("gv1_%d" % i, [128, 512], BF16) for i in range(2)]
        b_gv1 = cx.bufs(2, "gv1")

        def load_kv(T):
            i = cnt["ld"] % 2
            cnt["ld"] += 1
            ph.op("sp", lambda e: e.dma_start(out=kd1[i][:, :], in_=GKD_d.ap()[T]), writes=[b_kd1[i]], dma=b_kd1[i])
            ph.op("sp", lambda e: e.dma_start(out=gv1[i][:, :], in_=GV_d.ap()[T]), writes=[b_gv1[i]], dma=b_gv1[i])
            return i

        def kv_mm1(i):
            def f(e):
                ins = None
                for hp in range(2):
                    ins = e.matmul(out=psKV1[:, hp * 256:(hp + 1) * 256], lhsT=kd1[i][:, hp * 128:hp * 128 + 128],
                                   rhs=gv1[i][:, hp * 256:(hp + 1) * 256], start=True, stop=True)
                return ins
            return f

        def kv_mm(i, hp, d_):
            return lambda e: e.matmul(out=psKVt[hp][:, 0:256], lhsT=kd[i][:, d_ * 256 + hp * 128:d_ * 256 + hp * 128 + 128],
                                      rhs=gv[i][:, hp * 256:(hp + 1) * 256], start=True, stop=True)

        i = load_kv(META_T)
        ph.op("pe", kv_mm1(i), reads=[b_kd1[i], b_gv1[i]], writes=[b_psKV1])
        ph.op("act", lambda e: e.activation(
            out=Sinit[:, :, :], in_=psKV1[:, :].rearrange("p (h c) -> p h c", h=2), func=AF.Copy),
            reads=[b_psKV1], writes=[b_Sinit])

        S32pp = [S32, sbt("S32b", [128, 2, 256], F32)]
        b_S32pp = [b_S32, cx.buf("S32b")]

        def sweep1_step(s, t):
            T = s * 16 + t
            src, dst = S32pp[t % 2], S32pp[(t + 1) % 2]
            bsrc, bdst = b_S32pp[t % 2], b_S32pp[(t + 1) % 2]
            if t == 0:
                ph.op("pool", lambda e: e.tensor_copy(out=src[:, :, :], in_=Sinit[:, :, :]),
                      reads=[b_Sinit], writes=[bsrc])
            ph.op("act", lambda e: e.activation(out=Sf[:, T, :, :], in_=src[:, :, :], func=AF.Copy),
                  reads=[bsrc], writes=[b_Sf[T]])
            if t == 15:
                return
            i = load_kv(T)
            ph.op("pe", kv_mm1(i), reads=[b_kd1[i], b_gv1[i]], writes=[b_psKV1])
            for hp in range(2):
                ph.op("dve", (lambda hp: lambda e: e.scalar_tensor_tensor(
                    out=dst[:, hp, :], in0=src[:, hp, :], scalar=dec[:, T, hp:hp + 1],
                    in1=psKV1[:, hp * 256:(hp + 1) * 256], op0=ALU.mult, op1=ALU.add))(hp),
                    reads=[bsrc, b_psKV1, b_dec], writes=[bdst])

        for t in range(16):
            sweep1_step(0, t)
        tiles2 = [(s, t) for s in range(2) for t in range(15, -1, -1)]

        def part1(k):
            s, t = tiles2[k]
            T = s * 16 + t
            cur = k % 2
            nxt = 1 - cur
            if t == 15:
                ph.op("pool", lambda e: e.memset(Sb32[:, :, :], 0.0), writes=[b_Sb32])
                ph.op("pool", lambda e: e.memset(Sbb[cur][:, :, :], 0.0), writes=[b_Sbb[cur]])
            i = k % 3
            ph.op("sp", lambda e: e.dma_start(out=kd[i][:, :], in_=GKD_d.ap()[T]), writes=[b_kd[i]], dma=b_kd[i])
            ph.op("sp", lambda e: e.dma_start(out=gv[i][:, :], in_=GV_d.ap()[T]), writes=[b_gv[i]], dma=b_gv[i])
            ph.op("sp", lambda e: e.dma_start(out=gr[i][:, :], in_=GR_d.ap()[T]), writes=[b_gr[i]], dma=b_gr[i])
            ph.op("pool", lambda e: e.tensor_tensor(
                out=t2[k % 2][:, :].rearrange("p (h c) -> p h c", h=4),
                in0=gr[i][:, :].rearrange("p (h c) -> p h c", h=4),
                in1=_ap(ggla, 0, [[128, 128], [0, 4], [1, 128]]), op=ALU.mult),
                reads=[b_gr[i], b_ggla], writes=[b_t2[k % 2]])
            po = k % 2
            tsl = slice(t * 128, (t + 1) * 128)

            def rec_amm(h):
                hp, base = h // 2, 64 * (h % 2)
                pa = h % 2

                def amm(e):
                    ins = None
                    for d_ in range(2):
                        c = d_ * 2 + hp
                        ins = e.matmul(out=psA[pa][:, d_ * 128:(d_ + 1) * 128], lhsT=gks[s][base:base + 64, c, tsl],
                                       rhs=gqs[s][base:base + 64, c, tsl], start=True, stop=True)
                    return ins
                ph.op("pe", amm, reads=[b_gks[s], b_gqs[s]], writes=[b_psA[pa]])

            def rec_mask_omm(h):
                hp, base = h // 2, 64 * (h % 2)
                pa = h % 2
                for d_ in range(2):
                    ph.op("dve", (lambda d_: lambda e: e.tensor_tensor(
                        out=AT[pa * 2 + d_][:, :], in0=psA[pa][:, d_ * 128:(d_ + 1) * 128],
                        in1=(maskF if d_ == 0 else maskB)[:, :], op=ALU.mult))(d_),
                        reads=[b_psA[pa], b_cbf], writes=[b_AT[pa * 2 + d_]])

                def omm(e):
                    o = psO[po][:, h * 128:(h + 1) * 128]
                    vv = gv[i][:, h * 128:(h + 1) * 128]
                    sc = (h % 2) * 128
                    e.matmul(out=o, lhsT=AT[pa * 2][:, :], rhs=vv, start=True, stop=False)
                    e.matmul(out=o, lhsT=gqs[s][base:base + 64, hp, tsl],
                             rhs=Sf[base:base + 64, T, hp, sc:sc + 128], start=False, stop=False)
                    e.matmul(out=o, lhsT=AT[pa * 2 + 1][:, :], rhs=vv, start=False, stop=False)
                    return e.matmul(out=o, lhsT=gqs[s][base:base + 64, 2 + hp, tsl],
                                    rhs=Sbb[cur][base:base + 64, hp, sc:sc + 128], start=False, stop=True)
                ph.op("pe", omm, reads=[b_AT[pa * 2], b_AT[pa * 2 + 1], b_gv[i], b_gqs[s], b_Sf[T], b_Sbb[cur]],
                      writes=[b_psO[po]])

            if t > 0:
                for hp in range(2):
                    ph.op("pe", kv_mm(i, hp, 1), reads=[b_kd[i], b_gv[i]], writes=[b_psKV[hp]])
                    ph.op("dve", (lambda hp: lambda e: e.scalar_tensor_tensor(
                        out=Sb32[:, hp, :], in0=Sb32[:, hp, :], scalar=dec[:, T, 2 + hp:3 + hp], in1=psKVt[hp][:, 0:256],
                        op0=ALU.mult, op1=ALU.add))(hp),
                        reads=[b_Sb32, b_psKV[hp], b_dec], writes=[b_Sb32])
                ph.op("act", lambda e: e.activation(out=Sbb[nxt][:, :, :], in_=Sb32[:, :, :], func=AF.Copy),
                      reads=[b_Sb32], writes=[b_Sbb[nxt]])
            rec_amm(0)
            for h in range(4):
                if h + 1 < 4:
                    rec_amm(h + 1)
                rec_mask_omm(h)
        def part2(k):
            po = k % 2
            go = k % 2
            ob = k % 2
            ph.op("act", lambda e: e.activation(out=osb[ob][:, :], in_=psO[po][:, :], func=AF.Copy),
                  reads=[b_psO[po]], writes=[b_osb[ob]])
            for h in range(4):
                ph.op("dve", (lambda h: lambda e: e.scalar_tensor_tensor(
                    out=jk[:, h * 128:(h + 1) * 128], in0=osb[ob][:, h * 128:(h + 1) * 128], scalar=1.0,
                    in1=osb[ob][:, h * 128:(h + 1) * 128], op0=ALU.mult, op1=ALU.mult,
                    accum_out=ssg[ob][:, h:h + 1]))(h),
                    reads=[b_osb[ob]], writes=[b_jk[h], b_ssg[ob]])
            ph.op("act", lambda e: e.activation(out=ssg[ob][:, 4:8], in_=ssg[ob][:, 0:4], func=AF.Ln,
                                                scale=1.0 / 128, bias=EPS), reads=[b_ssg[ob]], writes=[b_ssg[ob]])
            ph.op("act", lambda e: e.activation(out=ssg[ob][:, 4:8], in_=ssg[ob][:, 4:8], func=AF.Exp, scale=-0.5),
                  reads=[b_ssg[ob]], writes=[b_ssg[ob]])
            ph.op("pool", lambda e: e.tensor_tensor(
                out=osb[ob][:, :].rearrange("p (h c) -> p h c", h=4),
                in0=osb[ob][:, :].rearrange("p (h c) -> p h c", h=4),
                in1=_ap(ssg[ob], 4, [[8, 128], [1, 4], [0, 128]]), op=ALU.mult),
                reads=[b_osb[ob], b_ssg[ob]], writes=[b_osb[ob]])
            ph.op("pool", lambda e: e.tensor_tensor(
                out=gout[go][:, :], in0=osb[ob][:, :], in1=t2[k % 2][:, :], op=ALU.mult),
                reads=[b_osb[ob], b_t2[k % 2]], writes=[b_gout[go]])

        def part3(k):
            s, t = tiles2[k]
            go = k % 2

            def trg(e):
                ins = None
                for c in range(4):
                    ins = e.transpose(out=psT[:, c * 128:(c + 1) * 128], in_=gout[go][:, c * 128:(c + 1) * 128],
                                      identity=ident_bf[:, :])
                return ins
            ph.op("pe", trg, reads=[b_gout[go], b_cbf], writes=[b_psT])
            ph.op("act", lambda e: e.activation(
                out=mst[go][:, :, :], in_=psT[:, 0:512].rearrange("p (c t) -> p c t", c=4), func=AF.Copy),
                reads=[b_psT], writes=[b_mst[go]])
            ph.op("pool", lambda e: e.dma_start(
                out=MIXT_d.ap()[s, 4:8, :, t * 128:(t + 1) * 128].rearrange("c p t -> p c t"),
                in_=mst[go][:, :, :]), reads=[b_mst[go]], dma=b_mst[go])

        n2 = 0 if P2_CUT in (1, 3, 4) else len(tiles2)
        for k in range(n2 + 2):
            if k < 16:
                sweep1_step(1, k)
            if k < n2:
                part1(k)
            if 0 <= k - 1 < n2:
                part2(k - 1)
            if 0 <= k - 2 < n2:
                part3(k - 2)
        ph.run()
        st.close()

    def phase3(wload):
        st = ExitStack()

        def sbt(name, shape, dt):
            return st.enter_context(nc.sbuf_tensor("p3_" + name, list(shape), dt))

        def pst(name, shape, dt):
            return st.enter_context(nc.psum_tensor("p3_" + name, list(shape), dt))

        ph = Ph(cx, "p3")
        EB = sbt("EB", [128, 8, NCFG, 128], BF16)
        b_EB = cx.bufs(8, "EB")
        tstage = [sbt("tstage%d" % i, [128, NCFG, 128], F32) for i in range(2)]
        b_tstage = cx.bufs(2, "tstage")
        mbias = sbt("mbias", [64, 8], F32)
        b_mbias = cx.buf("mbias")
        qT = sbt("qT", [128, 4, SEQ], BF16)
        b_qT = cx.bufs(4, "qT")
        kT = sbt("kT", [128, 4, SEQ], BF16)
        b_kT = cx.bufs(4, "kT")
        va = sbt("va", [128, 16, 520], BF16)
        b_va = cx.bufs(4, "va")
        ktm = sbt("ktm", [128, 4, 64], BF16)
        b_ktm = cx.buf("ktm")
        vam = sbt("vam", [64, 520], BF16)
        b_vam = cx.buf("vam")
        PP = [sbt("PP%d" % i, [128, 2, 5, 128], BF16) for i in range(2)]
        b_PP = cx.bufs(2, "PP")
        Pm = [sbt("Pm%d" % i, [64, 128], BF16) for i in range(4)]
        b_Pm = cx.bufs(4, "Pm")
        rden = sbt("rden", [128, 8], F32)
        b_rden = cx.buf("rden")
        nao = [sbt("nao%d" % i, [128, 512], BF16) for i in range(2)]
        b_nao = cx.bufs(2, "nao")
        mst = [sbt("mst%d" % i, [128, 4, 128], BF16) for i in range(2)]
        b_mst = cx.bufs(2, "mst")

        psS4 = pst("psS", [128, 2, 8, 128], F32)
        b_psS = cx.bufs(2, "psS")
        psO = [pst("psO%d" % i, [128, 4, 128], F32) for i in range(2)]
        b_psO = cx.bufs(2, "psO")
        psT = pst("psT", [128, 1024], BF16)
        b_psT = cx.buf("psT")

        def prep_tables():
          for h in range(8):
            i = h % 2
            ph.op("sp", (lambda i, h: lambda e: e.dma_start(
                out=tstage[i][:, :, :], in_=tbl_d.ap()[h].rearrange("c k q -> k c q")))(i, h),
                writes=[b_tstage[i]], dma=b_tstage[i])
            ph.op("act", (lambda i, h: lambda e: e.activation(
                out=EB[:, h, :, :], in_=tstage[i][:, :, :], func=AF.Exp))(i, h),
                reads=[b_tstage[i]], writes=[b_EB[h]])
        ph.op("pool", lambda e: e.memset(mbias[:, :], NEG), writes=[b_mbias])
        for h in range(8):
            ph.op("sp", (lambda h: lambda e: e.dma_start(
                out=mbias[48:64, h:h + 1], in_=mb_d.ap()[h:h + 1, :].rearrange("o m -> m o")))(h),
                writes=[b_mbias], dma=b_mbias)
        ph.op("sp", lambda e: e.dma_start(out=ktm[:, :, :],
                                          in_=KTM_d.ap()[:, :, 64:128].rearrange("c p t -> p c t")),
              writes=[b_ktm], dma=b_ktm)
        ph.op("sp", lambda e: e.dma_start(out=vam[:, :], in_=VA_d.ap()[META_T, 64:128, :]),
              writes=[b_vam], dma=b_vam)

        units = []
        for s in range(2):
            for t in range(16):
                if t < 2:
                    c0, nch, slot0 = 0, 4, (8 if t == 0 else 7)
                elif t >= 14:
                    c0, nch, slot0 = 12, 4, (6 if t == 14 else 5)
                else:
                    c0, nch, slot0 = t - 2, 5, 0
                for h in range(8):
                    units.append((s, t, h, c0, nch, slot0))

        def load_grp(s, g):
            gs_ = slice(g * 512, (g + 1) * 512)
            ph.op("sp", lambda e: e.dma_start(
                out=qT[:, :, gs_], in_=QT_d.ap()[s * 4:(s + 1) * 4, :, gs_].rearrange("c p t -> p c t")),
                writes=[b_qT[g]], dma=b_qT[g])
            ph.op("sp", lambda e: e.dma_start(
                out=kT[:, :, gs_], in_=KT_d.ap()[s * 4:(s + 1) * 4, :, gs_].rearrange("c p t -> p c t")),
                writes=[b_kT[g]], dma=b_kT[g])
            ph.op("sp", lambda e: e.dma_start(
                out=va[:, g * 4:(g + 1) * 4, :],
                in_=VA_d.ap()[s * 16 + g * 4:s * 16 + (g + 1) * 4].rearrange("c p t -> p c t")),
                writes=[b_va[g]], dma=b_va[g])

        def stage1_pair(ua, ub):
            s, t, ha, c0, nch, slot0 = units[ua]
            hb_ = units[ub][2]
            assert units[ub][1] == t and ha % 2 == 0 and hb_ == ha + 1 and ua % 2 == 0
            hp = ha // 2

            def smm2(e):
                qa = qT[0:64, hp, t * 128:(t + 1) * 128]
                qb = qT[64:128, hp, t * 128:(t + 1) * 128]
                for ci in range(nch):
                    c = c0 + ci
                    e.matmul(out=psS4[:, 0, ci, :], lhsT=kT[0:64, hp, c * 128:(c + 1) * 128], rhs=qa,
                             start=True, stop=True)
                    e.matmul(out=psS4[:, 1, ci, :], lhsT=kT[64:128, hp, c * 128:(c + 1) * 128], rhs=qb,
                             start=True, stop=True)
                e.matmul(out=psS4[0:64, 0, 5, :], lhsT=ktm[0:64, hp, :], rhs=qa, start=True, stop=True)
                return e.matmul(out=psS4[0:64, 1, 5, :], lhsT=ktm[64:128, hp, :], rhs=qb, start=True, stop=True)
            kgrps = sorted(set((c0 + ci) // 4 for ci in range(nch)))
            ph.op("pe", smm2, reads=[b_qT[t // 4], b_ktm] + [b_kT[g_] for g_ in kgrps],
                  writes=[b_psS[0], b_psS[1]])
            pp = (ua // 2) % 2
            ph.op("act", lambda e: e.activation(
                out=PP[pp][:, :, 0:nch, :], in_=psS4[:, :, 0:nch, :], func=AF.Exp),
                reads=[b_psS[0], b_psS[1]], writes=[b_PP[pp]])
            for ui, pi, h in ((ua, 0, ha), (ub, 1, hb_)):
                p4 = ui % 4
                ph.op("act", (lambda pi, p4, h: lambda e: e.activation(
                    out=Pm[p4][:, :], in_=psS4[0:64, pi, 5, :], func=AF.Exp, bias=mbias[:, h:h + 1]))(pi, p4, h),
                    reads=[b_psS[pi], b_mbias], writes=[b_Pm[p4]])
            ph.op("dve", lambda e: e.tensor_tensor(
                out=PP[pp][:, :, 0:nch, :], in0=PP[pp][:, :, 0:nch, :],
                in1=EB[:, ha:ha + 2, slot0:slot0 + nch, :], op=ALU.mult),
                reads=[b_PP[pp], b_EB[ha], b_EB[ha + 1]], writes=[b_PP[pp]])

        def stage2(ui):
            s, t, h, c0, nch, slot0 = units[ui]
            pi = ui % 4

            def pvm(e):
                o = psO[h // 4][:, h % 4, 0:65]
                for ci in range(nch):
                    e.matmul(out=o, lhsT=PP[(ui // 2) % 2][:, ui % 2, ci, :], rhs=va[:, c0 + ci, h * 65:(h + 1) * 65],
                             start=(ci == 0), stop=False)
                return e.matmul(out=o, lhsT=Pm[pi][:, :], rhs=vam[:, h * 65:(h + 1) * 65],
                                start=False, stop=True)
            kgrps = sorted(set((c0 + ci) // 4 for ci in range(nch)))
            ph.op("pe", pvm, reads=[b_PP[(ui // 2) % 2], b_Pm[pi], b_vam] + [b_va[g_] for g_ in kgrps],
                  writes=[b_psO[h // 4]])
            if h == 7:
                k2 = (ui // 8) % 2
                for hb in range(2):
                    ph.op("dve", (lambda hb: lambda e: e.reciprocal(
                        out=rden[:, hb * 4:(hb + 1) * 4], in_=psO[hb][:, :, 64]))(hb),
                        reads=[b_psO[hb]], writes=[b_rden])
                    ph.op("dve", (lambda hb: lambda e: e.tensor_tensor(
                        out=nao[k2][:, hb * 256:(hb + 1) * 256].rearrange("p (h c) -> p h c", h=4),
                        in0=psO[hb][:, :, 0:64],
                        in1=_ap(rden, hb * 4, [[8, 128], [1, 4], [0, 64]]), op=ALU.mult))(hb),
                        reads=[b_psO[hb], b_rden], writes=[b_nao[k2]])

        def stage3(ui):
            s, t, h, c0, nch, slot0 = units[ui]
            if h != 7:
                return
            k2 = (ui // 8) % 2

            def trn(e):
                ins = None
                for c in range(4):
                    ins = e.transpose(out=psT[:, c * 128:(c + 1) * 128], in_=nao[k2][:, c * 128:(c + 1) * 128],
                                      identity=ident_bf[:, :])
                return ins
            ph.op("pe", trn, reads=[b_nao[k2], b_cbf], writes=[b_psT])
            ph.op("act", lambda e: e.activation(
                out=mst[k2][:, :, :], in_=psT[:, 0:512].rearrange("p (c t) -> p c t", c=4), func=AF.Copy),
                reads=[b_psT], writes=[b_mst[k2]])
            ph.op("act", lambda e: e.dma_start(
                out=MIXT_d.ap()[s, 0:4, :, t * 128:(t + 1) * 128].rearrange("c p t -> p c t"),
                in_=mst[k2][:, :, :]), reads=[b_mst[k2]], dma=b_mst[k2])

        nu = len(units)
        load_grp(0, 0)
        prep_tables()
        for g_ in range(1, 4):
            load_grp(0, g_)
        for s in range(2):
            us = [ui for ui in range(nu) if units[ui][0] == s]
            n = len(us)
            np_ = n // 2
            for p in range(np_ + 3):
                k = 2 * p
                if s == 0 and k == 24:
                    wload(ph, b_psT)
                if s == 0:
                    for g_ in range(4):
                        if k == min(n + 2, (4 * g_ + 7) * 8 + 2):
                            load_grp(1, g_)
                if p < np_:
                    stage1_pair(us[2 * p], us[2 * p + 1])
                if 0 <= p - 1 < np_:
                    stage2(us[2 * (p - 1)])
                    stage2(us[2 * (p - 1) + 1])
                if 0 <= p - 2 < np_:
                    stage3(us[2 * (p - 2)])
                    stage3(us[2 * (p - 2) + 1])
        ph.run()
        st.close()

    def phase4a(wout, b_wout, wload):
        st = ExitStack()

        def sbt(name, shape, dt):
            return st.enter_context(nc.sbuf_tensor("p4a_" + name, list(shape), dt))

        def pst(name, shape, dt):
            return st.enter_context(nc.psum_tensor("p4a_" + name, list(shape), dt))

        ph = Ph(cx, "p4a")
        gffn = sbt("gffn", [128, D], F32)
        b_gffn = cx.buf("gffn")
        mixt = [sbt("mixt%d" % i, [128, 8, 128], BF16) for i in range(2)]
        b_mixt = cx.bufs(2, "mixt")
        xt = [sbt("xt%d" % i, [128, D], F32) for i in range(2)]
        b_xt = cx.bufs(2, "xt")
        h1 = [sbt("h1%d" % i, [128, D], F32) for i in range(2)]
        b_h1 = cx.bufs(2, "h1")
        junk = sbt("junk", [128, D], BF16)
        b_junk = cx.buf("junk")
        ss = sbt("ss", [128, 4], F32)
        b_ss = cx.bufs(2, "ss")
        hn = [sbt("hn%d" % i, [128, D], BF16) for i in range(2)]
        b_hn = cx.bufs(2, "hn")
        hst = [sbt("hst%d" % i, [128, 8, 128], BF16) for i in range(2)]
        b_hst = cx.bufs(2, "hst")
        psH = [pst("psH%d" % i, [128, D], F32) for i in range(2)]
        b_psH = cx.bufs(2, "psH")
        psT = [pst("psT%d" % i, [128, D], BF16) for i in range(2)]
        b_psT = cx.bufs(2, "psT")

        ph.op("sp", lambda e: e.dma_start(out=gffn[:, :], in_=gffn_d.ap().broadcast_to([128, D])),
              writes=[b_gffn], dma=b_gffn)
        def partA(T):
            s, t = T // 16, T % 16
            i = T % 2
            ph.op("sp", lambda e: e.dma_start(
                out=mixt[i][:, :, :],
                in_=MIXT_d.ap()[s, :, :, t * 128:(t + 1) * 128].rearrange("c p t -> p c t")),
                writes=[b_mixt[i]], dma=b_mixt[i])
            ph.op("sp", lambda e: e.dma_start(
                out=xt[i][:, :], in_=x_d.ap()[s, t * 128:(t + 1) * 128, :]),
                writes=[b_xt[i]], dma=b_xt[i])

            def mm(e):
                ins = None
                for n in range(2):
                    for kc in range(8):
                        ins = e.matmul(out=psH[i][:, n * 512:(n + 1) * 512], lhsT=mixt[i][:, kc, :],
                                       rhs=wout[:, kc, n * 512:(n + 1) * 512], start=(kc == 0), stop=(kc == 7))
                return ins
            ph.op("pe", mm, reads=[b_mixt[i]] + b_wout, writes=[b_psH[i]])
            ph.op("dve", lambda e: e.tensor_tensor(
                out=h1[i][:, :], in0=psH[i][:, :], in1=xt[i][:, :], op=ALU.add),
                reads=[b_psH[i], b_xt[i]], writes=[b_h1[i]])
            ph.op("act", lambda e: e.dma_start(out=H1_d.ap()[T], in_=h1[i][:, :]),
                  reads=[b_h1[i]], dma=b_h1[i])
            rms_rstd(ph, "dve", lambda: h1[i][:, :], b_h1[i], ss, b_ss[i], i * 2,
                     lambda: junk[:, :], b_junk, D)
            ph.op("dve", lambda e: e.scalar_tensor_tensor(
                out=hn[i][:, :], in0=h1[i][:, :], scalar=ss[:, i * 2 + 1:i * 2 + 2], in1=gffn[:, :],
                op0=ALU.mult, op1=ALU.mult),
                reads=[b_h1[i], b_ss[i], b_gffn], writes=[b_hn[i]])

        def partB(T):
            g, j = T // 4, T % 4
            i = T % 2

            def tr(e):
                ins = None
                for kc in range(8):
                    ins = e.transpose(out=psT[i][:, kc * 128:(kc + 1) * 128],
                                      in_=hn[i][:, kc * 128:(kc + 1) * 128], identity=ident_bf[:, :])
                return ins
            ph.op("pe", tr, reads=[b_hn[i], b_cbf], writes=[b_psT[i]])
            ph.op("act", lambda e: e.activation(
                out=hst[i][:, :, :], in_=psT[i][:, :].rearrange("p (k t) -> p k t", k=8), func=AF.Copy),
                reads=[b_psT[i]], writes=[b_hst[i]])
            ph.op("act", lambda e: e.dma_start(
                out=HNT_d.ap()[g].rearrange("p (k t) -> p k t", k=8)[:, :, j * 128:(j + 1) * 128],
                in_=hst[i][:, :, :]), reads=[b_hst[i]], dma=b_hst[i])

        for T in range(33):
            if T == 3:
                wload(ph, b_psH[0])
            if T < 32:
                partA(T)
            if T >= 1:
                partB(T - 1)
        ph.run()
        st.close()

    def phase4b(wg, wu, wd, b_wg, b_wu, b_wd):
        st = ExitStack()

        def sbt(name, shape, dt):
            return st.enter_context(nc.sbuf_tensor("p4b_" + name, list(shape), dt))

        def pst(name, shape, dt):
            return st.enter_context(nc.psum_tensor("p4b_" + name, list(shape), dt))

        ph = Ph(cx, "p4b")
        gfin = sbt("gfin", [128, D], F32)
        b_gfin = cx.buf("gfin")
        hnT = [sbt("hnT%d" % i, [128, 8, 512], BF16) for i in range(2)]
        b_hnT = cx.bufs(2, "hnT")
        actT = sbt("actT", [128, NFF, 512], BF16)
        b_actT = cx.bufs(NFF, "actT")
        sg = [sbt("sg%d" % i, [128, 512], F32) for i in range(2)]
        b_sg = cx.bufs(2, "sg")
        h1 = [sbt("h1%d" % i, [128, D], F32) for i in range(2)]
        b_h1 = cx.bufs(2, "h1")
        junk_ap = sg[0][:, :].bitcast(BF16)
        b_junk = b_sg[0]
        ss = sbt("ss", [128, 4], F32)
        b_ss = cx.bufs(2, "ss")
        psG = [pst("psG%d" % i, [128, 512], F32) for i in range(2)]
        b_psG = cx.bufs(2, "psG")
        psU = [pst("psU%d" % i, [128, 512], F32) for i in range(2)]
        b_psU = cx.bufs(2, "psU")
        psD = [pst("psD%d" % i, [128, D], F32) for i in range(2)]
        b_psD = cx.bufs(2, "psD")

        ph.op("sp", lambda e: e.dma_start(out=gfin[:, :], in_=gfin_d.ap().broadcast_to([128, D])),
              writes=[b_gfin], dma=b_gfin)
        fc = 0
        tcn = 0
        for g in range(8):
            gi = g % 2
            ph.op("sp", (lambda gi, g: lambda e: e.dma_start(
                out=hnT[gi][:, :, :], in_=HNT_d.ap()[g].rearrange("p (k t) -> p k t", k=8)))(gi, g),
                writes=[b_hnT[gi]], dma=b_hnT[gi])
            for f in range(NFF):
                pi = fc % 2
                fc += 1

                def gmm(e, pi=pi, f=f, gi=gi):
                    ins = None
                    for kc in range(8):
                        ins = e.matmul(out=psG[pi][:, :], lhsT=wg[:, kc, f * 128:(f + 1) * 128],
                                       rhs=hnT[gi][:, kc, :], start=(kc == 0), stop=(kc == 7))
                    return ins

                def umm(e, pi=pi, f=f, gi=gi):
                    ins = None
                    for kc in range(8):
                        ins = e.matmul(out=psU[pi][:, :], lhsT=wu[:, kc, f * 128:(f + 1) * 128],
                                       rhs=hnT[gi][:, kc, :], start=(kc == 0), stop=(kc == 7))
                    return ins
                ph.op("pe", gmm, reads=[b_hnT[gi]] + b_wg, writes=[b_psG[pi]])
                ph.op("pe", umm, reads=[b_hnT[gi]] + b_wu, writes=[b_psU[pi]])
                ph.op("act", (lambda pi: lambda e: e.activation(out=sg[pi][:, :], in_=psG[pi][:, :], func=AF.Silu))(pi),
                      reads=[b_psG[pi]], writes=[b_sg[pi]])
                ph.op("dve", (lambda pi, f: lambda e: e.tensor_tensor(
                    out=actT[:, f, :], in0=psU[pi][:, :], in1=sg[pi][:, :], op=ALU.mult))(pi, f),
                    reads=[b_psU[pi], b_sg[pi]], writes=[b_actT[f]])
            for j in range(4):
                T = g * 4 + j
                s, t = T // 16, T % 16
                i = tcn % 2
                tcn += 1
                ph.op("sp", (lambda i, T: lambda e: e.dma_start(out=h1[i][:, :], in_=H1_d.ap()[T]))(i, T),
                      writes=[b_h1[i]], dma=b_h1[i])

                def dmm(e, i=i, j=j):
                    ins = None
                    for n in range(2):
                        for f in range(NFF):
                            ins = e.matmul(out=psD[i][:, n * 512:(n + 1) * 512], lhsT=actT[:, f, j * 128:(j + 1) * 128],
                                           rhs=wd[:, f, n * 512:(n + 1) * 512], start=(f == 0), stop=(f == NFF - 1))
                    return ins
                ph.op("pe", dmm, reads=b_actT + b_wd, writes=[b_psD[i]])
                ph.op("dve", (lambda i: lambda e: e.tensor_tensor(
                    out=h1[i][:, :], in0=psD[i][:, :], in1=h1[i][:, :], op=ALU.add))(i),
                    reads=[b_psD[i], b_h1[i]], writes=[b_h1[i]])
                rms_rstd(ph, "dve", (lambda i: lambda: h1[i][:, :])(i), b_h1[i], ss, b_ss[i], i * 2,
                         lambda: junk_ap, b_junk, D)
                ph.op("dve", (lambda i: lambda e: e.scalar_tensor_tensor(
                    out=h1[i][:, :], in0=h1[i][:, :], scalar=ss[:, i * 2 + 1:i * 2 + 2], in1=gfin[:, :],
                    op0=ALU.mult, op1=ALU.mult))(i),
                    reads=[b_h1[i], b_ss[i], b_gfin], writes=[b_h1[i]])
                ph.op("pool", (lambda i, s, t: lambda e: e.dma_start(
                    out=out_d.ap()[s, t * 128:(t + 1) * 128, :], in_=h1[i][:, :]))(i, s, t),
                    reads=[b_h1[i]], dma=b_h1[i])
        ph.run()
        st.close()

    phase1()
    if LAST_PHASE >= 2:
        phase2()
    if LAST_PHASE >= 3:
        wst = ExitStack()
        wst_o = ExitStack()
        wg = wst.enter_context(nc.sbuf_tensor("w_wg", [128, 8, DFF], BF16))
        wu = wst.enter_context(nc.sbuf_tensor("w_wu", [128, 8, DFF], BF16))
        wout = wst_o.enter_context(nc.sbuf_tensor("w_wout", [128, 8, D], BF16))
        b_wout = cx.bufs(8, "wout")
        b_wg = cx.bufs(8, "wg")
        b_wu = cx.bufs(8, "wu")

        def wload3(ph, after):
            for kc in range(8):
                ph.op("pool", (lambda kc: lambda e: e.dma_start(
                    out=wout[:, kc, :], in_=wout_d.ap()[kc * 128:(kc + 1) * 128, :], max_dma_last_dim=4096))(kc),
                    reads=([after] if kc == 0 else []), writes=[b_wout[kc]], dma=b_wout[kc])
            for kc in range(8):
                ph.op("pool", (lambda kc: lambda e: e.dma_start(
                    out=wg[:, kc, :], in_=wg_d.ap()[kc * 128:(kc + 1) * 128, :], max_dma_last_dim=4096))(kc),
                    writes=[b_wg[kc]], dma=b_wg[kc])
                ph.op("pool", (lambda kc: lambda e: e.dma_start(
                    out=wu[:, kc, :], in_=wu_d.ap()[kc * 128:(kc + 1) * 128, :], max_dma_last_dim=4096))(kc),
                    writes=[b_wu[kc]], dma=b_wu[kc])
        phase3(wload3)
        if LAST_PHASE >= 4:
            wd = wst_o.enter_context(nc.sbuf_tensor("w_wd", [128, NFF, D], BF16))
            b_wd = cx.bufs(NFF, "wd")

            def wload4(ph, after):
                for f in range(NFF):
                    ph.op("pool", (lambda f: lambda e: e.dma_start(
                        out=wd[:, f, :], in_=wd_d.ap()[f * 128:(f + 1) * 128, :], max_dma_last_dim=4096))(f),
                        reads=([after] if f == 0 else []), writes=[b_wd[f]], dma=b_wd[f])
            phase4a(wout, b_wout, wload4)
            if LAST_PHASE >= 5:
                phase4b(wg, wu, wd, b_wg, b_wu, b_wd)
        wst_o.close()
        wst.close()
    return nc, es


def _host_consts():
    c = np.zeros((128, NCONST), np.float32)
    j = np.arange(128)[:, None]
    i = np.arange(128)[None, :]
    c[:, K_ID:K_ID + 128] = (j == i)
    c[:, K_TRIF:K_TRIF + 128] = (j <= i) * (-1.0 / 16.0)
    c[:, K_TRIB:K_TRIB + 128] = (j >= i) * (-1.0 / 16.0)
    c[:, K_SUF:K_SUF + 128] = (j > i) * (-1.0 / 16.0)
    c[:, K_SUB:K_SUB + 128] = (j < i) * (-1.0 / 16.0)
    c[:, K_MF:K_MF + 128] = (j <= i)
    c[:, K_MB:K_MB + 128] = (j >= i)
    return c


def _host_na_table(rpb):
    rpb = np.asarray(rpb, np.float32).reshape(8, 15, 31)
    ext = np.concatenate([rpb.reshape(8, -1), np.full((8, 1), NEG, np.float32)], axis=1)
    SENT = 15 * 31
    cols = np.arange(64)
    col_start = np.clip(cols - 8, 0, 48)
    kc = cols[:, None]
    qc = cols[None, :]
    colvalid = (kc >= col_start[None, :]) & (kc < col_start[None, :] + 16)
    dc = np.clip(kc - qc, -15, 15) + 15
    cfgs = [(-2, True), (-1, False), (0, False), (1, False), (2, True),
            (-3, False), (-2, False), (-1, False), (0, False), (1, False), (2, False), (3, False)]
    idx = np.full((NCFG, 128, 128), SENT, np.int64)
    for ci, (d, interior) in enumerate(cfgs):
        for rl in range(2):
            for il in range(2):
                rel = 2 * d + rl - il
                dr = rel + 7
                if dr < 0 or dr > 14:
                    continue
                if interior and not (-4 <= rel <= 3):
                    continue
                blk = np.where(colvalid, dr * 31 + dc, SENT)
                idx[ci, rl * 64:(rl + 1) * 64, il * 64:(il + 1) * 64] = blk
    return np.ascontiguousarray(ext[:, idx])


def kernel(x, meta_tokens, norm_mix_gain, w_in, rpb, meta_bias, w_gate_up_fwd, b_gate_fwd,
           w_gate_up_bwd, b_gate_bwd, gla_norm_gain, w_out, norm_ffn_gain, w_ffn_gate,
           w_ffn_up, w_ffn_down, norm_final_gain):
    f = lambda a: np.ascontiguousarray(np.asarray(a, np.float32))
    nc, es = build_program()
    shared = {
        "meta_tokens": f(meta_tokens), "norm_mix_gain": f(norm_mix_gain).reshape(1, D),
        "w_in": f(w_in).reshape(D, IN_W), "na_tbl": _host_na_table(rpb),
        "meta_bias": f(meta_bias).reshape(8, 16),
        "w_gate_up_fwd": f(w_gate_up_fwd).reshape(16, 256), "b_gate_fwd": f(b_gate_fwd).reshape(1, 256),
        "w_gate_up_bwd": f(w_gate_up_bwd).reshape(16, 256), "b_gate_bwd": f(b_gate_bwd).reshape(1, 256),
        "gla_norm_gain": f(gla_norm_gain).reshape(1, 128), "w_out": f(w_out).reshape(D, D),
        "norm_ffn_gain": f(norm_ffn_gain).reshape(1, D), "w_ffn_gate": f(w_ffn_gate).reshape(D, DFF),
        "w_ffn_up": f(w_ffn_up).reshape(D, DFF), "w_ffn_down": f(w_ffn_down).reshape(DFF, D),
        "norm_final_gain": f(norm_final_gain).reshape(1, D), "consts": _host_consts(),
    }
    xs = f(x)
    in_maps = []
    for c in range(8):
        m = dict(shared)
        m["x"] = xs[2 * c:2 * c + 2]
        in_maps.append(m)
    res = run_bass_kernel_spmd(nc, in_maps, core_ids=list(range(8)))
    es.close()
    kernel.last_results = res.results
    return np.concatenate([r["out"] for r in res.results], axis=0)
```

```python
import numpy as np
from contextlib import ExitStack
import concourse.bass as bass
import concourse.mybir as mybir
from concourse.bass_utils import run_bass_kernel_spmd

F32 = mybir.dt.float32
BF16 = mybir.dt.bfloat16
AF = mybir.ActivationFunctionType
ALU = mybir.AluOpType
AX = mybir.AxisListType

DEBUG = False
LAST_PHASE = 99
P2_CUT = 0

D = 1024
SEQ = 2048
NT = 16
NTILES = 33
META_T = 32
IN_W = 3104
DFF = 2816
NFF = 22
EPS = 1e-6
NEG = -30000.0

C_NAQ, C_NAK, C_NAV = 0, 512, 1024
C_GQ, C_GK, C_GV, C_GR, C_GF, C_GB = 1536, 1792, 2048, 2560, 3072, 3088

K_ID, K_TRIF, K_TRIB, K_SUF, K_SUB, K_MF, K_MB = [128 * i for i in range(7)]
NCONST = 128 * 7

NCFG = 12


class Buf:
    __slots__ = ("name", "last_w", "readers", "sem", "cnt", "kind", "sub")

    def __init__(self, name):
        self.name = name
        self.last_w = None
        self.readers = []
        self.sem = None
        self.cnt = 0
        self.kind = None
        self.sub = None


class Ctx:
    def __init__(self, nc, es):
        self.nc = nc
        self.es = es
        self.eng_sem = {}
        self.eng_cnt = {}
        for e in ("pe", "dve", "act", "pool", "sp"):
            self.eng_sem[e] = es.enter_context(nc.semaphore("es_" + e))
            self.eng_cnt[e] = 0
        self.dma_bufs = []
        self.sem_pool = {"sw": [], "hw": []}
        self.nsem = 0
        self.nbuf = 0

    def buf(self, name=None):
        self.nbuf += 1
        return Buf(name or "b%d" % self.nbuf)

    def bufs(self, n, name=None):
        return [self.buf((name or "b") + str(i)) for i in range(n)]


class Ph:
    def __init__(self, cx, name):
        self.cx = cx
        self.name = name
        self.ops = {e: [] for e in ("pe", "dve", "act", "pool", "sp")}
        self.waited = {e: {} for e in self.ops}
        self.touched = []

    def _add_dep(self, deps, d, eng, raw, waw=False):
        if d is None:
            return
        sem, val, deng, is_dma = d
        if deng == eng and not is_dma:
            if eng == "pe":
                return
            if not raw and eng != "pool" and not waw:
                return
        deps.append((sem, val))

    def op(self, eng, fn, reads=(), writes=(), dma=None):
        cx = self.cx
        deps = []
        for b in reads:
            self._add_dep(deps, b.last_w, eng, True)
        for b in writes:
            self._add_dep(deps, b.last_w, eng, False, True)
            for r in b.readers:
                self._add_dep(deps, r, eng, False)
        if dma is not None:
            kind = "sw" if eng == "pool" else "hw"
            if dma.sub is None:
                dma.sub = {}
            if kind not in dma.sub:
                dma.sub[kind] = Buf(dma.name + "_" + kind)
            dma = dma.sub[kind]
            if dma.sem is None:
                if cx.sem_pool[kind]:
                    dma.sem, dma.cnt = cx.sem_pool[kind].pop()
                else:
                    cx.nsem += 1
                    dma.sem = cx.es.enter_context(cx.nc.semaphore("ds%d" % cx.nsem))
                    dma.cnt = 0
                dma.kind = kind
                cx.dma_bufs.append(dma)
            assert dma.kind == kind, dma.name
            dma.cnt += 16
            sig = (dma.sem, dma.cnt, eng, True)
            inc = (dma.sem, 16)
        else:
            cx.eng_cnt[eng] += 1
            sig = (cx.eng_sem[eng], cx.eng_cnt[eng], eng, False)
            inc = (cx.eng_sem[eng], 1)
        w = self.waited[eng]
        waits = []
        best = {}
        for sem, val in deps:
            k = id(sem)
            if w.get(k, 0) >= val:
                continue
            if k not in best or best[k][1] < val:
                best[k] = (sem, val)
        for k, (sem, val) in best.items():
            w[k] = val
            waits.append((sem, val))
        self.ops[eng].append((waits, fn, inc))
        for b in reads:
            b.readers.append(sig)
            self.touched.append(b)
        for b in writes:
            b.last_w = sig
            b.readers = []
            self.touched.append(b)
        return sig

    def run(self):
        cx = self.cx
        nc = cx.nc
        fin = [(b.sem, b.cnt) for b in cx.dma_bufs]

        def emit(engname):
            lst = self.ops[engname]

            def body(e):
                for waits, fn, inc in lst:
                    for sem, val in waits:
                        e.wait_ge(sem, val)
                    ins = fn(e)
                    ins.then_inc(inc[0], inc[1])
                if engname == "sp":
                    for sem, val in fin:
                        e.wait_ge(sem, val)
            return body

        with nc.Block() as blk:
            blk.sync(emit("sp"))
            blk.tensor(emit("pe"))
            blk.vector(emit("dve"))
            blk.scalar(emit("act"))
            blk.gpsimd(emit("pool"))
        for b in self.touched:
            b.last_w = None
            b.readers = []
        for b in cx.dma_bufs:
            cx.sem_pool[b.kind].append((b.sem, b.cnt))
            b.sem = None
        cx.dma_bufs = []


def _ap(t, off, pat):
    return bass.AP(tensor=t, offset=off, ap=[list(p) for p in pat])


def build_program():
    nc = bass.Bass("TRN2", target_bir_lowering=False)
    es = ExitStack()
    cx = Ctx(nc, es)
    skind = "ExternalOutput" if DEBUG else "Internal"

    def din(name, shape, dt=F32):
        return nc.dram_tensor(name, list(shape), dt, kind="ExternalInput")

    def dscr(name, shape, dt=BF16):
        return nc.dram_tensor(name, list(shape), dt, kind=skind)

    x_d = din("x", [2, SEQ, D])
    meta_d = din("meta_tokens", [16, D])
    gmix_d = din("norm_mix_gain", [1, D])
    win_d = din("w_in", [D, IN_W])
    tbl_d = din("na_tbl", [8, NCFG, 128, 128])
    mb_d = din("meta_bias", [8, 16])
    wgf_d = din("w_gate_up_fwd", [16, 256])
    bgf_d = din("b_gate_fwd", [1, 256])
    wgb_d = din("w_gate_up_bwd", [16, 256])
    bgb_d = din("b_gate_bwd", [1, 256])
    ggla_d = din("gla_norm_gain", [1, 128])
    wout_d = din("w_out", [D, D])
    gffn_d = din("norm_ffn_gain", [1, D])
    wg_d = din("w_ffn_gate", [D, DFF])
    wu_d = din("w_ffn_up", [D, DFF])
    wd_d = din("w_ffn_down", [DFF, D])
    gfin_d = din("norm_final_gain", [1, D])
    consts_d = din("consts", [128, NCONST])
    out_d = nc.dram_tensor("out", [2, SEQ, D], F32, kind="ExternalOutput")

    QT_d = dscr("s_qt", [8, 128, SEQ])
    KT_d = dscr("s_kt", [8, 128, SEQ])
    KTM_d = dscr("s_ktm", [4, 128, 128])
    VA_d = dscr("s_va", [NTILES, 128, 520])
    GQ_d = dscr("s_gq", [8, 128, SEQ])
    GK_d = dscr("s_gk", [8, 128, SEQ])
    GKD_d = dscr("s_gkd", [NTILES, 128, 512])
    GV_d = dscr("s_gv", [NTILES, 128, 512])
    GR_d = dscr("s_gr", [NTILES, 128, 512])
    MIXT_d = dscr("s_mixt", [2, 8, 128, SEQ])
    H1_d = dscr("s_h1", [32, 128, D], F32)
    HNT_d = dscr("s_hnt", [8, 128, 8 * 512])

    def sb(name, shape, dt):
        return es.enter_context(nc.sbuf_tensor("sb_" + name, list(shape), dt))

    consts = sb("consts", [128, NCONST], F32)
    ident_bf = sb("ident_bf", [128, 128], BF16)
    maskF = sb("maskF", [128, 128], BF16)
    maskB = sb("maskB", [128, 128], BF16)
    maskFB = sb("maskFB", [128, 256], BF16)
    dec = sb("dec", [128, NTILES, 4], F32)
    b_consts = cx.buf("consts")
    b_cbf = cx.buf("cbf")
    b_dec = cx.buf("dec")

    def phase1():
        st = ExitStack()

        def sbt(name, shape, dt):
            return st.enter_context(nc.sbuf_tensor("p1_" + name, list(shape), dt))

        def pst(name, shape, dt):
            return st.enter_context(nc.psum_tensor("p1_" + name, list(shape), dt))

        ph = Ph(cx, "p1")
        win = sbt("win", [128, 8, IN_W], BF16)
        WBLK = 776
        b_win = cx.bufs(4, "win")

        def wb(c0, n):
            return [b_win[b] for b in range(4) if b * WBLK < c0 + n and (b + 1) * WBLK > c0]
        w2 = sbt("w2", [64, 512], BF16)
        w2f = sbt("w2f", [64, 512], F32)
        b_w2f = cx.buf("w2f")
        b_w2 = cx.buf("w2")
        gmix = sbt("gmix", [128, D], F32)
        b_gmix = cx.buf("gmix")
        xt = [sbt("xt%d" % i, [128, D], F32) for i in range(3)]
        b_xt = cx.bufs(3, "xt")
        junk = sbt("junk", [128, D], BF16)
        b_junk = cx.buf("junk")
        ss = sbt("ss", [128, 8], F32)
        b_ss = cx.bufs(4, "ss")
        xn = [sbt("xn%d" % i, [128, D], BF16) for i in range(4)]
        b_xn = cx.bufs(4, "xn")
        xnT = [sbt("xnT%d" % i, [128, 8, 512], BF16) for i in range(2)]
        b_xnT = [cx.bufs(4, "xnT%d_" % i) for i in range(2)]
        glT = [sbt("glT%d" % i, [64, 512], BF16) for i in range(2)]
        b_glT = cx.bufs(2, "glT")
        stQK = [sbt("stQK%d" % i, [128, 8, 512], BF16) for i in range(2)]
        b_stQK = cx.bufs(2, "stQK")
        gqk = [sbt("gqk%d" % i, [128, 4, 512], F32) for i in range(2)]
        b_gqk = cx.bufs(2, "gqk")
        stG = [sbt("stG%d" % i, [128, 8, 512], BF16) for i in range(2)]
        b_stG = cx.bufs(2, "stG")
        stVA = [sbt("stVA%d" % i, [128, 520], BF16) for i in range(2)]
        b_stVA = cx.bufs(2, "stVA")
        stGV = [sbt("stGV%d" % i, [128, 512], BF16) for i in range(2)]
        b_stGV = cx.bufs(2, "stGV")
        rraw = [sbt("rraw%d" % i, [128, 512], BF16) for i in range(2)]
        b_rraw = cx.bufs(2, "rraw")
        stGR = [sbt("stGR%d" % i, [128, 512], BF16) for i in range(2)]
        b_stGR = cx.bufs(2, "stGR")
        ktok = [sbt("ktok%d" % i, [128, 256], F32) for i in range(2)]
        b_ktok = cx.bufs(2, "ktok")
        stKD = [sbt("stKD%d" % i, [128, 512], BF16) for i in range(2)]
        b_stKD = cx.bufs(2, "stKD")
        ge = [sbt("ge%d" % i, [128, 512], F32) for i in range(2)]
        b_ge = cx.bufs(2, "ge")
        gp = [sbt("gp%d" % i, [128, 512], F32) for i in range(2)]
        b_gp = cx.bufs(2, "gp")
        Eq = [sbt("Eq%d" % i, [128, 4, 128], F32) for i in range(2)]
        b_Eq = cx.bufs(2, "Eq")
        Ek = [sbt("Ek%d" % i, [128, 4, 128], F32) for i in range(2)]
        b_Ek = cx.bufs(2, "Ek")
        Ed = [sbt("Ed%d" % i, [128, 512], F32) for i in range(2)]
        b_Ed = cx.bufs(2, "Ed")

        psT = [pst("psT%d" % i, [128, D], BF16) for i in range(2)]
        b_psT = cx.bufs(2, "psT")
        psF = [pst("psF%d" % i, [128, 512], F32) for i in range(4)]
        b_psF = cx.bufs(4, "psF")
        psK = psF
        b_psK = b_psF
        psB = pst("psB", [128, 4, 128], F32)
        b_psB = cx.buf("psB")
        psS = pst("psS", [128, 512], F32)
        b_psS = cx.buf("psS")

        ph.op("sp", lambda e: e.dma_start(out=consts[:, :], in_=consts_d.ap()),
              writes=[b_consts], dma=b_consts)
        ph.op("sp", lambda e: e.dma_start(out=gmix[:, :], in_=gmix_d.ap().broadcast_to([128, D])),
              writes=[b_gmix], dma=b_gmix)
        ph.op("dve", lambda e: e.tensor_copy(out=ident_bf[:, :], in_=consts[:, K_ID:K_ID + 128]),
              reads=[b_consts], writes=[b_cbf])
        ph.op("dve", lambda e: e.tensor_copy(out=maskF[:, :], in_=consts[:, K_MF:K_MF + 128]),
              reads=[b_consts], writes=[b_cbf])
        ph.op("dve", lambda e: e.tensor_copy(out=maskB[:, :], in_=consts[:, K_MB:K_MB + 128]),
              reads=[b_consts], writes=[b_cbf])
        ph.op("dve", lambda e: e.tensor_copy(out=maskFB[:, :], in_=consts[:, K_MF:K_MF + 256]),
              reads=[b_consts], writes=[b_cbf])
        ph.op("pool", lambda e: e.memset(w2f[:, :], 0.0), writes=[b_w2f])
        ph.op("sp", lambda e: e.dma_start(out=w2f[0:16, 0:256], in_=wgf_d.ap()), writes=[b_w2f], dma=b_w2f)
        ph.op("sp", lambda e: e.dma_start(out=w2f[16:32, 256:512], in_=wgb_d.ap()), writes=[b_w2f], dma=b_w2f)
        ph.op("sp", lambda e: e.dma_start(out=w2f[32:33, 0:256], in_=bgf_d.ap()), writes=[b_w2f], dma=b_w2f)
        ph.op("sp", lambda e: e.dma_start(out=w2f[32:33, 256:512], in_=bgb_d.ap()), writes=[b_w2f], dma=b_w2f)
        ph.op("dve", lambda e: e.tensor_copy(out=w2[:, :], in_=w2f[:, :]), reads=[b_w2f], writes=[b_w2])
        for i in range(2):
            ph.op("pool", (lambda i: lambda e: e.memset(glT[i][32:64, :], 1.0))(i), writes=[b_glT[i]])
            ph.op("pool", (lambda i: lambda e: e.memset(stVA[i][:, :], 1.0))(i), writes=[b_stVA[i]])
        for b in range(4):
            ph.op("pool", (lambda b: lambda e: e.dma_start(
                out=win[:, :, b * WBLK:(b + 1) * WBLK],
                in_=win_d.ap()[:, b * WBLK:(b + 1) * WBLK].rearrange("(k p) c -> p k c", p=128),
                max_dma_last_dim=4096))(b),
                writes=[b_win[b]], dma=b_win[b])

        state = {"xi": 0}

        def stageA1(g):
            tiles = [META_T] if g == 8 else [g * 4 + j for j in range(4)]
            for j, T in enumerate(tiles):
                xi = state["xi"] % 3
                state["xi"] += 1
                ti = j
                if T == META_T:
                    ph.op("pool", (lambda xi: lambda e: e.memset(xt[xi][:, :], 0.0))(xi), writes=[b_xt[xi]])
                    ph.op("sp", (lambda xi: lambda e: e.dma_start(out=xt[xi][112:128, :], in_=meta_d.ap()))(xi),
                          writes=[b_xt[xi]], dma=b_xt[xi])
                else:
                    s_, t_ = T // 16, T % 16
                    ph.op("sp", (lambda xi, s_, t_: lambda e: e.dma_start(
                        out=xt[xi][:, :], in_=x_d.ap()[s_, t_ * 128:(t_ + 1) * 128, :]))(xi, s_, t_),
                        writes=[b_xt[xi]], dma=b_xt[xi])
                ph.op("dve", (lambda xi, ti: lambda e: e.scalar_tensor_tensor(
                    out=junk[:, :], in0=xt[xi][:, :], scalar=1.0, in1=xt[xi][:, :],
                    op0=ALU.mult, op1=ALU.mult, accum_out=ss[:, ti * 2:ti * 2 + 1]))(xi, ti),
                    reads=[b_xt[xi]], writes=[b_junk, b_ss[ti]])
                ph.op("act", (lambda ti: lambda e: e.activation(
                    out=ss[:, ti * 2 + 1:ti * 2 + 2], in_=ss[:, ti * 2:ti * 2 + 1], func=AF.Ln,
                    scale=1.0 / D, bias=EPS))(ti), reads=[b_ss[ti]], writes=[b_ss[ti]])
                ph.op("act", (lambda ti: lambda e: e.activation(
                    out=ss[:, ti * 2 + 1:ti * 2 + 2], in_=ss[:, ti * 2 + 1:ti * 2 + 2], func=AF.Exp,
                    scale=-0.5))(ti), reads=[b_ss[ti]], writes=[b_ss[ti]])
                ph.op("dve", (lambda xi, ti: lambda e: e.scalar_tensor_tensor(
                    out=xn[ti][:, :], in0=xt[xi][:, :], scalar=ss[:, ti * 2 + 1:ti * 2 + 2], in1=gmix[:, :],
                    op0=ALU.mult, op1=ALU.mult))(xi, ti),
                    reads=[b_xt[xi], b_ss[ti], b_gmix], writes=[b_xn[ti]])

        def stageA2(g, gs):
            tiles = [META_T] if g == 8 else [g * 4 + j for j in range(4)]
            for j, T in enumerate(tiles):
                ti = j
                pt = j % 2

                def tr(e, ti=ti, pt=pt):
                    ins = None
                    for kc in range(8):
                        ins = e.transpose(out=psT[pt][:, kc * 128:(kc + 1) * 128],
                                          in_=xn[ti][:, kc * 128:(kc + 1) * 128], identity=ident_bf[:, :])
                    return ins
                ph.op("pe", tr, reads=[b_xn[ti], b_cbf], writes=[b_psT[pt]])
                ph.op("act", (lambda pt, gs, j: lambda e: e.activation(
                    out=xnT[gs][:, :, j * 128:(j + 1) * 128],
                    in_=psT[pt][:, :].rearrange("p (k t) -> p k t", k=8), func=AF.Copy))(pt, gs, j),
                    reads=[b_psT[pt]], writes=[b_xnT[gs][j]])

        fcount = {"f": 0, "k": 0}

        def stageB(g, gs):
            tiles = [META_T] if g == 8 else [g * 4 + j for j in range(4)]
            nt = len(tiles)
            N = nt * 128
            xb = b_xnT[gs][:nt]
            s = g // 4
            tok0 = (g % 4) * 512
            fm = [(C_NAQ + 128 * i, 128, "q", i) for i in range(4)] + \
                 [(C_NAK + 128 * i, 128, "k", i) for i in range(4)] + \
                 [(C_GQ + 128 * i, 128, "gq", i) for i in range(2)] + \
                 [(C_GK + 128 * i, 128, "gk", i) for i in range(2)] + \
                 [(C_GF, 32, "gl", 0)]
            for (c0, M, kind, i) in fm:
                pi = fcount["f"] % 4
                fcount["f"] += 1

                def mm(e, c0=c0, M=M, pi=pi):
                    ins = None
                    for kc in range(8):
                        ins = e.matmul(out=psF[pi][0:M, 0:N], lhsT=win[:, kc, c0:c0 + M],
                                       rhs=xnT[gs][:, kc, 0:N], start=(kc == 0), stop=(kc == 7))
                    return ins
                ph.op("pe", mm, reads=wb(c0, M) + xb, writes=[b_psF[pi]])
                if kind == "q":
                    ph.op("act", (lambda pi, i: lambda e: e.activation(
                        out=stQK[gs][:, i, 0:N], in_=psF[pi][:, 0:N], func=AF.Copy, scale=0.125))(pi, i),
                        reads=[b_psF[pi]], writes=[b_stQK[gs]])
                elif kind == "k":
                    ph.op("dve", (lambda pi, i: lambda e: e.tensor_copy(
                        out=stQK[gs][:, 4 + i, 0:N], in_=psF[pi][:, 0:N]))(pi, i),
                        reads=[b_psF[pi]], writes=[b_stQK[gs]])
                elif kind == "gq":
                    ph.op("act", (lambda pi, i: lambda e: e.activation(
                        out=gqk[gs][:, i, 0:N], in_=psF[pi][:, 0:N], func=AF.Copy, scale=0.125))(pi, i),
                        reads=[b_psF[pi]], writes=[b_gqk[gs]])
                elif kind == "gk":
                    ph.op("dve", (lambda pi, i: lambda e: e.tensor_copy(
                        out=gqk[gs][:, 2 + i, 0:N], in_=psF[pi][:, 0:N]))(pi, i),
                        reads=[b_psF[pi]], writes=[b_gqk[gs]])
                else:
                    ph.op("dve", (lambda pi: lambda e: e.tensor_copy(
                        out=glT[gs][0:32, 0:N], in_=psF[pi][0:32, 0:N]))(pi),
                        reads=[b_psF[pi]], writes=[b_glT[gs]])
            if g == 8:
                ph.op("pool", lambda e: e.dma_start(
                    out=KTM_d.ap().rearrange("c p t -> p c t"), in_=stQK[gs][:, 4:8, 0:128]),
                    reads=[b_stQK[gs]], dma=b_stQK[gs])
            else:
                ph.op("pool", lambda e: e.dma_start(
                    out=QT_d.ap()[s * 4:(s + 1) * 4, :, tok0:tok0 + 512].rearrange("c p t -> p c t"),
                    in_=stQK[gs][:, 0:4, :]), reads=[b_stQK[gs]], dma=b_stQK[gs])
                ph.op("pool", lambda e: e.dma_start(
                    out=KT_d.ap()[s * 4:(s + 1) * 4, :, tok0:tok0 + 512].rearrange("c p t -> p c t"),
                    in_=stQK[gs][:, 4:8, :]), reads=[b_stQK[gs]], dma=b_stQK[gs])

            def tile_ops(j, T):
                k2 = fcount["k"] % 2
                fcount["k"] += 1
                tsl = slice(j * 128, (j + 1) * 128)

                def tokmm(e, c0, n, pi, j=j):
                    ins = None
                    for kc in range(8):
                        ins = e.matmul(out=psK[pi][:, 0:n], lhsT=xnT[gs][:, kc, j * 128:(j + 1) * 128],
                                       rhs=win[:, kc, c0:c0 + n], start=(kc == 0), stop=(kc == 7))
                    return ins
                pi = fcount["f"] % 4; fcount["f"] += 1
                ph.op("pe", (lambda pi, tk: lambda e: tk(e, C_NAV, 512, pi))(pi, tokmm),
                      reads=wb(C_NAV, 512) + [xb[j]], writes=[b_psK[pi]])
                ph.op("act", (lambda pi, k2: lambda e: e.activation(
                    out=stVA[k2][:, :].rearrange("p (h c) -> p h c", h=8)[:, :, 0:64],
                    in_=psK[pi][:, :].rearrange("p (h c) -> p h c", h=8), func=AF.Copy))(pi, k2),
                    reads=[b_psK[pi]], writes=[b_stVA[k2]])
                ph.op("pool", (lambda k2, T: lambda e: e.dma_start(out=VA_d.ap()[T], in_=stVA[k2][:, :]))(k2, T),
                      reads=[b_stVA[k2]], dma=b_stVA[k2])
                pi = fcount["f"] % 4; fcount["f"] += 1
                ph.op("pe", (lambda pi, tk: lambda e: tk(e, C_GV, 512, pi))(pi, tokmm),
                      reads=wb(C_GV, 512) + [xb[j]], writes=[b_psK[pi]])
                ph.op("dve", (lambda pi, k2: lambda e: e.tensor_copy(out=stGV[k2][:, :], in_=psK[pi][:, :]))(pi, k2),
                      reads=[b_psK[pi]], writes=[b_stGV[k2]])
                ph.op("pool", (lambda k2, T: lambda e: e.dma_start(out=GV_d.ap()[T], in_=stGV[k2][:, :]))(k2, T),
                      reads=[b_stGV[k2]], dma=b_stGV[k2])
                pi = fcount["f"] % 4; fcount["f"] += 1
                ph.op("pe", (lambda pi, tk: lambda e: tk(e, C_GR, 512, pi))(pi, tokmm),
                      reads=wb(C_GR, 512) + [xb[j]], writes=[b_psK[pi]])
                ph.op("act", (lambda pi, k2: lambda e: e.activation(
                    out=ge[k2][:, :], in_=psK[pi][:, :], func=AF.Exp, scale=-1.0))(pi, k2),
                    reads=[b_psK[pi]], writes=[b_ge[k2]])
                ph.op("act", (lambda pi, k2: lambda e: e.activation(
                    out=rraw[k2][:, :], in_=psK[pi][:, :], func=AF.Copy))(pi, k2),
                    reads=[b_psK[pi]], writes=[b_rraw[k2]])
                ph.op("act", (lambda k2: lambda e: e.activation(
                    out=ge[k2][:, :], in_=ge[k2][:, :], func=AF.Ln, bias=1.0))(k2),
                    reads=[b_ge[k2]], writes=[b_ge[k2]])
                ph.op("act", (lambda k2: lambda e: e.activation(
                    out=ge[k2][:, :], in_=ge[k2][:, :], func=AF.Exp, scale=-1.0))(k2),
                    reads=[b_ge[k2]], writes=[b_ge[k2]])
                ph.op("dve", (lambda k2: lambda e: e.tensor_tensor(
                    out=stGR[k2][:, :], in0=rraw[k2][:, :], in1=ge[k2][:, :], op=ALU.mult))(k2),
                    reads=[b_rraw[k2], b_ge[k2]], writes=[b_stGR[k2]])
                ph.op("pool", (lambda k2, T: lambda e: e.dma_start(out=GR_d.ap()[T], in_=stGR[k2][:, :]))(k2, T),
                      reads=[b_stGR[k2]], dma=b_stGR[k2])
                pi = fcount["f"] % 4; fcount["f"] += 1
                ph.op("pe", (lambda pi, tk: lambda e: tk(e, C_GK, 256, pi))(pi, tokmm),
                      reads=wb(C_GK, 256) + [xb[j]], writes=[b_psK[pi]])
                ph.op("act", (lambda pi, k2: lambda e: e.activation(
                    out=ktok[k2][:, :], in_=psK[pi][:, 0:256], func=AF.Copy))(pi, k2),
                    reads=[b_psK[pi]], writes=[b_ktok[k2]])
                pi = fcount["f"] % 4; fcount["f"] += 1
                ph.op("pe", (lambda pi, j: lambda e: e.matmul(
                    out=psK[pi][:, :], lhsT=glT[gs][:, j * 128:(j + 1) * 128], rhs=w2[:, :],
                    start=True, stop=True))(pi, j),
                    reads=[b_glT[gs], b_w2], writes=[b_psK[pi]])
                ph.op("act", (lambda pi, k2: lambda e: e.activation(
                    out=gp[k2][:, :], in_=psK[pi][:, :], func=AF.Exp, scale=-1.0))(pi, k2),
                    reads=[b_psK[pi]], writes=[b_gp[k2]])
                ph.op("act", (lambda k2: lambda e: e.activation(
                    out=gp[k2][:, :], in_=gp[k2][:, :], func=AF.Ln, bias=1.0))(k2),
                    reads=[b_gp[k2]], writes=[b_gp[k2]])

                yield
                def cum(e, k2=k2):
                    ins = None
                    for d_ in range(2):
                        tri = consts[:, K_TRIF:K_TRIF + 128] if d_ == 0 else consts[:, K_TRIB:K_TRIB + 128]
                        for hp in range(2):
                            c0 = d_ * 256 + hp * 128
                            ins = e.matmul(out=psB[:, d_ * 2 + hp, :], lhsT=gp[k2][:, c0:c0 + 128], rhs=tri,
                                           start=True, stop=True)
                    for d_ in range(2):
                        su = consts[:, K_SUF:K_SUF + 128] if d_ == 0 else consts[:, K_SUB:K_SUB + 128]
                        ins = e.matmul(out=psS[:, d_ * 256:(d_ + 1) * 256], lhsT=su,
                                       rhs=gp[k2][:, d_ * 256:(d_ + 1) * 256], start=True, stop=True)
                    return ins
                ph.op("pe", cum, reads=[b_gp[k2], b_consts], writes=[b_psB, b_psS])
                ph.op("act", (lambda k2: lambda e: e.activation(
                    out=Eq[k2][:, :, :], in_=psB[:, :, :], func=AF.Exp))(k2),
                    reads=[b_psB], writes=[b_Eq[k2]])
                ph.op("act", (lambda k2: lambda e: e.activation(
                    out=Ek[k2][:, :, :], in_=psB[:, :, :], func=AF.Exp, scale=-1.0))(k2),
                    reads=[b_psB], writes=[b_Ek[k2]])
                ph.op("act", (lambda k2: lambda e: e.activation(
                    out=Ed[k2][:, :], in_=psS[:, :], func=AF.Exp))(k2),
                    reads=[b_psS], writes=[b_Ed[k2]])
                ph.op("pool", (lambda k2, T: lambda e: e.tensor_copy(
                    out=dec[:, T, 0:2], in_=Eq[k2][:, 0:2, 127]))(k2, T),
                    reads=[b_Eq[k2]], writes=[b_dec])
                ph.op("pool", (lambda k2, T: lambda e: e.tensor_copy(
                    out=dec[:, T, 2:4], in_=Eq[k2][:, 2:4, 0]))(k2, T),
                    reads=[b_Eq[k2]], writes=[b_dec])
                for d_ in range(2):
                    ph.op("dve", (lambda k2, d_: lambda e: e.tensor_tensor(
                        out=stKD[k2][:, d_ * 256:(d_ + 1) * 256], in0=ktok[k2][:, :],
                        in1=Ed[k2][:, d_ * 256:(d_ + 1) * 256], op=ALU.mult))(k2, d_),
                        reads=[b_ktok[k2], b_Ed[k2]], writes=[b_stKD[k2]])
                ph.op("pool", (lambda k2, T: lambda e: e.dma_start(out=GKD_d.ap()[T], in_=stKD[k2][:, :]))(k2, T),
                      reads=[b_stKD[k2]], dma=b_stKD[k2])
                if T != META_T:
                    for d_ in range(2):
                        ph.op("dve", (lambda k2, d_, tsl: lambda e: e.tensor_tensor(
                            out=stG[gs][:, d_ * 2:d_ * 2 + 2, tsl], in0=gqk[gs][:, 0:2, tsl],
                            in1=Eq[k2][:, d_ * 2:d_ * 2 + 2, :], op=ALU.mult))(k2, d_, tsl),
                            reads=[b_gqk[gs], b_Eq[k2]], writes=[b_stG[gs]])
                        ph.op("pool", (lambda k2, d_, tsl: lambda e: e.tensor_tensor(
                            out=stG[gs][:, 4 + d_ * 2:4 + d_ * 2 + 2, tsl], in0=gqk[gs][:, 2:4, tsl],
                            in1=Ek[k2][:, d_ * 2:d_ * 2 + 2, :], op=ALU.mult))(k2, d_, tsl),
                            reads=[b_gqk[gs], b_Ek[k2]], writes=[b_stG[gs]])
            gens = [tile_ops(j, T) for j, T in enumerate(tiles)]
            for j in range(len(gens)):
                next(gens[j])
                if j >= 1:
                    for _ in gens[j - 1]:
                        pass
            for _ in gens[-1]:
                pass
            if g != 8:
                ph.op("pool", lambda e: e.dma_start(
                    out=GQ_d.ap()[s * 4:(s + 1) * 4, :, tok0:tok0 + 512].rearrange("c p t -> p c t"),
                    in_=stG[gs][:, 0:4, :]), reads=[b_stG[gs]], dma=b_stG[gs])
                ph.op("pool", lambda e: e.dma_start(
                    out=GK_d.ap()[s * 4:(s + 1) * 4, :, tok0:tok0 + 512].rearrange("c p t -> p c t"),
                    in_=stG[gs][:, 4:8, :]), reads=[b_stG[gs]], dma=b_stG[gs])

        order = [8, 0, 1, 2, 3, 4, 5, 6, 7]
        stageA1(order[0])
        stageA2(order[0], 0)
        for i, g in enumerate(order):
            if i + 1 < len(order):
                stageA1(order[i + 1])
            stageB(g, i % 2)
            if i + 1 < len(order):
                stageA2(order[i + 1], (i + 1) % 2)
        ph.run()
        st.close()


    def rms_rstd(ph, eng_ss, src_ap_fn, b_src, ssbuf, b_ssb, col, junk_ap_fn, b_junkb, n):
        ph.op("dve", lambda e: e.scalar_tensor_tensor(
            out=junk_ap_fn(), in0=src_ap_fn(), scalar=1.0, in1=src_ap_fn(),
            op0=ALU.mult, op1=ALU.mult, accum_out=ssbuf[:, col:col + 1]),
            reads=[b_src], writes=[b_junkb, b_ssb])
        ph.op("act", lambda e: e.activation(
            out=ssbuf[:, col + 1:col + 2], in_=ssbuf[:, col:col + 1], func=AF.Ln, scale=1.0 / n, bias=EPS),
            reads=[b_ssb], writes=[b_ssb])
        ph.op("act", lambda e: e.activation(
            out=ssbuf[:, col + 1:col + 2], in_=ssbuf[:, col + 1:col + 2], func=AF.Exp, scale=-0.5),
            reads=[b_ssb], writes=[b_ssb])

    def phase2():
        st = ExitStack()

        def sbt(name, shape, dt):
            return st.enter_context(nc.sbuf_tensor("p2_" + name, list(shape), dt))

        def pst(name, shape, dt):
            return st.enter_context(nc.psum_tensor("p2_" + name, list(shape), dt))

        ph = Ph(cx, "p2")
        Sf = sbt("Sf", [128, 32, 2, 256], BF16)
        b_Sf = cx.bufs(32, "Sf")
        Sinit = sbt("Sinit", [128, 2, 256], F32)
        b_Sinit = cx.buf("Sinit")
        S32 = sbt("S32", [128, 2, 256], F32)
        b_S32 = cx.buf("S32")
        Sb32 = sbt("Sb32", [128, 2, 256], F32)
        b_Sb32 = cx.buf("Sb32")
        Sbb = [sbt("Sbb%d" % i, [128, 2, 256], BF16) for i in range(2)]
        b_Sbb = cx.bufs(2, "Sbb")
        ggla = sbt("ggla", [128, 128], F32)
        b_ggla = cx.buf("ggla")
        kd = [sbt("kd%d" % i, [128, 512], BF16) for i in range(3)]
        b_kd = cx.bufs(3, "kd")
        gv = [sbt("gv%d" % i, [128, 512], BF16) for i in range(3)]
        b_gv = cx.bufs(3, "gv")
        gr = [sbt("gr%d" % i, [128, 512], BF16) for i in range(3)]
        b_gr = cx.bufs(3, "gr")
        gqs = [sbt("gqs%d" % i, [128, 4, SEQ], BF16) for i in range(2)]
        b_gqs = cx.bufs(2, "gqs")
        gks = [sbt("gks%d" % i, [128, 4, SEQ], BF16) for i in range(2)]
        b_gks = cx.bufs(2, "gks")
        ATp = [sbt("ATp%d" % i, [128, 2, 2, 128], BF16) for i in range(2)]
        b_ATp = cx.bufs(2, "ATp")
        osb = [sbt("osb%d" % i, [128, 512], F32) for i in range(2)]
        b_osb = cx.bufs(2, "osb")
        jk = sbt("jk", [128, 512], BF16)
        b_jk = cx.bufs(4, "jk")
        ssg = [sbt("ssg%d" % i, [128, 8], F32) for i in range(2)]
        b_ssg = cx.bufs(2, "ssg")
        t2 = [sbt("t2_%d" % i, [128, 512], F32) for i in range(2)]
        b_t2 = cx.bufs(2, "t2")
        gout = [sbt("gout%d" % i, [128, 512], BF16) for i in range(2)]
        b_gout = cx.bufs(2, "gout")
        mst = [sbt("mst%d" % i, [128, 4, 128], BF16) for i in range(2)]
        b_mst = cx.bufs(2, "mst")

        psA = [pst("psA%d" % i, [128, 512], F32) for i in range(2)]
        b_psA = cx.bufs(2, "psA")
        psO = [pst("psO%d" % i, [128, 512], F32) for i in range(2)]
        b_psO = cx.bufs(2, "psO")
        psKVt = [pst("psKV%d" % i, [128, 512], F32) for i in range(2)]
        b_psKV = cx.bufs(2, "psKV")
        psT = pst("psT", [128, 1024], BF16)
        b_psT = cx.buf("psT")
        psKV1 = pst("psKVs", [128, 512], F32)
        b_psKV1 = cx.buf("psKV1")

        ph.op("sp", lambda e: e.dma_start(out=ggla[:, :], in_=ggla_d.ap().broadcast_to([128, 128])),
              writes=[b_ggla], dma=b_ggla)

        for s_ in range(2):
            ph.op("sp", (lambda s_: lambda e: e.dma_start(
                out=gqs[s_][:, :, :], in_=GQ_d.ap()[s_ * 4:(s_ + 1) * 4].rearrange("c p t -> p c t")))(s_),
                writes=[b_gqs[s_]], dma=b_gqs[s_])
            ph.op("sp", (lambda s_: lambda e: e.dma_start(
                out=gks[s_][:, :, :], in_=GK_d.ap()[s_ * 4:(s_ + 1) * 4].rearrange("c p t -> p c t")))(s_),
                writes=[b_gks[s_]], dma=b_gks[s_])
        cnt = {"ld": 0, "at": 0}
        kd1 = [sbt("kd1_%d" % i, [128, 512], BF16) for i in range(2)]
        b_kd1 = cx.bufs(2, "kd1")
        gv1 = [sbt("gv1_%d" % i, [128, 512], BF16) for i in range(2)]
        b_gv1 = cx.bufs(2, "gv1")

        def load_kv(T):
            i = cnt["ld"] % 2
            cnt["ld"] += 1
            ph.op("sp", lambda e: e.dma_start(out=kd1[i][:, :], in_=GKD_d.ap()[T]), writes=[b_kd1[i]], dma=b_kd1[i])
            ph.op("sp", lambda e: e.dma_start(out=gv1[i][:, :], in_=GV_d.ap()[T]), writes=[b_gv1[i]], dma=b_gv1[i])
            return i

        def kv_mm1(i):
            def f(e):
                ins = None
                for hp in range(2):
                    ins = e.matmul(out=psKV1[:, hp * 256:(hp + 1) * 256], lhsT=kd1[i][:, hp * 128:hp * 128 + 128],
                                   rhs=gv1[i][:, hp * 256:(hp + 1) * 256], start=True, stop=True)
                return ins
            return f

        def kv_mm(i, hp, d_):
            return lambda e: e.matmul(out=psKVt[hp][:, 0:256], lhsT=kd[i][:, d_ * 256 + hp * 128:d_ * 256 + hp * 128 + 128],
                                      rhs=gv[i][:, hp * 256:(hp + 1) * 256], start=True, stop=True)

        i = load_kv(META_T)
        ph.op("pe", kv_mm1(i), reads=[b_kd1[i], b_gv1[i]], writes=[b_psKV1])
        ph.op("act", lambda e: e.activation(
            out=Sinit[:, :, :], in_=psKV1[:, :].rearrange("p (h c) -> p h c", h=2), func=AF.Copy),
            reads=[b_psKV1], writes=[b_Sinit])

        S32pp = [S32, sbt("S32b", [128, 2, 256], F32)]
        b_S32pp = [b_S32, cx.buf("S32b")]

        def sweep1_step(s, t):
            T = s * 16 + t
            src, dst = S32pp[t % 2], S32pp[(t + 1) % 2]
            bsrc, bdst = b_S32pp[t % 2], b_S32pp[(t + 1) % 2]
            if t == 0:
                ph.op("pool", lambda e: e.tensor_copy(out=src[:, :, :], in_=Sinit[:, :, :]),
                      reads=[b_Sinit], writes=[bsrc])
            ph.op("act", lambda e: e.activation(out=Sf[:, T, :, :], in_=src[:, :, :], func=AF.Copy),
                  reads=[bsrc], writes=[b_Sf[T]])
            if t == 15:
                return
            i = load_kv(T)
            ph.op("pe", kv_mm1(i), reads=[b_kd1[i], b_gv1[i]], writes=[b_psKV1])
            for hp in range(2):
                ph.op("dve", (lambda hp: lambda e: e.scalar_tensor_tensor(
                    out=dst[:, hp, :], in0=src[:, hp, :], scalar=dec[:, T, hp:hp + 1],
                    in1=psKV1[:, hp * 256:(hp + 1) * 256], op0=ALU.mult, op1=ALU.add))(hp),
                    reads=[bsrc, b_psKV1, b_dec], writes=[bdst])

        for t in range(16):
            sweep1_step(0, t)
        tiles2 = [(s, t) for s in range(2) for t in range(15, -1, -1)]

        def part1(k):
            s, t = tiles2[k]
            T = s * 16 + t
            cur = k % 2
            nxt = 1 - cur
            if t == 15:
                ph.op("pool", lambda e: e.memset(Sb32[:, :, :], 0.0), writes=[b_Sb32])
                ph.op("pool", lambda e: e.memset(Sbb[cur][:, :, :], 0.0), writes=[b_Sbb[cur]])
            i = k % 3
            ph.op("sp", lambda e: e.dma_start(out=kd[i][:, :], in_=GKD_d.ap()[T]), writes=[b_kd[i]], dma=b_kd[i])
            ph.op("sp", lambda e: e.dma_start(out=gv[i][:, :], in_=GV_d.ap()[T]), writes=[b_gv[i]], dma=b_gv[i])
            ph.op("sp", lambda e: e.dma_start(out=gr[i][:, :], in_=GR_d.ap()[T]), writes=[b_gr[i]], dma=b_gr[i])
            ph.op("pool", lambda e: e.tensor_tensor(
                out=t2[k % 2][:, :].rearrange("p (h c) -> p h c", h=4),
                in0=gr[i][:, :].rearrange("p (h c) -> p h c", h=4),
                in1=_ap(ggla, 0, [[128, 128], [0, 4], [1, 128]]), op=ALU.mult),
                reads=[b_gr[i], b_ggla], writes=[b_t2[k % 2]])
            po = k % 2
            tsl = slice(t * 128, (t + 1) * 128)

            def rec_amm_pair(hp):
                def amm(e):
                    ins = None
                    for d_ in range(2):
                        c = d_ * 2 + hp
                        for hl in range(2):
                            base = 64 * hl
                            ins = e.matmul(out=psA[hl][:, hp * 256 + d_ * 128:hp * 256 + (d_ + 1) * 128],
                                           lhsT=gks[s][base:base + 64, c, tsl], rhs=gqs[s][base:base + 64, c, tsl],
                                           start=True, stop=True)
                    return ins
                ph.op("pe", amm, reads=[b_gks[s], b_gqs[s]], writes=[b_psA[0], b_psA[1]])

            def rec_masks():
                for hl in range(2):
                    ph.op("dve", (lambda hl: lambda e: e.tensor_tensor(
                        out=ATp[hl][:, :, :, :].rearrange("p h d i -> p h (d i)"),
                        in0=psA[hl][:, :].rearrange("p (h x) -> p h x", h=2),
                        in1=_ap(maskFB, 0, [[256, 128], [0, 2], [1, 256]]), op=ALU.mult))(hl),
                        reads=[b_psA[hl], b_cbf], writes=[b_ATp[hl]])

            def rec_omm(h):
                hp, hl = h // 2, h % 2
                base = 64 * hl

                def omm(e):
                    o = psO[po][:, h * 128:(h + 1) * 128]
                    vv = gv[i][:, h * 128:(h + 1) * 128]
                    sc = hl * 128
                    e.matmul(out=o, lhsT=ATp[hl][:, hp, 0, :], rhs=vv, start=True, stop=False)
                    e.matmul(out=o, lhsT=gqs[s][base:base + 64, hp, tsl],
                             rhs=Sf[base:base + 64, T, hp, sc:sc + 128], start=False, stop=False)
                    e.matmul(out=o, lhsT=ATp[hl][:, hp, 1, :], rhs=vv, start=False, stop=False)
                    return e.matmul(out=o, lhsT=gqs[s][base:base + 64, 2 + hp, tsl],
                                    rhs=Sbb[cur][base:base + 64, hp, sc:sc + 128], start=False, stop=True)
                ph.op("pe", omm, reads=[b_ATp[hl], b_gv[i], b_gqs[s], b_Sf[T], b_Sbb[cur]],
                      writes=[b_psO[po]])

            if t > 0:
                for hp in range(2):
                    ph.op("pe", kv_mm(i, hp, 1), reads=[b_kd[i], b_gv[i]], writes=[b_psKV[hp]])
                    ph.op("dve", (lambda hp: lambda e: e.scalar_tensor_tensor(
                        out=Sb32[:, hp, :], in0=Sb32[:, hp, :], scalar=dec[:, T, 2 + hp:3 + hp], in1=psKVt[hp][:, 0:256],
                        op0=ALU.mult, op1=ALU.add))(hp),
                        reads=[b_Sb32, b_psKV[hp], b_dec], writes=[b_Sb32])
                ph.op("act", lambda e: e.activation(out=Sbb[nxt][:, :, :], in_=Sb32[:, :, :], func=AF.Copy),
                      reads=[b_Sb32], writes=[b_Sbb[nxt]])
            rec_amm_pair(0)
            rec_amm_pair(1)
            rec_masks()
            for h in range(4):
                rec_omm(h)
        def part2(k):
            po = k % 2
            go = k % 2
            ob = k % 2
            ph.op("act", lambda e: e.activation(out=osb[ob][:, :], in_=psO[po][:, :], func=AF.Copy),
                  reads=[b_psO[po]], writes=[b_osb[ob]])
            for h in range(4):
                ph.op("dve", (lambda h: lambda e: e.scalar_tensor_tensor(
                    out=jk[:, h * 128:(h + 1) * 128], in0=osb[ob][:, h * 128:(h + 1) * 128], scalar=1.0,
                    in1=osb[ob][:, h * 128:(h + 1) * 128], op0=ALU.mult, op1=ALU.mult,
                    accum_out=ssg[ob][:, h:h + 1]))(h),
                    reads=[b_osb[ob]], writes=[b_jk[h], b_ssg[ob]])
            ph.op("act", lambda e: e.activation(out=ssg[ob][:, 4:8], in_=ssg[ob][:, 0:4], func=AF.Ln,
                                                scale=1.0 / 128, bias=EPS), reads=[b_ssg[ob]], writes=[b_ssg[ob]])
            ph.op("act", lambda e: e.activation(out=ssg[ob][:, 4:8], in_=ssg[ob][:, 4:8], func=AF.Exp, scale=-0.5),
                  reads=[b_ssg[ob]], writes=[b_ssg[ob]])
            ph.op("pool", lambda e: e.tensor_tensor(
                out=osb[ob][:, :].rearrange("p (h c) -> p h c", h=4),
                in0=osb[ob][:, :].rearrange("p (h c) -> p h c", h=4),
                in1=_ap(ssg[ob], 4, [[8, 128], [1, 4], [0, 128]]), op=ALU.mult),
                reads=[b_osb[ob], b_ssg[ob]], writes=[b_osb[ob]])
            ph.op("pool", lambda e: e.tensor_tensor(
                out=gout[go][:, :], in0=osb[ob][:, :], in1=t2[k % 2][:, :], op=ALU.mult),
                reads=[b_osb[ob], b_t2[k % 2]], writes=[b_gout[go]])

        def part3(k):
            s, t = tiles2[k]
            go = k % 2

            def trg(e):
                ins = None
                for c in range(4):
                    ins = e.transpose(out=psT[:, c * 128:(c + 1) * 128], in_=gout[go][:, c * 128:(c + 1) * 128],
                                      identity=ident_bf[:, :])
                return ins
            ph.op("pe", trg, reads=[b_gout[go], b_cbf], writes=[b_psT])
            ph.op("act", lambda e: e.activation(
                out=mst[go][:, :, :], in_=psT[:, 0:512].rearrange("p (c t) -> p c t", c=4), func=AF.Copy),
                reads=[b_psT], writes=[b_mst[go]])
            ph.op("pool", lambda e: e.dma_start(
                out=MIXT_d.ap()[s, 4:8, :, t * 128:(t + 1) * 128].rearrange("c p t -> p c t"),
                in_=mst[go][:, :, :]), reads=[b_mst[go]], dma=b_mst[go])

        n2 = 0 if P2_CUT in (1, 3, 4) else len(tiles2)
        for k in range(n2 + 2):
            if k < 16:
                sweep1_step(1, k)
            if k < n2:
                part1(k)
            if 0 <= k - 1 < n2:
                part2(k - 1)
            if 0 <= k - 2 < n2:
                part3(k - 2)
        ph.run()
        st.close()

    def phase3(wload):
        st = ExitStack()

        def sbt(name, shape, dt):
            return st.enter_context(nc.sbuf_tensor("p3_" + name, list(shape), dt))

        def pst(name, shape, dt):
            return st.enter_context(nc.psum_tensor("p3_" + name, list(shape), dt))

        ph = Ph(cx, "p3")
        EB = sbt("EB", [128, 8, NCFG, 128], BF16)
        b_EB = cx.bufs(8, "EB")
        tstage = [sbt("tstage%d" % i, [128, NCFG, 128], F32) for i in range(2)]
        b_tstage = cx.bufs(2, "tstage")
        mbias = sbt("mbias", [64, 8], F32)
        b_mbias = cx.buf("mbias")
        qT = sbt("qT", [128, 4, SEQ], BF16)
        b_qT = cx.bufs(4, "qT")
        kT = sbt("kT", [128, 4, SEQ], BF16)
        b_kT = cx.bufs(4, "kT")
        va = sbt("va", [128, 16, 520], BF16)
        b_va = cx.bufs(4, "va")
        ktm = sbt("ktm", [128, 4, 64], BF16)
        b_ktm = cx.buf("ktm")
        vam = sbt("vam", [64, 520], BF16)
        b_vam = cx.buf("vam")
        P = [sbt("P%d" % i, [128, 5, 128], BF16) for i in range(4)]
        b_P = cx.bufs(4, "P")
        Pm = [sbt("Pm%d" % i, [64, 128], BF16) for i in range(4)]
        b_Pm = cx.bufs(4, "Pm")
        rden = sbt("rden", [128, 8], F32)
        b_rden = cx.buf("rden")
        nao = [sbt("nao%d" % i, [128, 512], BF16) for i in range(2)]
        b_nao = cx.bufs(2, "nao")
        mst = [sbt("mst%d" % i, [128, 4, 128], BF16) for i in range(2)]
        b_mst = cx.bufs(2, "mst")

        psS = [pst("psS%d" % i, [128, 8, 128], F32) for i in range(2)]
        b_psS = cx.bufs(2, "psS")
        psO = [pst("psO%d" % i, [128, 4, 128], F32) for i in range(2)]
        b_psO = cx.bufs(2, "psO")
        psT = pst("psT", [128, 1024], BF16)
        b_psT = cx.buf("psT")

        def prep_tables():
          for h in range(8):
            i = h % 2
            ph.op("sp", (lambda i, h: lambda e: e.dma_start(
                out=tstage[i][:, :, :], in_=tbl_d.ap()[h].rearrange("c k q -> k c q")))(i, h),
                writes=[b_tstage[i]], dma=b_tstage[i])
            ph.op("act", (lambda i, h: lambda e: e.activation(
                out=EB[:, h, :, :], in_=tstage[i][:, :, :], func=AF.Exp))(i, h),
                reads=[b_tstage[i]], writes=[b_EB[h]])
        ph.op("pool", lambda e: e.memset(mbias[:, :], NEG), writes=[b_mbias])
        for h in range(8):
            ph.op("sp", (lambda h: lambda e: e.dma_start(
                out=mbias[48:64, h:h + 1], in_=mb_d.ap()[h:h + 1, :].rearrange("o m -> m o")))(h),
                writes=[b_mbias], dma=b_mbias)
        ph.op("sp", lambda e: e.dma_start(out=ktm[:, :, :],
                                          in_=KTM_d.ap()[:, :, 64:128].rearrange("c p t -> p c t")),
              writes=[b_ktm], dma=b_ktm)
        ph.op("sp", lambda e: e.dma_start(out=vam[:, :], in_=VA_d.ap()[META_T, 64:128, :]),
              writes=[b_vam], dma=b_vam)

        units = []
        for s in range(2):
            for t in range(16):
                if t < 2:
                    c0, nch, slot0 = 0, 4, (8 if t == 0 else 7)
                elif t >= 14:
                    c0, nch, slot0 = 12, 4, (6 if t == 14 else 5)
                else:
                    c0, nch, slot0 = t - 2, 5, 0
                for h in range(8):
                    units.append((s, t, h, c0, nch, slot0))

        def load_grp(s, g):
            gs_ = slice(g * 512, (g + 1) * 512)
            ph.op("sp", lambda e: e.dma_start(
                out=qT[:, :, gs_], in_=QT_d.ap()[s * 4:(s + 1) * 4, :, gs_].rearrange("c p t -> p c t")),
                writes=[b_qT[g]], dma=b_qT[g])
            ph.op("sp", lambda e: e.dma_start(
                out=kT[:, :, gs_], in_=KT_d.ap()[s * 4:(s + 1) * 4, :, gs_].rearrange("c p t -> p c t")),
                writes=[b_kT[g]], dma=b_kT[g])
            ph.op("sp", lambda e: e.dma_start(
                out=va[:, g * 4:(g + 1) * 4, :],
                in_=VA_d.ap()[s * 16 + g * 4:s * 16 + (g + 1) * 4].rearrange("c p t -> p c t")),
                writes=[b_va[g]], dma=b_va[g])

        def stage1_pair(ua, ub):
            s, t, ha, c0, nch, slot0 = units[ua]
            hb_ = units[ub][2]
            assert units[ub][1] == t and ha % 2 == 0 and hb_ == ha + 1 and ua % 2 == 0
            hp = ha // 2

            def smm2(e):
                qa = qT[0:64, hp, t * 128:(t + 1) * 128]
                qb = qT[64:128, hp, t * 128:(t + 1) * 128]
                for ci in range(nch):
                    c = c0 + ci
                    e.matmul(out=psS[0][:, ci, :], lhsT=kT[0:64, hp, c * 128:(c + 1) * 128], rhs=qa,
                             start=True, stop=True)
                    e.matmul(out=psS[1][:, ci, :], lhsT=kT[64:128, hp, c * 128:(c + 1) * 128], rhs=qb,
                             start=True, stop=True)
                e.matmul(out=psS[0][0:64, 5, :], lhsT=ktm[0:64, hp, :], rhs=qa, start=True, stop=True)
                return e.matmul(out=psS[1][0:64, 5, :], lhsT=ktm[64:128, hp, :], rhs=qb, start=True, stop=True)
            kgrps = sorted(set((c0 + ci) // 4 for ci in range(nch)))
            ph.op("pe", smm2, reads=[b_qT[t // 4], b_ktm] + [b_kT[g_] for g_ in kgrps],
                  writes=[b_psS[0], b_psS[1]])
            for ui, pi, h in ((ua, 0, ha), (ub, 1, hb_)):
                p4 = ui % 4
                ph.op("act", (lambda pi, p4: lambda e: e.activation(
                    out=P[p4][:, 0:nch, :], in_=psS[pi][:, 0:nch, :], func=AF.Exp))(pi, p4),
                    reads=[b_psS[pi]], writes=[b_P[p4]])
                ph.op("act", (lambda pi, p4, h: lambda e: e.activation(
                    out=Pm[p4][:, :], in_=psS[pi][0:64, 5, :], func=AF.Exp, bias=mbias[:, h:h + 1]))(pi, p4, h),
                    reads=[b_psS[pi], b_mbias], writes=[b_Pm[p4]])
                ph.op("dve", (lambda p4, h: lambda e: e.tensor_tensor(
                    out=P[p4][:, 0:nch, :], in0=P[p4][:, 0:nch, :], in1=EB[:, h, slot0:slot0 + nch, :],
                    op=ALU.mult))(p4, h), reads=[b_P[p4], b_EB[h]], writes=[b_P[p4]])

        def stage2(ui):
            s, t, h, c0, nch, slot0 = units[ui]
            pi = ui % 4

            def pvm(e):
                o = psO[h // 4][:, h % 4, 0:65]
                for ci in range(nch):
                    e.matmul(out=o, lhsT=P[pi][:, ci, :], rhs=va[:, c0 + ci, h * 65:(h + 1) * 65],
                             start=(ci == 0), stop=False)
                return e.matmul(out=o, lhsT=Pm[pi][:, :], rhs=vam[:, h * 65:(h + 1) * 65],
                                start=False, stop=True)
            kgrps = sorted(set((c0 + ci) // 4 for ci in range(nch)))
            ph.op("pe", pvm, reads=[b_P[pi], b_Pm[pi], b_vam] + [b_va[g_] for g_ in kgrps], writes=[b_psO[h // 4]])
            if h == 7:
                k2 = (ui // 8) % 2
                for hb in range(2):
                    ph.op("dve", (lambda hb: lambda e: e.reciprocal(
                        out=rden[:, hb * 4:(hb + 1) * 4], in_=psO[hb][:, :, 64]))(hb),
                        reads=[b_psO[hb]], writes=[b_rden])
                    ph.op("dve", (lambda hb: lambda e: e.tensor_tensor(
                        out=nao[k2][:, hb * 256:(hb + 1) * 256].rearrange("p (h c) -> p h c", h=4),
                        in0=psO[hb][:, :, 0:64],
                        in1=_ap(rden, hb * 4, [[8, 128], [1, 4], [0, 64]]), op=ALU.mult))(hb),
                        reads=[b_psO[hb], b_rden], writes=[b_nao[k2]])

        def stage3(ui):
            s, t, h, c0, nch, slot0 = units[ui]
            if h != 7:
                return
            k2 = (ui // 8) % 2

            def trn(e):
                ins = None
                for c in range(4):
                    ins = e.transpose(out=psT[:, c * 128:(c + 1) * 128], in_=nao[k2][:, c * 128:(c + 1) * 128],
                                      identity=ident_bf[:, :])
                return ins
            ph.op("pe", trn, reads=[b_nao[k2], b_cbf], writes=[b_psT])
            ph.op("act", lambda e: e.activation(
                out=mst[k2][:, :, :], in_=psT[:, 0:512].rearrange("p (c t) -> p c t", c=4), func=AF.Copy),
                reads=[b_psT], writes=[b_mst[k2]])
            ph.op("act", lambda e: e.dma_start(
                out=MIXT_d.ap()[s, 0:4, :, t * 128:(t + 1) * 128].rearrange("c p t -> p c t"),
                in_=mst[k2][:, :, :]), reads=[b_mst[k2]], dma=b_mst[k2])

        nu = len(units)
        load_grp(0, 0)
        prep_tables()
        for g_ in range(1, 4):
            load_grp(0, g_)
        for s in range(2):
            us = [ui for ui in range(nu) if units[ui][0] == s]
            n = len(us)
            np_ = n // 2
            for p in range(np_ + 3):
                k = 2 * p
                if s == 0 and k == 24:
                    wload(ph, b_psT)
                if s == 0:
                    for g_ in range(4):
                        if k == min(n + 2, (4 * g_ + 7) * 8 + 2):
                            load_grp(1, g_)
                if p < np_:
                    stage1_pair(us[2 * p], us[2 * p + 1])
                if 0 <= p - 1 < np_:
                    stage2(us[2 * (p - 1)])
                    stage2(us[2 * (p - 1) + 1])
                if 0 <= p - 2 < np_:
                    stage3(us[2 * (p - 2)])
                    stage3(us[2 * (p - 2) + 1])
        ph.run()
        st.close()

    def phase4a(wout, b_wout, wload):
        st = ExitStack()

        def sbt(name, shape, dt):
            return st.enter_context(nc.sbuf_tensor("p4a_" + name, list(shape), dt))

        def pst(name, shape, dt):
            return st.enter_context(nc.psum_tensor("p4a_" + name, list(shape), dt))

        ph = Ph(cx, "p4a")
        gffn = sbt("gffn", [128, D], F32)
        b_gffn = cx.buf("gffn")
        mixt = [sbt("mixt%d" % i, [128, 8, 128], BF16) for i in range(2)]
        b_mixt = cx.bufs(2, "mixt")
        xt = [sbt("xt%d" % i, [128, D], F32) for i in range(2)]
        b_xt = cx.bufs(2, "xt")
        h1 = [sbt("h1%d" % i, [128, D], F32) for i in range(2)]
        b_h1 = cx.bufs(2, "h1")
        junk = sbt("junk", [128, D], BF16)
        b_junk = cx.buf("junk")
        ss = sbt("ss", [128, 4], F32)
        b_ss = cx.bufs(2, "ss")
        hn = [sbt("hn%d" % i, [128, D], BF16) for i in range(2)]
        b_hn = cx.bufs(2, "hn")
        hst = [sbt("hst%d" % i, [128, 8, 128], BF16) for i in range(2)]
        b_hst = cx.bufs(2, "hst")
        psH = [pst("psH%d" % i, [128, D], F32) for i in range(2)]
        b_psH = cx.bufs(2, "psH")
        psT = [pst("psT%d" % i, [128, D], BF16) for i in range(2)]
        b_psT = cx.bufs(2, "psT")

        ph.op("sp", lambda e: e.dma_start(out=gffn[:, :], in_=gffn_d.ap().broadcast_to([128, D])),
              writes=[b_gffn], dma=b_gffn)
        def partA(T):
            s, t = T // 16, T % 16
            i = T % 2
            ph.op("sp", lambda e: e.dma_start(
                out=mixt[i][:, :, :],
                in_=MIXT_d.ap()[s, :, :, t * 128:(t + 1) * 128].rearrange("c p t -> p c t")),
                writes=[b_mixt[i]], dma=b_mixt[i])
            ph.op("sp", lambda e: e.dma_start(
                out=xt[i][:, :], in_=x_d.ap()[s, t * 128:(t + 1) * 128, :]),
                writes=[b_xt[i]], dma=b_xt[i])

            def mm(e):
                ins = None
                for n in range(2):
                    for kc in range(8):
                        ins = e.matmul(out=psH[i][:, n * 512:(n + 1) * 512], lhsT=mixt[i][:, kc, :],
                                       rhs=wout[:, kc, n * 512:(n + 1) * 512], start=(kc == 0), stop=(kc == 7))
                return ins
            ph.op("pe", mm, reads=[b_mixt[i]] + b_wout, writes=[b_psH[i]])
            ph.op("dve", lambda e: e.tensor_tensor(
                out=h1[i][:, :], in0=psH[i][:, :], in1=xt[i][:, :], op=ALU.add),
                reads=[b_psH[i], b_xt[i]], writes=[b_h1[i]])
            ph.op("act", lambda e: e.dma_start(out=H1_d.ap()[T], in_=h1[i][:, :]),
                  reads=[b_h1[i]], dma=b_h1[i])
            rms_rstd(ph, "dve", lambda: h1[i][:, :], b_h1[i], ss, b_ss[i], i * 2,
                     lambda: junk[:, :], b_junk, D)
            ph.op("dve", lambda e: e.scalar_tensor_tensor(
                out=hn[i][:, :], in0=h1[i][:, :], scalar=ss[:, i * 2 + 1:i * 2 + 2], in1=gffn[:, :],
                op0=ALU.mult, op1=ALU.mult),
                reads=[b_h1[i], b_ss[i], b_gffn], writes=[b_hn[i]])

        def partB(T):
            g, j = T // 4, T % 4
            i = T % 2

            def tr(e):
                ins = None
                for kc in range(8):
                    ins = e.transpose(out=psT[i][:, kc * 128:(kc + 1) * 128],
                                      in_=hn[i][:, kc * 128:(kc + 1) * 128], identity=ident_bf[:, :])
                return ins
            ph.op("pe", tr, reads=[b_hn[i], b_cbf], writes=[b_psT[i]])
            ph.op("act", lambda e: e.activation(
                out=hst[i][:, :, :], in_=psT[i][:, :].rearrange("p (k t) -> p k t", k=8), func=AF.Copy),
                reads=[b_psT[i]], writes=[b_hst[i]])
            ph.op("act", lambda e: e.dma_start(
                out=HNT_d.ap()[g].rearrange("p (k t) -> p k t", k=8)[:, :, j * 128:(j + 1) * 128],
                in_=hst[i][:, :, :]), reads=[b_hst[i]], dma=b_hst[i])

        for T in range(33):
            if T == 3:
                wload(ph, b_psH[0])
            if T < 32:
                partA(T)
            if T >= 1:
                partB(T - 1)
        ph.run()
        st.close()

    def phase4b(wg, wu, wd, b_wg, b_wu, b_wd):
        st = ExitStack()

        def sbt(name, shape, dt):
            return st.enter_context(nc.sbuf_tensor("p4b_" + name, list(shape), dt))

        def pst(name, shape, dt):
            return st.enter_context(nc.psum_tensor("p4b_" + name, list(shape), dt))

        ph = Ph(cx, "p4b")
        gfin = sbt("gfin", [128, D], F32)
        b_gfin = cx.buf("gfin")
        hnT = [sbt("hnT%d" % i, [128, 8, 512], BF16) for i in range(2)]
        b_hnT = cx.bufs(2, "hnT")
        actT = sbt("actT", [128, NFF, 512], BF16)
        b_actT = cx.bufs(NFF, "actT")
        sg = [sbt("sg%d" % i, [128, 512], F32) for i in range(2)]
        b_sg = cx.bufs(2, "sg")
        h1 = [sbt("h1%d" % i, [128, D], F32) for i in range(2)]
        b_h1 = cx.bufs(2, "h1")
        junk_ap = sg[0][:, :].bitcast(BF16)
        b_junk = b_sg[0]
        ss = sbt("ss", [128, 4], F32)
        b_ss = cx.bufs(2, "ss")
        psG = [pst("psG%d" % i, [128, 512], F32) for i in range(2)]
        b_psG = cx.bufs(2, "psG")
        psU = [pst("psU%d" % i, [128, 512], F32) for i in range(2)]
        b_psU = cx.bufs(2, "psU")
        psD = [pst("psD%d" % i, [128, D], F32) for i in range(2)]
        b_psD = cx.bufs(2, "psD")

        ph.op("sp", lambda e: e.dma_start(out=gfin[:, :], in_=gfin_d.ap().broadcast_to([128, D])),
              writes=[b_gfin], dma=b_gfin)
        fc = 0
        tcn = 0
        for g in range(8):
            gi = g % 2
            ph.op("sp", (lambda gi, g: lambda e: e.dma_start(
                out=hnT[gi][:, :, :], in_=HNT_d.ap()[g].rearrange("p (k t) -> p k t", k=8)))(gi, g),
                writes=[b_hnT[gi]], dma=b_hnT[gi])
            for f in range(NFF):
                pi = fc % 2
                fc += 1

                def gmm(e, pi=pi, f=f, gi=gi):
                    ins = None
                    for kc in range(8):
                        ins = e.matmul(out=psG[pi][:, :], lhsT=wg[:, kc, f * 128:(f + 1) * 128],
                                       rhs=hnT[gi][:, kc, :], start=(kc == 0), stop=(kc == 7))
                    return ins

                def umm(e, pi=pi, f=f, gi=gi):
                    ins = None
                    for kc in range(8):
                        ins = e.matmul(out=psU[pi][:, :], lhsT=wu[:, kc, f * 128:(f + 1) * 128],
                                       rhs=hnT[gi][:, kc, :], start=(kc == 0), stop=(kc == 7))
                    return ins
                ph.op("pe", gmm, reads=[b_hnT[gi]] + b_wg, writes=[b_psG[pi]])
                ph.op("pe", umm, reads=[b_hnT[gi]] + b_wu, writes=[b_psU[pi]])
                ph.op("act", (lambda pi: lambda e: e.activation(out=sg[pi][:, :], in_=psG[pi][:, :], func=AF.Silu))(pi),
                      reads=[b_psG[pi]], writes=[b_sg[pi]])
                ph.op("dve", (lambda pi, f: lambda e: e.tensor_tensor(
                    out=actT[:, f, :], in0=psU[pi][:, :], in1=sg[pi][:, :], op=ALU.mult))(pi, f),
                    reads=[b_psU[pi], b_sg[pi]], writes=[b_actT[f]])
            for j in range(4):
                T = g * 4 + j
                s, t = T // 16, T % 16
                i = tcn % 2
                tcn += 1
                ph.op("sp", (lambda i, T: lambda e: e.dma_start(out=h1[i][:, :], in_=H1_d.ap()[T]))(i, T),
                      writes=[b_h1[i]], dma=b_h1[i])

                def dmm(e, i=i, j=j):
                    ins = None
                    for n in range(2):
                        for f in range(NFF):
                            ins = e.matmul(out=psD[i][:, n * 512:(n + 1) * 512], lhsT=actT[:, f, j * 128:(j + 1) * 128],
                                           rhs=wd[:, f, n * 512:(n + 1) * 512], start=(f == 0), stop=(f == NFF - 1))
                    return ins
                ph.op("pe", dmm, reads=b_actT + b_wd, writes=[b_psD[i]])
                ph.op("dve", (lambda i: lambda e: e.tensor_tensor(
                    out=h1[i][:, :], in0=psD[i][:, :], in1=h1[i][:, :], op=ALU.add))(i),
                    reads=[b_psD[i], b_h1[i]], writes=[b_h1[i]])
                rms_rstd(ph, "dve", (lambda i: lambda: h1[i][:, :])(i), b_h1[i], ss, b_ss[i], i * 2,
                         lambda: junk_ap, b_junk, D)
                ph.op("dve", (lambda i: lambda e: e.scalar_tensor_tensor(
                    out=h1[i][:, :], in0=h1[i][:, :], scalar=ss[:, i * 2 + 1:i * 2 + 2], in1=gfin[:, :],
                    op0=ALU.mult, op1=ALU.mult))(i),
                    reads=[b_h1[i], b_ss[i], b_gfin], writes=[b_h1[i]])
                ph.op("pool", (lambda i, s, t: lambda e: e.dma_start(
                    out=out_d.ap()[s, t * 128:(t + 1) * 128, :], in_=h1[i][:, :]))(i, s, t),
                    reads=[b_h1[i]], dma=b_h1[i])
        ph.run()
        st.close()

    phase1()
    if LAST_PHASE >= 2:
        phase2()
    if LAST_PHASE >= 3:
        wst = ExitStack()
        wst_o = ExitStack()
        wg = wst.enter_context(nc.sbuf_tensor("w_wg", [128, 8, DFF], BF16))
        wu = wst.enter_context(nc.sbuf_tensor("w_wu", [128, 8, DFF], BF16))
        wout = wst_o.enter_context(nc.sbuf_tensor("w_wout", [128, 8, D], BF16))
        b_wout = cx.bufs(8, "wout")
        b_wg = cx.bufs(8, "wg")
        b_wu = cx.bufs(8, "wu")

        def wload3(ph, after):
            for kc in range(8):
                ph.op("pool", (lambda kc: lambda e: e.dma_start(
                    out=wout[:, kc, :], in_=wout_d.ap()[kc * 128:(kc + 1) * 128, :], max_dma_last_dim=4096))(kc),
                    reads=([after] if kc == 0 else []), writes=[b_wout[kc]], dma=b_wout[kc])
            for kc in range(8):
                ph.op("pool", (lambda kc: lambda e: e.dma_start(
                    out=wg[:, kc, :], in_=wg_d.ap()[kc * 128:(kc + 1) * 128, :], max_dma_last_dim=4096))(kc),
                    writes=[b_wg[kc]], dma=b_wg[kc])
                ph.op("pool", (lambda kc: lambda e: e.dma_start(
                    out=wu[:, kc, :], in_=wu_d.ap()[kc * 128:(kc + 1) * 128, :], max_dma_last_dim=4096))(kc),
                    writes=[b_wu[kc]], dma=b_wu[kc])
        phase3(wload3)
        if LAST_PHASE >= 4:
            wd = wst_o.enter_context(nc.sbuf_tensor("w_wd", [128, NFF, D], BF16))
            b_wd = cx.bufs(NFF, "wd")

            def wload4(ph, after):
                for f in range(NFF):
                    ph.op("pool", (lambda f: lambda e: e.dma_start(
                        out=wd[:, f, :], in_=wd_d.ap()[f * 128:(f + 1) * 128, :], max_dma_last_dim=4096))(f),
                        reads=([after] if f == 0 else []), writes=[b_wd[f]], dma=b_wd[f])
            phase4a(wout, b_wout, wload4)
            if LAST_PHASE >= 5:
                phase4b(wg, wu, wd, b_wg, b_wu, b_wd)
        wst_o.close()
        wst.close()
    return nc, es


def _host_consts():
    c = np.zeros((128, NCONST), np.float32)
    j = np.arange(128)[:, None]
    i = np.arange(128)[None, :]
    c[:, K_ID:K_ID + 128] = (j == i)
    c[:, K_TRIF:K_TRIF + 128] = (j <= i) * (-1.0 / 16.0)
    c[:, K_TRIB:K_TRIB + 128] = (j >= i) * (-1.0 / 16.0)
    c[:, K_SUF:K_SUF + 128] = (j > i) * (-1.0 / 16.0)
    c[:, K_SUB:K_SUB + 128] = (j < i) * (-1.0 / 16.0)
    c[:, K_MF:K_MF + 128] = (j <= i)
    c[:, K_MB:K_MB + 128] = (j >= i)
    return c


def _host_na_table(rpb):
    rpb = np.asarray(rpb, np.float32).reshape(8, 15, 31)
    ext = np.concatenate([rpb.reshape(8, -1), np.full((8, 1), NEG, np.float32)], axis=1)
    SENT = 15 * 31
    cols = np.arange(64)
    col_start = np.clip(cols - 8, 0, 48)
    kc = cols[:, None]
    qc = cols[None, :]
    colvalid = (kc >= col_start[None, :]) & (kc < col_start[None, :] + 16)
    dc = np.clip(kc - qc, -15, 15) + 15
    cfgs = [(-2, True), (-1, False), (0, False), (1, False), (2, True),
            (-3, False), (-2, False), (-1, False), (0, False), (1, False), (2, False), (3, False)]
    idx = np.full((NCFG, 128, 128), SENT, np.int64)
    for ci, (d, interior) in enumerate(cfgs):
        for rl in range(2):
            for il in range(2):
                rel = 2 * d + rl - il
                dr = rel + 7
                if dr < 0 or dr > 14:
                    continue
                if interior and not (-4 <= rel <= 3):
                    continue
                blk = np.where(colvalid, dr * 31 + dc, SENT)
                idx[ci, rl * 64:(rl + 1) * 64, il * 64:(il + 1) * 64] = blk
    return np.ascontiguousarray(ext[:, idx])


def kernel(x, meta_tokens, norm_mix_gain, w_in, rpb, meta_bias, w_gate_up_fwd, b_gate_fwd,
           w_gate_up_bwd, b_gate_bwd, gla_norm_gain, w_out, norm_ffn_gain, w_ffn_gate,
           w_ffn_up, w_ffn_down, norm_final_gain):
    f = lambda a: np.ascontiguousarray(np.asarray(a, np.float32))
    nc, es = build_program()
    shared = {
        "meta_tokens": f(meta_tokens), "norm_mix_gain": f(norm_mix_gain).reshape(1, D),
        "w_in": f(w_in).reshape(D, IN_W), "na_tbl": _host_na_table(rpb),
        "meta_bias": f(meta_bias).reshape(8, 16),
        "w_gate_up_fwd": f(w_gate_up_fwd).reshape(16, 256), "b_gate_fwd": f(b_gate_fwd).reshape(1, 256),
        "w_gate_up_bwd": f(w_gate_up_bwd).reshape(16, 256), "b_gate_bwd": f(b_gate_bwd).reshape(1, 256),
        "gla_norm_gain": f(gla_norm_gain).reshape(1, 128), "w_out": f(w_out).reshape(D, D),
        "norm_ffn_gain": f(norm_ffn_gain).reshape(1, D), "w_ffn_gate": f(w_ffn_gate).reshape(D, DFF),
        "w_ffn_up": f(w_ffn_up).reshape(D, DFF), "w_ffn_down": f(w_ffn_down).reshape(DFF, D),
        "norm_final_gain": f(norm_final_gain).reshape(1, D), "consts": _host_consts(),
    }
    xs = f(x)
    in_maps = []
    for c in range(8):
        m = dict(shared)
        m["x"] = xs[2 * c:2 * c + 2]
        in_maps.append(m)
    res = run_bass_kernel_spmd(nc, in_maps, core_ids=list(range(8)))
    es.close()
    kernel.last_results = res.results
    return np.concatenate([r["out"] for r in res.results], axis=0)
```
